# Optimizing a Trainium2 kernel written in Bass

```python
import jax, jax.numpy as jnp
from jax import lax
import numpy as np

D_MODEL = 4096
BATCH = 4
SEQ = 4096
DEPTH = 1
DEC_BATCH = 32
DEC_SEQ = 16
PAST_LEN = 4096

CHUNK = 64
MIX_WIDTH = D_MODEL
WIDTH_A = MIX_WIDTH // 2
WIDTH_B = MIX_WIDTH - WIDTH_A
GMLP_CHUNK = 128
HEADS_A = 8
HEAD_DIM_A = WIDTH_A // HEADS_A
HGRN_EXPAND = 128
HEADS_B = WIDTH_B // HGRN_EXPAND
DK_B = HGRN_EXPAND
DV_B = WIDTH_B // HEADS_B
D_FF = 4 * D_MODEL
N_IN = 2 * WIDTH_A + 4 * WIDTH_B
SPLITS = [WIDTH_A, 2 * WIDTH_A, 2 * WIDTH_A + WIDTH_B, 2 * WIDTH_A + 2 * WIDTH_B, 2 * WIDTH_A + 3 * WIDTH_B]
EPS = 1e-6

kernel_name = 'hybrid_gmlp_hgrn2_streaming_step'


def rmsnorm(x, g):
    x32 = x.astype(jnp.float32)
    y = x32 * lax.rsqrt(jnp.mean(x32 * x32, axis=-1, keepdims=True) + EPS)
    return (y * g.astype(jnp.float32)).astype(x.dtype)


def chunk_causal_mask(n):
    blk = np.arange(n) // CHUNK
    return jnp.asarray(blk[None, :] <= blk[:, None])


def gmlp_spatial(u, v, w_s, b_s):
    B, T, H, Dh = v.shape
    L = min(T, GMLP_CHUNK)
    n = T // L
    w = jnp.where(chunk_causal_mask(L)[None], w_s[:, :L, :L], 0.0).astype(v.dtype)
    vc = v.reshape(B, n, L, H, Dh)
    mixed = jnp.einsum('hij,bnjhd->bnihd', w, vc) + b_s[:, :L].T.astype(v.dtype)[None, None, :, :, None]
    return u * mixed.reshape(B, T, H, Dh)


def hgrn2_chunkwise(q, k, v, log_f, s0):
    f32 = jnp.float32
    B, T, H, DK = q.shape
    DV = v.shape[-1]
    L = min(T, CHUNK)
    n = T // L
    qc = q.astype(f32).reshape(B, n, L, H, DK)
    kc = k.astype(f32).reshape(B, n, L, H, DK)
    vc = v.astype(f32).reshape(B, n, L, H, DV)
    b = jnp.cumsum(log_f.astype(f32).reshape(B, n, L, H, DK), axis=2)
    b_ref = b[:, :, L // 2:L // 2 + 1]
    b_end = b[:, :, -1]
    q_rel = qc * jnp.exp(b - b_ref)
    k_rel = kc * jnp.exp(b_ref - b)
    scores = jnp.einsum('bnthd,bnshd->bnhts', q_rel, k_rel)
    scores = jnp.where(jnp.tril(jnp.ones((L, L), bool)), scores, 0.0)
    o_intra = jnp.einsum('bnhts,bnshv->bnthv', scores, vc)
    k_end = kc * jnp.exp(b_end[:, :, None] - b)
    ds = jnp.einsum('bnshd,bnshv->bnhdv', k_end, vc)
    decay = jnp.exp(b_end)

    def step(s, inp):
        dec, d = inp
        return dec[..., None] * s + d, s

    s_final, s_start = lax.scan(step, s0.astype(f32), (jnp.moveaxis(decay, 1, 0), jnp.moveaxis(ds, 1, 0)))
    s_start = jnp.moveaxis(s_start, 0, 1)
    o_inter = jnp.einsum('bnthd,bnhdv->bnthv', qc * jnp.exp(b), s_start)
    return (o_intra + o_inter).reshape(B, T, H, DV), s_final


def trunk_layer(x, s0, lb, norm1, w_in, w_s, b_s, norm_v, norm_o, w_out, norm2, w_up, w_down):
    B, T, _ = x.shape
    h = rmsnorm(x, norm1)
    proj = h @ w_in.astype(h.dtype)
    u, v, q, fz, i, g = jnp.split(proj, SPLITS, axis=-1)
    u = jax.nn.gelu(u).reshape(B, T, HEADS_A, HEAD_DIM_A)
    v = rmsnorm(jax.nn.gelu(v).reshape(B, T, HEADS_A, HEAD_DIM_A), norm_v)
    y_a = gmlp_spatial(u, v, w_s, b_s).reshape(B, T, WIDTH_A)
    f = lb + (1.0 - lb) * jax.nn.sigmoid(fz.astype(jnp.float32))
    k = 1.0 - f
    log_f = jnp.log(f)
    qh = jax.nn.silu(q).reshape(B, T, HEADS_B, DK_B)
    o, s_new = hgrn2_chunkwise(qh, k.reshape(B, T, HEADS_B, DK_B), i.reshape(B, T, HEADS_B, DV_B),
                               log_f.reshape(B, T, HEADS_B, DK_B), s0)
    o = rmsnorm(o, norm_o) * jax.nn.silu(g.reshape(B, T, HEADS_B, DV_B)).astype(jnp.float32)
    y_b = o.reshape(B, T, WIDTH_B).astype(x.dtype)
    x = x + jnp.concatenate([y_a, y_b], axis=-1) @ w_out.astype(x.dtype)
    hid = jnp.square(jax.nn.relu(rmsnorm(x, norm2) @ w_up.astype(x.dtype)))
    x = x + hid @ w_down.astype(x.dtype)
    return x, s_new, v.reshape(B, T, WIDTH_A)


def setup_inputs(seed: int = 0) -> dict:
    key = jax.random.key(seed)
    ks = jax.random.split(key, 16)
    f32 = jnp.float32

    def nrm(k, shape, scale):
        return jax.random.normal(k, shape, f32) * scale

    return {
        'x_prompt': nrm(ks[0], (BATCH, SEQ, D_MODEL), 1.0),
        'x_sample': nrm(ks[1], (DEC_BATCH, DEC_SEQ, D_MODEL), 1.0),
        'state_hgrn': nrm(ks[2], (DEPTH, DEC_BATCH, HEADS_B, DK_B, DV_B), 0.5),
        'norm1': 1.0 + nrm(ks[3], (DEPTH, D_MODEL), 0.05),
        'w_in': nrm(ks[4], (DEPTH, D_MODEL, N_IN), D_MODEL ** -0.5),
        'w_s': nrm(ks[5], (DEPTH, HEADS_A, GMLP_CHUNK, GMLP_CHUNK), GMLP_CHUNK ** -0.5),
        'b_s': 1.0 + nrm(ks[6], (DEPTH, HEADS_A, GMLP_CHUNK), 0.1),
        'norm_v': 1.0 + nrm(ks[7], (DEPTH, HEADS_A, HEAD_DIM_A), 0.05),
        'lb_logits': nrm(ks[8], (DEPTH + 1, WIDTH_B), 0.1),
        'norm_o': 1.0 + nrm(ks[9], (DEPTH, HEADS_B, DV_B), 0.05),
        'w_out': nrm(ks[10], (DEPTH, MIX_WIDTH, D_MODEL), MIX_WIDTH ** -0.5),
        'norm2': 1.0 + nrm(ks[11], (DEPTH, D_MODEL), 0.05),
        'w_up': nrm(ks[12], (DEPTH, D_MODEL, D_FF), D_MODEL ** -0.5),
        'w_down': nrm(ks[13], (DEPTH, D_FF, D_MODEL), D_FF ** -0.5),
        'norm_f': 1.0 + nrm(ks[14], (D_MODEL,), 0.05),
    }


def reference(x_prompt, x_sample, state_hgrn, norm1, w_in, w_s, b_s, norm_v, lb_logits, norm_o,
              w_out, norm2, w_up, w_down, norm_f):
    lower_bounds = jnp.cumsum(jax.nn.softmax(lb_logits.astype(jnp.float32), axis=0), axis=0)
    s0_prompt = jnp.zeros((x_prompt.shape[0], HEADS_B, DK_B, DV_B), jnp.float32)
    xp, xs = x_prompt, x_sample
    s_prompt, s_sample, v_sample = [], [], []
    for l in range(DEPTH):
        params = (norm1[l], w_in[l], w_s[l], b_s[l], norm_v[l], norm_o[l], w_out[l], norm2[l], w_up[l], w_down[l])
        xp, sp, _ = trunk_layer(xp, s0_prompt, lower_bounds[l], *params)
        xs, ss, vs = trunk_layer(xs, state_hgrn[l], lower_bounds[l], *params)
        s_prompt.append(sp)
        s_sample.append(ss)
        v_sample.append(vs)
    y_prompt = rmsnorm(xp, norm_f)
    y_sample = rmsnorm(xs, norm_f)
    return (y_prompt, y_sample, jnp.stack(s_prompt), jnp.stack(s_sample), jnp.stack(v_sample))
```

```python
import numpy as np
import ml_dtypes
from contextlib import ExitStack
import concourse.bass as bass
import concourse.mybir as mybir
from concourse.bass_utils import run_bass_kernel_spmd

F32 = mybir.dt.float32
BF16 = mybir.dt.bfloat16
AF = mybir.ActivationFunctionType
ALU = mybir.AluOpType
EPS = 1e-6


class Cfg:
    def __init__(self, D=4096, HA=8, HB=16, NP=2048, NSTR=4, LS=16, S=4, n_cores=8, WSLOTS=4):
        self.D = D
        self.HA = HA
        self.HB = HB
        self.WA = D // 2
        self.WB = D // 2
        self.HDA = self.WA // HA
        assert self.HDA == 256 and self.WB // HB == 128
        self.DFF = 4 * D
        self.KT = D // 128
        self.NIN = 2 * self.WA + 4 * self.WB
        self.NP = NP
        self.NSTR = NSTR
        self.LS = LS
        self.NS = NSTR * LS
        assert self.NS == 64
        self.S = S
        self.T = S * 128
        assert NP % self.T == 0
        self.n_cores = n_cores
        self.WSLOTS = WSLOTS
        self.KS = min(8, self.KT)
        self.FC = min(2048, self.DFF)
        self.NQ = self.WB // 512
        self.NA = self.WA // 512
        assert self.WA % 512 == 0 and self.WB % 512 == 0


class Op:
    __slots__ = ("eng", "fn", "deps", "waited", "semval", "dma", "dma_idx")

    def __init__(self, eng, fn, dma):
        self.eng = eng
        self.fn = fn
        self.dma = dma
        self.waited = False
        self.semval = 0
        self.deps = []
        self.dma_idx = -1


class Res:
    __slots__ = ("writers", "readers", "excl")

    def __init__(self, excl=False):
        self.writers = []
        self.readers = []
        self.excl = excl


NDMASEM = 6
STOP_AT = None


class _Stop(Exception):
    pass


def ck(name):
    if STOP_AT == name:
        raise _Stop()
ENGS = ("pe", "act", "dve", "pool", "sp")


class Prog:
    def __init__(self):
        self.q = {e: [] for e in ENGS}
        self.dma_ops = {"sp": [], "pool": [], "act": []}

    def op(self, eng, fn, reads=(), writes=(), upd=(), dma=False):
        o = Op(eng, fn, dma)
        deps = []
        for r in reads:
            deps.extend(r.writers)
            if r.excl:
                deps.extend(x for x in r.readers if x.eng != eng)
        for w in writes:
            deps.extend(w.writers)
            deps.extend(w.readers)
        if dma:
            lst = self.dma_ops[eng]
            if len(lst) >= NDMASEM:
                deps.append(lst[-NDMASEM])
            o.dma_idx = len(lst)
            lst.append(o)
        seen = set()
        i = 0
        while i < len(deps):
            d = deps[i]
            i += 1
            if id(d) in seen:
                continue
            seen.add(id(d))
            if d.fn is None:
                deps.extend(d.deps)
                continue
            if d.eng == "pe" and eng == "pe" and not d.dma:
                continue
            d.waited = True
            o.deps.append(d)
        for w in writes:
            w.writers = [o]
            w.readers = []
        for w in upd:
            w.writers = [o]
        for r in reads:
            r.readers = [x for x in r.readers if x.dma or x.eng != eng] + [o]
        self.q[eng].append(o)
        return o

    def emit(self, nc, es):
        sems = {e: es.enter_context(nc.semaphore("s_" + e)) for e in ENGS}
        dsems = {e: [es.enter_context(nc.semaphore("d_%s%d" % (e, i))) for i in range(NDMASEM)]
                 for e in self.dma_ops}
        for e in ENGS:
            c = 0
            for o in self.q[e]:
                if o.waited and not o.dma:
                    c += 1
                    o.semval = c
        block = es.enter_context(nc.Block())

        def run(e, eng):
            waited = {}
            for o in self.q[e]:
                for d in o.deps:
                    if d.dma:
                        sem = dsems[d.eng][d.dma_idx % NDMASEM]
                        val = 16 * (d.dma_idx // NDMASEM + 1)
                    else:
                        sem = sems[d.eng]
                        val = d.semval
                    key = id(sem)
                    if waited.get(key, 0) >= val:
                        continue
                    waited[key] = val
                    eng.wait_ge(sem, val)
                if o.fn is None:
                    continue
                inst = o.fn(eng)
                if o.dma:
                    inst.then_inc(dsems[e][o.dma_idx % NDMASEM], 16)
                elif o.waited:
                    inst.then_inc(sems[e], 1)

        @block.tensor
        def _(eng):
            run("pe", eng)

        @block.scalar
        def _(eng):
            run("act", eng)

        @block.vector
        def _(eng):
            run("dve", eng)

        @block.gpsimd
        def _(eng):
            run("pool", eng)

        @block.sync
        def _(eng):
            run("sp", eng)


def make_consts():
    c = {}
    c["ident_f"] = np.eye(128, dtype=np.float32)
    c["ident_b"] = np.eye(128).astype(ml_dtypes.bfloat16)
    for tag, P, L in (("p", 128, 64), ("s", 64, 16)):
        s = np.arange(P)[:, None]
        t = np.arange(P)[None, :]
        same = (s // L) == (t // L)
        tri = same & (s <= t)
        ref = (t // L) * L + L // 2
        refm = same & (s <= ref)
        endm = same
        c["tri_" + tag] = tri.astype(np.float32)
        c["d1m_" + tag] = tri.astype(np.float32) - refm.astype(np.float32)
        c["d2m_" + tag] = endm.astype(np.float32) - tri.astype(np.float32)
        nc_ = P // L
        chk = (np.arange(P)[:, None] // L) == np.arange(nc_)[None, :]
        c["chk_" + tag] = chk.astype(np.float32)
        c["tri4_" + tag] = np.tile(tri.astype(np.float32), (1, 4)).astype(ml_dtypes.bfloat16)
    ms = (np.arange(64)[None, :] // 16) == np.arange(4)[:, None]
    c["ms"] = np.broadcast_to(ms[None].astype(np.float32), (128, 4, 64)).astype(ml_dtypes.bfloat16).copy()
    j = np.arange(128)[:, None]
    i = np.arange(128)[None, :]
    c["maskg"] = (~((i < 64) & (j >= 64))).astype(np.float32)
    return c


def build(cfg):
    nc = bass.Bass("TRN2", target_bir_lowering=False)
    D, KT, S, T = cfg.D, cfg.KT, cfg.S, cfg.T
    WA, WB, HA, HB, DFF = cfg.WA, cfg.WB, cfg.HA, cfg.HB, cfg.DFF
    NP, NS, NSTR, LS = cfg.NP, cfg.NS, cfg.NSTR, cfg.LS
    KS = cfg.KS

    def din(name, shape, dt=F32):
        return nc.dram_tensor(name, list(shape), dt, kind="ExternalInput").ap()

    def dout(name, shape, dt=F32):
        return nc.dram_tensor(name, list(shape), dt, kind="ExternalOutput").ap()

    xp = din("xp", [NP, D])
    xprev = din("xprev", [NP, D])
    xs = din("xs", [NS, D])
    s0 = din("s0", [NSTR, HB, 128, 128])
    w_in = din("w_in", [D, cfg.NIN])
    w_out = din("w_out", [D, D])
    w_up = din("w_up", [D, DFF])
    w_down = din("w_down", [DFF, D])
    norm1 = din("norm1", [D])
    norm2 = din("norm2", [D])
    norm_f = din("norm_f", [D])
    norm_v = din("norm_v", [WA])
    norm_o = din("norm_o", [WB])
    lbl = din("lb_logits", [2, WB])
    w_s = din("w_s", [HA, 128, 128])
    b_s = din("b_s", [HA, 128])
    cst = {}
    for k, v in make_consts().items():
        cst[k] = din("c_" + k, v.shape, BF16 if v.dtype == ml_dtypes.bfloat16 else F32)

    yp = dout("yp", [NP, D])
    ys = dout("ys", [NS, D])
    spo = dout("spo", [HB, 128, 128])
    sso = dout("sso", [NSTR, HB, 128, 128])
    vso = dout("vso", [NS, WA])

    NSLAB = (D * cfg.NIN + D * D + 2 * D * DFF) // (KS * 128 * 512)
    WSC_CH = 64
    wsc_t = [nc.dram_tensor("wscratch%d" % i, [WSC_CH, 128, KS * 512], BF16, kind="Internal").ap()
             for i in range((NSLAB + WSC_CH - 1) // WSC_CH)]

    class _Wsc:
        def __getitem__(self, key):
            idx = key[0]
            return wsc_t[idx // WSC_CH][(idx % WSC_CH,) + tuple(key[1:])]
    wsc = _Wsc()
    wcache = {}
    r_wsc = {}

    base = [nc._sbuf_addr_for_side(None)]
    base[0] = (base[0] + 63) // 64 * 64
    limit = base[0] + nc.sbuf_bytes_remaining - 64

    def salloc(name, shape, dt, at=None):
        nbytes = int(np.prod(shape[1:])) * (2 if dt == BF16 else 4)
        nbytes = (nbytes + 63) // 64 * 64
        if at is None:
            off = base[0]
            base[0] += nbytes
            assert base[0] <= limit, ("SBUF overflow", name, base[0], limit)
        else:
            off = at
        return nc.alloc_sbuf_tensor_at(name, list(shape), dt, offset=off), off, nbytes

    X1, X1off, X1bytes = salloc("X1", [128, S, D], F32)
    if X1bytes < 65536:
        _, _, padb = salloc("X1pad", [128, (65536 - X1bytes) // 4], F32)
        X1bytes += padb
    HT, _, _ = salloc("HT", [128, KT, T], BF16)
    YT, YToff, YTbytes = salloc("YT", [128, KT, T], BF16)
    need_yt = max(2 * (cfg.FC // 128) * T * 2, D * 4)
    if YTbytes < need_yt:
        _, _, padb = salloc("YTpad", [128, (need_yt - YTbytes) // 4], F32)
        YTbytes += padb
    WR = [salloc("WR%d" % i, [128, KS, 512], BF16)[0] for i in range(cfg.WSLOTS)]
    S32, _, _ = salloc("S32", [128, max(HB, 16), 128], F32)
    LB, _, _ = salloc("LB", [128, WB], F32)
    OML, _, _ = salloc("OML", [128, WB], F32)
    ident_f, _, _ = salloc("ident_f", [128, 128], F32)
    ident_b, _, _ = salloc("ident_b", [128, 128], BF16)
    C = {}
    for tag, P in (("p", 128), ("s", 64)):
        for nm in ("tri", "d1m", "d2m"):
            C[nm + "_" + tag], _, _ = salloc(nm + "_" + tag, [P, P], F32)
        C["chk_" + tag], _, _ = salloc("chk_" + tag, [P, P // (64 if tag == "p" else 16)], F32)
        C["tri4_" + tag], _, _ = salloc("tri4_" + tag, [P, 4 * P], BF16)
    MS, _, _ = salloc("MS", [128, 4, 64], BF16)
    WT, _, _ = salloc("WT", [128, HA, 128], BF16)
    WTS, _, _ = salloc("WTS", [64, HA, 64], BF16)
    BSC, _, _ = salloc("BSC", [128, HA], F32)
    BSS, _, _ = salloc("BSS", [64, HA], F32)
    GC1, _, _ = salloc("GC1", [128, KT], F32)
    GC2, _, _ = salloc("GC2", [128, KT], F32)
    SM, _, _ = salloc("SM", [128, 64], F32)
    EPSC, _, _ = salloc("EPSC", [128, 1], F32)

    ov = [X1off]

    def oalloc(name, shape, dt):
        t, off, nb = salloc(name, shape, dt, at=ov[0])
        ov[0] += nb
        assert ov[0] <= X1off + X1bytes, ("overlay overflow", name)
        return t

    NVO = oalloc("NVO", [128, max(WA, WB)], F32)
    RAW = [oalloc("RAW%d" % i, [128, S, 512], F32) for i in range(3)]
    IBt = oalloc("IB", [128, S, 512], BF16)
    Ft = oalloc("Ft", [128, 512], F32)
    KKt = oalloc("KKt", [128, 512], F32)
    Et = [oalloc("Et%d" % i, [128, 512], F32) for i in range(2)]
    GNt = oalloc("GNt", [128, 512], F32)
    BFt = [oalloc("BFt%d" % i, [128, 512], BF16) for i in range(6)]
    TTt = [oalloc("TTt%d" % i, [128, 4, 128], BF16) for i in range(3)]
    SBt = oalloc("SBt", [128, 4, 4, 128], BF16)
    DECt = oalloc("DECt", [128, 16], F32)
    if S >= 4:
        raw2_off = X1off + (max(WA, WB) * 4 + 63) // 64 * 64 + 2 * (S * 512 * 4)
        KEM = nc.alloc_sbuf_tensor_at("KEM", [64, 4, 512], BF16, offset=raw2_off + 2048)
        QBM = nc.alloc_sbuf_tensor_at("QBM", [128, 4, 4, 64], BF16, offset=raw2_off + 2048 + 4096)
    else:
        KEM = oalloc("KEM", [64, 4, 512], BF16)
        QBM = oalloc("QBM", [128, 4, 4, 64], BF16)
    stage1_end = ov[0]
    ov[0] = X1off
    XIN = [oalloc("XIN%d" % i, [128, D], F32) for i in range(2)]
    HIDT = [nc.alloc_sbuf_tensor_at("HIDT%d" % i, [128, cfg.FC // 128, T], BF16,
                                    offset=YToff + i * (cfg.FC // 128) * T * 2) for i in range(2)]
    assert 2 * (cfg.FC // 128) * T * 2 <= YTbytes
    JUNK, _, _ = salloc("JUNK", [128, 512], BF16)
    XSCP = [salloc("XSCP%d" % i, [128, 512], F32)[0] for i in range(2)]
    RT, _, _ = salloc("RT", [128, 512], F32)
    HTM = [salloc("HTM%d" % i, [128, 512], BF16)[0] for i in range(2)]
    NFB = None
    NFBt = nc.alloc_sbuf_tensor_at("NFB", [128, D], F32, offset=YToff)
    assert D * 4 <= YTbytes

    GPS = [nc.alloc_psum_tensor("gps%d" % i, [128, 512], F32) for i in range(4)]
    AUX = [nc.alloc_psum_tensor("aux%d" % i, [128, 512], F32) for i in range(4)]

    pg = Prog()
    R = {}

    def res(name):
        if name not in R:
            R[name] = Res()
        return R[name]

    r_gps = [Res(excl=True) for _ in range(4)]
    r_aux = [Res(excl=True) for _ in range(4)]
    r_wr = [Res() for _ in range(cfg.WSLOTS)]
    r_x1reg = Res()
    aux_i = [0]

    def aux():
        i = aux_i[0] % 4
        aux_i[0] += 1
        return AUX[i], r_aux[i]

    wr_i = [0]

    flip = [0]

    def evac_eng():
        flip[0] ^= 1
        return "act" if flip[0] else "dve"

    def copy_op(eng, out, in_, reads, writes, upd=()):
        if eng == "act":
            return pg.op("act", lambda e: e.activation(out=out, in_=in_, func=AF.Copy), reads, writes, upd)
        return pg.op("dve", lambda e: e.tensor_copy(out=out, in_=in_), reads, writes, upd)

    deferred = []

    def pump():
        if deferred:
            deferred.pop(0)()

    def flush():
        while deferred:
            deferred.pop(0)()

    def drive(gen):
        def step():
            try:
                next(gen)
                deferred.insert(0, step) if False else deferred.append(step)
            except StopIteration:
                pass
        deferred.append(step)

    def dma(eng, out, in_, reads=(), writes=()):
        return pg.op(eng, lambda e: e.dma_start(out=out, in_=in_, allow_slow_non_contiguous=True), reads, writes, dma=True)

    r_const = res("const")
    setup_ops = []
    setup_ops.append(dma("sp", ident_f[:, :], cst["ident_f"][:, :]))
    setup_ops.append(dma("sp", ident_b[:, :], cst["ident_b"][:, :]))
    for tag in ("p", "s"):
        for nm in ("tri", "d1m", "d2m", "chk", "tri4"):
            k = nm + "_" + tag
            setup_ops.append(dma("sp", C[k][:, :], cst[k][:, :]))
    setup_ops.append(dma("sp", MS[:, :, :], cst["ms"][:, :, :]))
    with nc.allow_non_contiguous_dma(reason="tiny parameter loads"):
        setup_ops.append(dma("sp", GC1[:, :], norm1.rearrange("(kt p) -> p kt", p=128)))
        setup_ops.append(dma("sp", GC2[:, :], norm2.rearrange("(kt p) -> p kt", p=128)))
        setup_ops.append(dma("sp", BSC[:, :], b_s.rearrange("h i -> i h")))
        for st in range(NSTR):
            setup_ops.append(dma("sp", BSS[st * LS:(st + 1) * LS, :], b_s[:, 0:LS].rearrange("h i -> i h")))
    setup_ops.append(dma("sp", LB[:, :], lbl[0, :].partition_broadcast(128)))
    setup_ops.append(dma("sp", OML[:, :], lbl[1, :].partition_broadcast(128)))
    r_setup = Res()
    r_setup.writers = list(setup_ops)
    o1 = pg.op("dve", lambda e: e.tensor_tensor(out=LB[:, :], in0=LB[:, :], in1=OML[:, :], op=ALU.subtract),
               reads=[r_setup], writes=[res("LB")])
    pg.op("act", lambda e: e.activation(out=LB[:, :], in_=LB[:, :], func=AF.Sigmoid), reads=[], writes=[res("LB")])
    pg.op("dve", lambda e: e.tensor_scalar(out=OML[:, :], in0=LB[:, :], scalar1=-1.0, scalar2=1.0,
                                           op0=ALU.mult, op1=ALU.add), reads=[res("LB")], writes=[res("OML")])
    pg.op("dve", lambda e: e.memset(EPSC[:, :], EPS), writes=[res("EPSC")])
    pg.op("dve", lambda e: e.memset(S32[:, :, :], 0.0), writes=[res("S32")])

    r_stage0 = res("x1region_setup")
    WSTG = XIN[0]
    assert HA * 128 <= D
    MG = XIN[1]
    o_ws = dma("sp", WSTG[:, 0:HA * 128].rearrange("p (h j) -> p h j", j=128), w_s.rearrange("h i j -> i h j"),
               writes=[r_stage0])
    o_mg = dma("sp", MG[:, D - 128:D], cst["maskg"][:, :], reads=[r_stage0])
    r_mg = Res()
    r_mg.writers = [o_mg]
    for h in range(HA):
        ps, rps = aux()
        pg.op("pe", lambda e, h=h, ps=ps: e.transpose(out=ps[:, 0:128], in_=WSTG[:, h * 128:(h + 1) * 128],
                                                      identity=ident_f[:, :]),
              reads=[r_stage0, r_setup], writes=[rps])
        pg.op("dve", lambda e, h=h, ps=ps: e.tensor_tensor(out=WT[:, h, :], in0=ps[:, 0:128], in1=MG[:, D - 128:D],
                                                           op=ALU.mult),
              reads=[rps, r_mg], upd=[res("WT")])
    WSS = XIN[1]
    o_z = pg.op("dve", lambda e: e.memset(WSS[0:64, 0:HA * 64], 0.0), reads=[r_stage0], writes=[res("WSS")])
    with nc.allow_non_contiguous_dma(reason="tiny parameter loads"):
        dd = []
        for st in range(NSTR):
            dd.append(dma("sp", WSS[st * LS:(st + 1) * LS, 0:HA * 64].rearrange("p (h j) -> p h j", j=64)[:, :, st * LS:(st + 1) * LS],
                          w_s[:, 0:LS, 0:LS].rearrange("h i j -> i h j"), reads=[res("WSS")]))
    r_wss = Res()
    r_wss.writers = dd
    for h in range(HA):
        ps, rps = aux()
        pg.op("pe", lambda e, h=h, ps=ps: e.transpose(out=ps[0:64, 0:64], in_=WSS[0:64, h * 64:(h + 1) * 64],
                                                      identity=ident_f[0:64, 0:64]),
              reads=[r_wss, r_setup], writes=[rps])
        pg.op("dve", lambda e, h=h, ps=ps: e.tensor_copy(out=WTS[:, h, :], in_=ps[0:64, 0:64]),
              reads=[rps], upd=[res("WT")])
    r_x1 = res("x1coarse")

    def stage_barrier(tag):
        pass

    def rstd_from_ss(ssap, P, n, invn, rkey):
        pg.op("act", lambda e: e.activation(out=ssap, in_=ssap, func=AF.Ln, scale=invn, bias=EPSC[0:P, :]),
              reads=[rkey, res("EPSC")], writes=[rkey])
        pg.op("act", lambda e: e.activation(out=ssap, in_=ssap, func=AF.Exp, scale=-0.5), reads=[rkey], writes=[rkey])

    NPC = D // 512

    def sumsq(xap, P, ssap, rsrc, rss):
        rpart = res("sspart")
        ck("n0")
        for c in range(NPC):
            pg.op("act", lambda e, c=c: e.activation(out=JUNK[0:P, :], in_=xap[:, c * 512:(c + 1) * 512], func=AF.Square,
                                                     accum_out=SM[0:P, 32 + c:33 + c]),
                  reads=[rsrc], writes=[res("junk")], upd=[rpart])
        pg.op("dve", lambda e: e.tensor_reduce(out=ssap, in_=SM[0:P, 32:32 + NPC], axis=mybir.AxisListType.X, op=ALU.add),
              reads=[rpart], writes=[rss])
        rstd_from_ss(ssap, P, 1, 1.0 / D, rss)

    def norm_to_T(srcs, gcol, dstT, dst_res):
        for (s, xap, P, rsrc) in srcs:
            ssap = SM[0:P, s:s + 1]
            rss = res("ss%d" % s)
            sumsq(xap, P, ssap, rsrc, rss)
            ck("n1")
            wl = []
            for c in range(NPC):
                xb = XSCP[c % 2]
                rxs = res("xscp%d" % (c % 2))
                pg.op("dve", lambda e, c=c, xb=xb, xap=xap, ssap=ssap, P=P: e.tensor_scalar(
                    out=xb[0:P, :], in0=xap[:, c * 512:(c + 1) * 512], scalar1=ssap, scalar2=None, op0=ALU.mult),
                      reads=[rsrc, rss], writes=[rxs])
                ps, rps = aux()

                def tr(e, ps=ps, xb=xb, P=P):
                    last = None
                    for j in range(4):
                        last = e.transpose(out=ps[:, j * 128:j * 128 + P], in_=xb[0:P, j * 128:(j + 1) * 128],
                                           identity=ident_f[0:P, 0:P])
                    return last
                ck("n2")
                pg.op("pe", tr, reads=[rxs, r_setup], writes=[rps])
                ck("n3")
                eng = evac_eng()
                for j in range(4):
                    kt = c * 4 + j
                    out = dstT[:, kt, s * 128:s * 128 + P]
                    in_ = ps[:, j * 128:j * 128 + P]
                    if eng == "act":
                        o = pg.op("act", lambda e, out=out, in_=in_, kt=kt: e.activation(out=out, in_=in_, func=AF.Identity,
                                                                                          scale=gcol[:, kt:kt + 1]),
                                  reads=[rps, r_setup])
                    else:
                        o = pg.op("dve", lambda e, out=out, in_=in_, kt=kt: e.tensor_scalar(out=out, in0=in_,
                                                                                            scalar1=gcol[:, kt:kt + 1],
                                                                                            scalar2=None, op0=ALU.mult),
                                  reads=[rps, r_setup])
                    wl = [x for x in wl if x.eng != o.eng] + [o]
                    ck("n4%d" % j)
            dst_res[s].writers = wl
            ck("n5")

    def gemm(actT, act_res, nk, wsrc, row0, col0, subt, evac, do_pump=True, wname=None, npump=1):
        nsl = (nk + KS - 1) // KS
        for sl in range(nsl):
            kk = min(KS, nk - sl * KS)
            slot = wr_i[0] % cfg.WSLOTS
            wr_i[0] += 1
            wt = WR[slot]
            src = wsrc[row0 + sl * KS * 128: row0 + (sl * KS + kk) * 128, col0:col0 + 512].rearrange(
                "(kt p) n -> p kt n", p=128)
            key = (wname, row0 + sl * KS * 128, col0)
            if key in wcache:
                idx = wcache[key]
                pg.op("sp", lambda e, wt=wt, idx=idx, kk=kk: e.dma_start(
                    out=wt[:, 0:kk, :], in_=wsc[idx, :, 0:kk * 512].rearrange("p (k n) -> p k n", n=512)),
                    reads=[r_wsc[idx]], writes=[r_wr[slot]], dma=True)
            else:
                idx = len(wcache)
                wcache[key] = idx
                r_wsc[idx] = Res()
                pg.op("pool", lambda e, wt=wt, src=src, kk=kk: e.dma_start(out=wt[:, 0:kk, :], in_=src),
                      writes=[r_wr[slot]], dma=True)
                pg.op("sp", lambda e, wt=wt, idx=idx, kk=kk: e.dma_start(
                    out=wsc[idx, :, 0:kk * 512].rearrange("p (k n) -> p k n", n=512), in_=wt[:, 0:kk, :]),
                    reads=[r_wr[slot]], writes=[r_wsc[idx]], dma=True)
            for (s, P) in subt:
                def mm(e, s=s, P=P, sl=sl, kk=kk, wt=wt):
                    last = None
                    for j in range(kk):
                        kt = sl * KS + j
                        last = e.matmul(GPS[s][0:P, :], lhsT=actT[:, kt, s * 128:s * 128 + P], rhs=wt[:, j, :],
                                        start=(kt == 0), stop=(kt == nk - 1))
                    return last
                if sl == 0:
                    pg.op("pe", mm, reads=[act_res[s], r_wr[slot]], writes=[r_gps[s]])
                else:
                    pg.op("pe", mm, reads=[act_res[s], r_wr[slot]], upd=[r_gps[s]])
                if do_pump:
                    for _ in range(npump):
                        pump()
                if sl == nsl - 1:
                    evac(s, P, GPS[s][0:P, :], r_gps[s])

    r_ht = [Res() for _ in range(S)]
    r_yt = [Res() for _ in range(S)]
    r_xin = [Res(), Res()]

    def mixer_A(jp, subt, sample):
        U, V = RAW[0], RAW[1]
        for (s, P) in subt:
            rU, rV = res("raw0_%d" % s), res("raw1_%d" % s)
            rss = res("ssv")
            for hh in range(2):
                pg.op("act", lambda e, s=s, P=P, hh=hh: e.activation(out=JUNK[0:P, 0:256], in_=V[0:P, s, hh * 256:(hh + 1) * 256],
                                                                      func=AF.Square, accum_out=SM[0:P, 8 + hh:9 + hh]),
                      reads=[rV], writes=[res("junk")], upd=[rss])
            rss.writers = [pg.q["act"][-1]]
            rstd_from_ss(SM[0:P, 8:10], P, 2, 1.0 / 256, rss)
            VN = BFt[5]
            rvn = res("vn")
            first = True
            for hh in range(2):
                c0 = hh * 256
                if sample:
                    pg.op("dve", lambda e, s=s, P=P, hh=hh, c0=c0: e.scalar_tensor_tensor(
                        out=Et[0][0:P, c0:c0 + 256], in0=V[0:P, s, c0:c0 + 256], scalar=SM[0:P, 8 + hh:9 + hh],
                        in1=NVO[0:P, jp * 512 + c0: jp * 512 + c0 + 256], op0=ALU.mult, op1=ALU.mult),
                        reads=[rV, rss, res("NVO")], writes=[res("E0")] if first else [], upd=[] if first else [res("E0")])
                    pg.op("dve", lambda e, P=P, c0=c0: e.tensor_copy(out=VN[0:P, c0:c0 + 256], in_=Et[0][0:P, c0:c0 + 256]),
                          reads=[res("E0")], writes=[rvn] if first else [], upd=[] if first else [rvn])
                else:
                    pg.op("dve", lambda e, s=s, P=P, hh=hh, c0=c0: e.scalar_tensor_tensor(
                        out=VN[0:P, c0:c0 + 256], in0=V[0:P, s, c0:c0 + 256], scalar=SM[0:P, 8 + hh:9 + hh],
                        in1=NVO[0:P, jp * 512 + c0: jp * 512 + c0 + 256], op0=ALU.mult, op1=ALU.mult),
                        reads=[rV, rss, res("NVO")], writes=[rvn] if first else [], upd=[] if first else [rvn])
                first = False
            if sample:
                dma("sp", vso[0:P, jp * 512:(jp + 1) * 512], Et[0][0:P, :], reads=[res("E0")])
                out_dmas.append(pg.q["sp"][-1])
            yield
            ps, rps = aux()

            def mix(e, P=P, ps=ps):
                last = None
                for hh in range(2):
                    hg = 2 * jp + hh
                    lhsT = WTS[0:P, hg, :] if sample else WT[:, hg, :]
                    last = e.matmul(ps[0:P, hh * 256:(hh + 1) * 256], lhsT=lhsT, rhs=VN[0:P, hh * 256:(hh + 1) * 256],
                                    start=True, stop=True)
                return last
            pg.op("pe", mix, reads=[rvn, res("WT")], writes=[rps])
            YA = BFt[4]
            rya = res("ya")
            for hh in range(2):
                hg = 2 * jp + hh
                bcol = BSS[0:P, hg:hg + 1] if sample else BSC[:, hg:hg + 1]
                pg.op("dve", lambda e, s=s, P=P, hh=hh, ps=ps, bcol=bcol: e.scalar_tensor_tensor(
                    out=YA[0:P, hh * 256:(hh + 1) * 256], in0=ps[0:P, hh * 256:(hh + 1) * 256], scalar=bcol,
                    in1=U[0:P, s, hh * 256:(hh + 1) * 256], op0=ALU.add, op1=ALU.mult),
                    reads=[rps, rU, r_setup], writes=[rya] if hh == 0 else [], upd=[] if hh == 0 else [rya])
            yield
            y_to_T(YA, rya, s, P, 4 * jp)
            yield

    def y_to_T(Y, ry, s, P, kt0):
        ps, rps = aux()
        psb = ps[:, :].bitcast(BF16)

        def tr(e, P=P, psb=psb):
            last = None
            for j in range(4):
                last = e.transpose(out=psb[:, j * 128:j * 128 + P], in_=Y[0:P, j * 128:(j + 1) * 128],
                                   identity=ident_b[0:P, 0:P])
            return last
        pg.op("pe", tr, reads=[ry, r_setup], writes=[rps])
        eng = evac_eng()
        out = YT[:, kt0:kt0 + 4, s * 128:s * 128 + P]
        in_ = psb[:, 0:512].rearrange("p (j t) -> p j t", t=128)[:, :, 0:P]
        o = copy_op(eng, out, in_, [rps], [], upd=[])
        r_yt[s].writers = [x for x in r_yt[s].writers if x.eng != o.eng] + [o]

    def mixer_B(Q, subt, sample, pre):
        SG, QS, GS = RAW[0], RAW[1], RAW[2]
        tag = "s" if sample else "p"
        TRI, D1M, D2M, CHK, TRI4 = C["tri_" + tag], C["d1m_" + tag], C["d2m_" + tag], C["chk_" + tag], C["tri4_" + tag]
        for (s, P) in subt:
            L = LS if sample else 64
            NC = P // L
            rSG, rQS, rGS, rIB = res("raw0_%d" % s), res("raw1_%d" % s), res("raw2_%d" % s), res("ib_%d" % s)
            rF, rK = res("F"), res("KK")
            cq = slice(Q * 512, (Q + 1) * 512)
            pg.op("dve", lambda e, s=s, P=P: e.tensor_tensor(out=Ft[0:P, :], in0=SG[0:P, s, :], in1=OML[0:P, cq], op=ALU.mult),
                  reads=[rSG, res("OML")], writes=[rF])
            pg.op("dve", lambda e, P=P: e.tensor_tensor(out=Ft[0:P, :], in0=Ft[0:P, :], in1=LB[0:P, cq], op=ALU.add),
                  reads=[rF, res("LB")], writes=[rF])
            pg.op("dve", lambda e, P=P: e.tensor_scalar(out=KKt[0:P, :], in0=Ft[0:P, :], scalar1=-1.0, scalar2=1.0,
                                                        op0=ALU.mult, op1=ALU.add), reads=[rF], writes=[rK])
            pg.op("act", lambda e, P=P: e.activation(out=Ft[0:P, :], in_=Ft[0:P, :], func=AF.Ln), reads=[rF, rK], writes=[rF])
            if not pre:
                rGN = res("GN")
                pg.op("dve", lambda e, s=s, P=P: e.tensor_tensor(out=GNt[0:P, :], in0=GS[0:P, s, :], in1=NVO[0:P, cq], op=ALU.mult),
                      reads=[rGS, res("NVO")], writes=[rGN])
            yield
            def cum(M, P=P):
                ps, rps = aux()
                pg.op("pe", lambda e, ps=ps: e.matmul(ps[0:P, :], lhsT=M[0:P, 0:P], rhs=Ft[0:P, :], start=True, stop=True),
                      reads=[rF, r_setup], writes=[rps])
                return ps, rps
            psD2, rD2 = cum(D2M)
            psDC, rDC = aux()

            def dec(e, P=P, psDC=psDC, NC=NC):
                last = None
                for h in range(4):
                    last = e.matmul(psDC[:, h * NC:(h + 1) * NC], lhsT=Ft[0:P, h * 128:(h + 1) * 128], rhs=CHK[0:P, 0:NC],
                                    start=True, stop=True)
                return last
            pg.op("pe", dec, reads=[rF, r_setup], writes=[rDC])
            rDEC = res("DEC")
            pg.op("act", lambda e, psDC=psDC, NC=NC: e.activation(out=DECt[:, 0:4 * NC], in_=psDC[:, 0:4 * NC], func=AF.Exp),
                  reads=[rDC], writes=[rDEC])
            KE, rKE = BFt[3], res("KE")
            rE1 = res("E1")
            pg.op("act", lambda e, P=P, psD2=psD2: e.activation(out=Et[1][0:P, :], in_=psD2[0:P, :], func=AF.Exp),
                  reads=[rD2], writes=[rE1])
            pg.op("dve", lambda e, P=P: e.tensor_tensor(out=KE[0:P, :], in0=KKt[0:P, :], in1=Et[1][0:P, :], op=ALU.mult),
                  reads=[rK, rE1], writes=[rKE])
            if not pre:
                psB, rB = cum(TRI)
                psD1, rD1 = cum(D1M)
                QR, KR, QB = BFt[0], BFt[1], BFt[2]
                rQR, rKR, rQB = res("QR"), res("KR"), res("QB")
                rE0 = res("E0")
                pg.op("act", lambda e, P=P, psD1=psD1: e.activation(out=Et[0][0:P, :], in_=psD1[0:P, :], func=AF.Exp),
                      reads=[rD1], writes=[rE0])
                pg.op("dve", lambda e, s=s, P=P: e.tensor_tensor(out=QR[0:P, :], in0=QS[0:P, s, :], in1=Et[0][0:P, :], op=ALU.mult),
                      reads=[rQS, rE0], writes=[rQR])
                pg.op("act", lambda e, P=P, psD1=psD1: e.activation(out=Et[1][0:P, :], in_=psD1[0:P, :], func=AF.Exp, scale=-1.0),
                      reads=[rD1, rKE], writes=[rE1])
                pg.op("dve", lambda e, P=P: e.tensor_tensor(out=KR[0:P, :], in0=KKt[0:P, :], in1=Et[1][0:P, :], op=ALU.mult),
                      reads=[rK, rE1], writes=[rKR])
                pg.op("act", lambda e, P=P, psB=psB: e.activation(out=Et[0][0:P, :], in_=psB[0:P, :], func=AF.Exp),
                      reads=[rB, rQR], writes=[rE0])
                pg.op("dve", lambda e, s=s, P=P: e.tensor_tensor(out=QB[0:P, :], in0=QS[0:P, s, :], in1=Et[0][0:P, :], op=ALU.mult),
                      reads=[rQS, rE0], writes=[rQB])
            yield
            rS = res("S32")
            rSB = res("SB")
            hq = slice(4 * Q, 4 * Q + 4)
            if sample:
                dd = []
                for st in range(NSTR):
                    dd.append(dma("sp", S32[:, st * 4:(st + 1) * 4, :], s0[st, 4 * Q:4 * Q + 4, :, :].rearrange("h d v -> d h v"),
                                  writes=[rS] if st == 0 else [], reads=[] if st == 0 else []))
                rS.writers = dd
                if not pre:
                    pg.op("act", lambda e: e.activation(out=SBt[:, :, :, :].rearrange("d h c v -> d c h v"),
                                                        in_=S32[:, 0:16, :].rearrange("d (c h) v -> d c h v", h=4), func=AF.Copy),
                          reads=[rS], writes=[rSB])
                rKEM = res("KEM")
                for st in range(NSTR):
                    pg.op("dve", lambda e, st=st: e.tensor_scalar(out=KEM[:, st, :], in0=KE[0:64, :], scalar1=CHK[0:64, st:st + 1],
                                                                  scalar2=None, op0=ALU.mult),
                          reads=[rKE, r_setup], writes=[rKEM] if st == 0 else [], upd=[] if st == 0 else [rKEM])
            else:
                if not pre:
                    pg.op("act", lambda e: e.activation(out=SBt[:, :, 0, :], in_=S32[:, hq, :], func=AF.Copy),
                          reads=[rS], writes=[rSB])
            psDS = []
            for c in range(NC):
                ps, rps = aux()
                psDS.append((ps, rps))

                def dsm(e, c=c, ps=ps, L=L, s=s):
                    last = None
                    for h in range(4):
                        if sample:
                            lhsT = KEM[:, c, h * 128:(h + 1) * 128]
                            rhs = IBt[0:64, s, h * 128:(h + 1) * 128]
                        else:
                            lhsT = KE[c * L:(c + 1) * L, h * 128:(h + 1) * 128]
                            rhs = IBt[c * L:(c + 1) * L, s, h * 128:(h + 1) * 128]
                        last = e.matmul(ps[:, h * 128:(h + 1) * 128], lhsT=lhsT, rhs=rhs, start=True, stop=True)
                    return last
                pg.op("pe", dsm, reads=[res("KEM") if sample else rKE, rIB], writes=[rps])
            yield
            for c in range(NC):
                ps, rps = psDS[c]
                for h in range(4):
                    sidx = (c * 4 + h) if sample else (4 * Q + h)
                    pg.op("dve", lambda e, c=c, h=h, ps=ps, sidx=sidx, NC=NC: e.scalar_tensor_tensor(
                        out=S32[:, sidx, :], in0=S32[:, sidx, :], scalar=DECt[:, h * NC + c:h * NC + c + 1],
                        in1=ps[:, h * 128:(h + 1) * 128], op0=ALU.mult, op1=ALU.add),
                        reads=[rps, rDEC, rS] + ([rSB] if (not pre and (sample or c == 0)) else []), writes=[rS])
                if (not sample) and (not pre) and c + 1 < NC:
                    pg.op("act", lambda e, c=c: e.activation(out=SBt[:, :, c + 1, :], in_=S32[:, hq, :], func=AF.Copy),
                          reads=[rS], upd=[rSB])
                    rSB.writers = [pg.q["act"][-1]]
            if sample:
                for st in range(NSTR):
                    dma("sp", sso[st, 4 * Q:4 * Q + 4, :, :].rearrange("h d v -> d h v"), S32[:, st * 4:(st + 1) * 4, :], reads=[rS])
                    out_dmas.append(pg.q["sp"][-1])
                    rS.readers.append(pg.q["sp"][-1])
            if pre:
                continue
            psT1, rT1 = aux()
            psT2, rT2 = aux()
            pT1 = psT1[:, :].bitcast(BF16)
            pT2 = psT2[:, :].bitcast(BF16)

            def trq(e, P=P, pT1=pT1, pT2=pT2):
                last = None
                for h in range(4):
                    e.transpose(out=pT1[:, h * 128:h * 128 + P], in_=QR[0:P, h * 128:(h + 1) * 128], identity=ident_b[0:P, 0:P])
                    e.transpose(out=pT1[:, 512 + h * 128:512 + h * 128 + P], in_=KR[0:P, h * 128:(h + 1) * 128],
                                identity=ident_b[0:P, 0:P])
                    last = e.transpose(out=pT2[:, h * 128:h * 128 + P], in_=QB[0:P, h * 128:(h + 1) * 128],
                                       identity=ident_b[0:P, 0:P])
                return last
            pg.op("pe", trq, reads=[rQR, rKR, rQB, r_setup], writes=[rT1, rT2])
            QRT, KRT, QBT = TTt[0], TTt[1], TTt[2]
            rQRT, rKRT, rQBT = res("QRT"), res("KRT"), res("QBT")
            copy_op("act", QRT[:, :, 0:P], pT1[:, 0:512].rearrange("p (h t) -> p h t", t=128)[:, :, 0:P], [rT1], [rQRT])
            copy_op("dve", KRT[:, :, 0:P], pT1[:, 512:1024].rearrange("p (h t) -> p h t", t=128)[:, :, 0:P], [rT1], [rKRT])
            copy_op("act", QBT[:, :, 0:P], pT2[:, 0:512].rearrange("p (h t) -> p h t", t=128)[:, :, 0:P], [rT2], [rQBT])
            if sample:
                rQBM = res("QBM")
                for st in range(NSTR):
                    pg.op("dve", lambda e, st=st: e.tensor_tensor(out=QBM[:, :, st, :], in0=QBT[:, :, 0:64],
                                                                  in1=MS[:, st:st + 1, :].broadcast_to([128, 4, 64]) if False else
                                                                  MS[:, st, :].rearrange("p (o t) -> p o t", o=1).broadcast_to([128, 4, 64]),
                                                                  op=ALU.mult),
                          reads=[rQBT, r_setup], writes=[rQBM] if st == 0 else [], upd=[] if st == 0 else [rQBM])
            yield
            psS, rpS = aux()

            def sc(e, P=P, psS=psS):
                last = None
                for h in range(4):
                    last = e.matmul(psS[0:P, h * P:(h + 1) * P], lhsT=KRT[:, h, 0:P], rhs=QRT[:, h, 0:P], start=True, stop=True)
                return last
            pg.op("pe", sc, reads=[rQRT, rKRT], writes=[rpS])
            SCM, rSCM = BFt[4], res("SCM")
            pg.op("dve", lambda e, P=P, psS=psS: e.tensor_tensor(out=SCM[0:P, 0:4 * P], in0=psS[0:P, 0:4 * P], in1=TRI4[0:P, 0:4 * P],
                                                                 op=ALU.mult), reads=[rpS, r_setup], writes=[rSCM])
            yield
            psO, rpO = aux()

            def om(e, s=s, P=P, psO=psO, NC=NC, L=L):
                last = None
                for h in range(4):
                    e.matmul(psO[0:P, h * 128:(h + 1) * 128], lhsT=SCM[0:P, h * P:(h + 1) * P], rhs=IBt[0:P, s, h * 128:(h + 1) * 128],
                             start=True, stop=False)
                    for c in range(NC):
                        if sample:
                            last = e.matmul(psO[0:P, h * 128:(h + 1) * 128], lhsT=QBM[:, h, c, :], rhs=SBt[:, h, c, :],
                                            start=False, stop=(c == NC - 1))
                        else:
                            last = e.matmul(psO[c * L:(c + 1) * L, h * 128:(h + 1) * 128], lhsT=QBT[:, h, c * L:(c + 1) * L],
                                            rhs=SBt[:, h, c, :], start=False, stop=True)
                return last
            pg.op("pe", om, reads=[rSCM, rIB, rSB, rQBT] + ([res("QBM")] if sample else []), writes=[rpO])
            rso = res("sso")
            for h in range(4):
                pg.op("act", lambda e, P=P, h=h, psO=psO: e.activation(out=JUNK[0:P, 0:128], in_=psO[0:P, h * 128:(h + 1) * 128],
                                                                        func=AF.Square, accum_out=SM[0:P, 16 + h:17 + h]),
                      reads=[rpO], writes=[res("junk")], upd=[rso])
            rso.writers = [pg.q["act"][-1]]
            rstd_from_ss(SM[0:P, 16:20], P, 4, 1.0 / 128, rso)
            YB, rYB = BFt[5], res("YB")
            for h in range(4):
                pg.op("dve", lambda e, P=P, h=h, psO=psO: e.scalar_tensor_tensor(
                    out=YB[0:P, h * 128:(h + 1) * 128], in0=psO[0:P, h * 128:(h + 1) * 128], scalar=SM[0:P, 16 + h:17 + h],
                    in1=GNt[0:P, h * 128:(h + 1) * 128], op0=ALU.mult, op1=ALU.mult),
                    reads=[rpO, rso, res("GN")], writes=[rYB] if h == 0 else [], upd=[] if h == 0 else [rYB])
            yield
            y_to_T(YB, rYB, s, P, KT // 2 + 4 * Q)
            yield

    out_dmas = []
    r_hidt = [Res(), Res()]
    last_barrier_dma = {"sp": 0, "pool": 0, "act": 0}

    def barrier():
        lasts = []
        for e in ENGS:
            for o in reversed(pg.q[e]):
                if not o.dma and o.fn is not None and getattr(o, "real", True):
                    lasts.append(o)
                    break
        dmas = []
        for e in ("sp",):
            dmas.extend(pg.dma_ops[e][last_barrier_dma[e]:])
            last_barrier_dma[e] = len(pg.dma_ops[e])
        rb = Res()
        rb.writers = lasts + dmas
        for e in ("pe", "act", "dve", "sp"):
            pg.op(e, None, reads=[rb])

    def load_bc(dst, src1d, n, name):
        return dma("sp", dst[:, 0:n], src1d.partition_broadcast(128), writes=[res(name)])

    def act_evac(func, dst_fn, rname):
        def ev(s, P, ps, rps):
            out = dst_fn(s, P)
            pg.op("act", lambda e: e.activation(out=out, in_=ps, func=func), reads=[rps], writes=[res(rname % s)])
        return ev

    def do_tile(kind, xsrc, tok0, subt, ydst):
        sample = kind == "sample"
        pre = kind == "pre"
        for (s, P) in subt:
            b = s % 2
            dma("sp", XIN[b][0:P, :], xsrc[tok0 + s * 128: tok0 + s * 128 + P, :], writes=[r_xin[b]])
            norm_to_T([(s, XIN[b][0:P, :], P, r_xin[b])], GC1, HT, r_ht)
        barrier()
        ck("s0")
        if not pre:
            load_bc(NVO, norm_v, WA, "NVO")
            for jp in range(cfg.NA):
                def ev_flush(inner):
                    def ev(s, P, ps, rps):
                        flush()
                        inner(s, P, ps, rps)
                    return ev
                gemm(HT, r_ht, KT, w_in, 0, jp * 512, subt,
                     ev_flush(act_evac(AF.Gelu_apprx_tanh, lambda s, P: RAW[0][0:P, s, :], "raw0_%d")), wname="w_in", npump=2)
                gemm(HT, r_ht, KT, w_in, 0, WA + jp * 512, subt,
                     ev_flush(act_evac(AF.Gelu_apprx_tanh, lambda s, P: RAW[1][0:P, s, :], "raw1_%d")), wname="w_in", npump=2)
                drive(mixer_A(jp, subt, sample))
            flush()
            load_bc(NVO, norm_o, WB, "NVO")
        for Q in range(cfg.NQ):
            def ev_flush(inner):
                def ev(s, P, ps, rps):
                    flush()
                    inner(s, P, ps, rps)
                return ev
            c_q, c_f, c_i, c_g = 2 * WA + Q * 512, 2 * WA + WB + Q * 512, 2 * WA + 2 * WB + Q * 512, 2 * WA + 3 * WB + Q * 512
            gemm(HT, r_ht, KT, w_in, 0, c_f, subt, ev_flush(act_evac(AF.Sigmoid, lambda s, P: RAW[0][0:P, s, :], "raw0_%d")), wname="w_in", npump=2)
            gemm(HT, r_ht, KT, w_in, 0, c_i, subt, ev_flush(act_evac(AF.Copy, lambda s, P: IBt[0:P, s, :], "ib_%d")), wname="w_in", npump=2)
            if not pre:
                gemm(HT, r_ht, KT, w_in, 0, c_q, subt, ev_flush(act_evac(AF.Silu, lambda s, P: RAW[1][0:P, s, :], "raw1_%d")), wname="w_in", npump=2)
                gemm(HT, r_ht, KT, w_in, 0, c_g, subt, ev_flush(act_evac(AF.Silu, lambda s, P: RAW[2][0:P, s, :], "raw2_%d")), wname="w_in", npump=2)
            drive(mixer_B(Q, subt, sample, pre))
        flush()
        barrier()
        ck("s1" + kind)
        if pre:
            return
        r_x = [res("x1_%d" % s) for s in range(S)]
        for (s, P) in subt:
            dma("sp", X1[0:P, s, :], xsrc[tok0 + s * 128: tok0 + s * 128 + P, :], writes=[r_x[s]])

        def add_evac(cb):
            def ev(s, P, ps, rps):
                pg.op("dve", lambda e: e.tensor_tensor(out=X1[0:P, s, cb * 512:(cb + 1) * 512], in0=ps,
                                                       in1=X1[0:P, s, cb * 512:(cb + 1) * 512], op=ALU.add),
                      reads=[rps, r_x[s]], writes=[r_x[s]])
            return ev
        for cb in range(D // 512):
            gemm(YT, r_yt, KT, w_out, 0, cb * 512, subt, add_evac(cb), wname="w_out")
        norm_to_T([(s, X1[0:P, s, :], P, r_x[s]) for (s, P) in subt], GC2, HT, r_ht)
        ck("s2")
        NCH = DFF // cfg.FC
        KF = cfg.FC // 128
        for j in range(NCH):
            hb = HIDT[j % 2]
            rh = r_hidt[j % 2]
            wl = {}
            for cb in range(cfg.FC // 512):
                def ev(s, P, ps, rps, cb=cb, hb=hb, rh=rh):
                    k = (cb * S + s) % 2
                    rrt, rhtm = res("RT"), res("HTM%d" % k)
                    pg.op("act", lambda e: e.activation(out=RT[0:P, :], in_=ps, func=AF.Relu), reads=[rps], writes=[rrt])
                    pg.op("dve", lambda e: e.tensor_tensor(out=HTM[k][0:P, :], in0=RT[0:P, :], in1=RT[0:P, :], op=ALU.mult),
                          reads=[rrt], writes=[rhtm])

                    def tr_step():
                        ps2, rps2 = aux()
                        psb = ps2[:, :].bitcast(BF16)

                        def tr(e):
                            last = None
                            for jj in range(4):
                                last = e.transpose(out=psb[:, jj * 128:jj * 128 + P], in_=HTM[k][0:P, jj * 128:(jj + 1) * 128],
                                                   identity=ident_b[0:P, 0:P])
                            return last
                        pg.op("pe", tr, reads=[rhtm, r_setup], writes=[rps2])
                        o = copy_op(evac_eng(), hb[:, cb * 4:cb * 4 + 4, s * 128:s * 128 + P],
                                    psb[:, 0:512].rearrange("p (j t) -> p j t", t=128)[:, :, 0:P], [rps2], [])
                        rh.writers = [x for x in rh.writers if x.eng != o.eng] + [o]
                    deferred.append(tr_step)
                if cb == 0:
                    pg.op("act", None, writes=[rh])
                    pg.op("dve", None, writes=[rh])
                    rh.writers = []
                gemm(HT, r_ht, KT, w_up, 0, j * cfg.FC + cb * 512, subt, ev, wname="w_up")
            flush()
            r_h4 = [rh] * S
            for cb in range(D // 512):
                gemm(hb, r_h4, KF, w_down, j * cfg.FC, cb * 512, subt, add_evac(cb), wname="w_down")
        barrier()
        ck("s3")
        load_bc(NFBt, norm_f, D, "NFB")
        for (s, P) in subt:
            ssap = SM[0:P, s:s + 1]
            rss = res("ss%d" % s)
            sumsq(X1[0:P, s, :], P, ssap, r_x[s], rss)
            pg.op("dve", lambda e, s=s, P=P, ssap=ssap: e.scalar_tensor_tensor(out=X1[0:P, s, :], in0=X1[0:P, s, :], scalar=ssap,
                                                                                in1=NFBt[0:P, :], op0=ALU.mult, op1=ALU.mult),
                  reads=[rss, res("NFB"), r_x[s]], writes=[r_x[s]])
            dma("sp", ydst[tok0 + s * 128: tok0 + s * 128 + P, :], X1[0:P, s, :], reads=[r_x[s]])
            out_dmas.append(pg.q["sp"][-1])
        barrier()

    full = [(s, 128) for s in range(S)]
    barrier()
    try:
        ck("setup")
        for t in range(NP // T):
            do_tile("pre", xprev, t * T, full, None)
            ck("pre%d" % t)
        for t in range(NP // T):
            do_tile("prompt", xp, t * T, full, yp)
            ck("prompt%d" % t)
        dma("sp", spo.rearrange("h d v -> d h v"), S32[:, 0:HB, :], reads=[res("S32")])
        out_dmas.append(pg.q["sp"][-1])
        res("S32").readers.append(pg.q["sp"][-1])
        barrier()
        do_tile("sample", xs, 0, [(0, 64)], ys)
    except _Stop:
        del deferred[:]
        barrier()
    rfin = Res()
    rfin.writers = list(out_dmas)
    pg.op("sp", None, reads=[rfin])
    return nc, pg


_CACHE = {}


def get_program(cfg_key, cfg):
    if cfg_key not in _CACHE:
        nc, pg = build(cfg)
        es = ExitStack()
        pg.emit(nc, es)
        es.close()
        _CACHE[cfg_key] = nc
    return _CACHE[cfg_key]


def run(cfg, x_prompt, x_sample, state_hgrn, norm1, w_in, w_s, b_s, norm_v, lb_logits, norm_o,
        w_out, norm2, w_up, w_down, norm_f, trace=False):
    f = lambda a: np.ascontiguousarray(np.asarray(a, dtype=np.float32))
    x_prompt, x_sample, state_hgrn = f(x_prompt), f(x_sample), f(state_hgrn)
    D, NP, NSTR, LS, HB, WA = cfg.D, cfg.NP, cfg.NSTR, cfg.LS, cfg.HB, cfg.WA
    n = cfg.n_cores
    B = x_prompt.shape[0]
    assert n == 2 * B and x_prompt.shape[1] == 2 * NP
    shared = {
        "w_in": f(w_in[0]), "w_out": f(w_out[0]), "w_up": f(w_up[0]), "w_down": f(w_down[0]),
        "norm1": f(norm1[0]), "norm2": f(norm2[0]), "norm_f": f(norm_f), "norm_v": f(norm_v[0]).reshape(-1),
        "norm_o": f(norm_o[0]).reshape(-1), "lb_logits": f(lb_logits), "w_s": f(w_s[0]), "b_s": f(b_s[0]),
    }
    for k, v in make_consts().items():
        shared["c_" + k] = v
    zeros = np.zeros((NP, D), np.float32)
    in_maps = []
    for c in range(n):
        b, half = c // 2, c % 2
        m = dict(shared)
        m["xp"] = x_prompt[b, half * NP:(half + 1) * NP]
        m["xprev"] = x_prompt[b, 0:NP] if half == 1 else zeros
        m["xs"] = x_sample[c * NSTR:(c + 1) * NSTR].reshape(NSTR * LS, D)
        m["s0"] = state_hgrn[0, c * NSTR:(c + 1) * NSTR]
        in_maps.append(m)
    nc = get_program((D, NP, n), cfg)
    r = run_bass_kernel_spmd(nc, in_maps, core_ids=list(range(n)), **({"trace": True} if trace else {}))
    outs = r.results
    y_prompt = np.zeros((B, 2 * NP, D), np.float32)
    y_sample = np.zeros((n * NSTR, LS, D), np.float32)
    st_p = np.zeros((1, B, HB, 128, 128), np.float32)
    st_s = np.zeros((1, n * NSTR, HB, 128, 128), np.float32)
    v_s = np.zeros((1, n * NSTR, LS, WA), np.float32)
    for c in range(n):
        b, half = c // 2, c % 2
        o = outs[c]
        y_prompt[b, half * NP:(half + 1) * NP] = o["yp"]
        y_sample[c * NSTR:(c + 1) * NSTR] = o["ys"].reshape(NSTR, LS, D)
        if half == 1:
            st_p[0, b] = o["spo"]
        st_s[0, c * NSTR:(c + 1) * NSTR] = o["sso"]
        v_s[0, c * NSTR:(c + 1) * NSTR] = o["vso"].reshape(NSTR, LS, WA)
    if trace:
        return (y_prompt, y_sample, st_p, st_s, v_s), r
    return (y_prompt, y_sample, st_p, st_s, v_s)


def kernel(**inputs):
    cfg = Cfg()
    return run(cfg, **inputs)
```

```python
import numpy as np
import ml_dtypes
from contextlib import ExitStack
import concourse.bass as bass
import concourse.mybir as mybir
from concourse.bass_utils import run_bass_kernel_spmd

F32 = mybir.dt.float32
BF16 = mybir.dt.bfloat16
AF = mybir.ActivationFunctionType
ALU = mybir.AluOpType
EPS = 1e-6


class Cfg:
    def __init__(self, D=4096, HA=8, HB=16, NP=2048, NSTR=4, LS=16, S=4, n_cores=8, WSLOTS=4):
        self.D = D
        self.HA = HA
        self.HB = HB
        self.WA = D // 2
        self.WB = D // 2
        self.HDA = self.WA // HA
        assert self.HDA == 256 and self.WB // HB == 128
        self.DFF = 4 * D
        self.KT = D // 128
        self.NIN = 2 * self.WA + 4 * self.WB
        self.NP = NP
        self.NSTR = NSTR
        self.LS = LS
        self.NS = NSTR * LS
        assert self.NS == 64
        self.S = S
        self.T = S * 128
        assert NP % self.T == 0
        self.n_cores = n_cores
        self.WSLOTS = WSLOTS
        self.KS = min(8, self.KT)
        self.FC = min(2048, self.DFF)
        self.NQ = self.WB // 512
        self.NA = self.WA // 512
        assert self.WA % 512 == 0 and self.WB % 512 == 0


class Op:
    __slots__ = ("eng", "fn", "deps", "waited", "semval", "dma", "dma_idx")

    def __init__(self, eng, fn, dma):
        self.eng = eng
        self.fn = fn
        self.dma = dma
        self.waited = False
        self.semval = 0
        self.deps = []
        self.dma_idx = -1


class Res:
    __slots__ = ("writers", "readers", "excl")

    def __init__(self, excl=False):
        self.writers = []
        self.readers = []
        self.excl = excl


NDMASEM = 6
STOP_AT = None


class _Stop(Exception):
    pass


def ck(name):
    if STOP_AT == name:
        raise _Stop()
ENGS = ("pe", "act", "dve", "pool", "sp")


class Prog:
    def __init__(self):
        self.q = {e: [] for e in ENGS}
        self.dma_ops = {"sp": [], "pool": [], "act": []}

    def op(self, eng, fn, reads=(), writes=(), upd=(), dma=False):
        o = Op(eng, fn, dma)
        deps = []
        for r in reads:
            deps.extend(r.writers)
            if r.excl:
                deps.extend(x for x in r.readers if x.eng != eng)
        for w in writes:
            deps.extend(w.writers)
            deps.extend(w.readers)
        if dma:
            lst = self.dma_ops[eng]
            if len(lst) >= NDMASEM:
                deps.append(lst[-NDMASEM])
            o.dma_idx = len(lst)
            lst.append(o)
        seen = set()
        i = 0
        while i < len(deps):
            d = deps[i]
            i += 1
            if id(d) in seen:
                continue
            seen.add(id(d))
            if d.fn is None:
                deps.extend(d.deps)
                continue
            if d.eng == "pe" and eng == "pe" and not d.dma:
                continue
            d.waited = True
            o.deps.append(d)
        for w in writes:
            w.writers = [o]
            w.readers = []
        for w in upd:
            w.writers = [o]
        for r in reads:
            r.readers = [x for x in r.readers if x.dma or x.eng != eng] + [o]
        self.q[eng].append(o)
        return o

    def emit(self, nc, es):
        sems = {e: es.enter_context(nc.semaphore("s_" + e)) for e in ENGS}
        dsems = {e: [es.enter_context(nc.semaphore("d_%s%d" % (e, i))) for i in range(NDMASEM)]
                 for e in self.dma_ops}
        for e in ENGS:
            c = 0
            for o in self.q[e]:
                if o.waited and not o.dma:
                    c += 1
                    o.semval = c
        block = es.enter_context(nc.Block())

        def run(e, eng):
            waited = {}
            for o in self.q[e]:
                for d in o.deps:
                    if d.dma:
                        sem = dsems[d.eng][d.dma_idx % NDMASEM]
                        val = 16 * (d.dma_idx // NDMASEM + 1)
                    else:
                        sem = sems[d.eng]
                        val = d.semval
                    key = id(sem)
                    if waited.get(key, 0) >= val:
                        continue
                    waited[key] = val
                    eng.wait_ge(sem, val)
                if o.fn is None:
                    continue
                inst = o.fn(eng)
                if o.dma:
                    inst.then_inc(dsems[e][o.dma_idx % NDMASEM], 16)
                elif o.waited:
                    inst.then_inc(sems[e], 1)

        @block.tensor
        def _(eng):
            run("pe", eng)

        @block.scalar
        def _(eng):
            run("act", eng)

        @block.vector
        def _(eng):
            run("dve", eng)

        @block.gpsimd
        def _(eng):
            run("pool", eng)

        @block.sync
        def _(eng):
            run("sp", eng)


def make_consts():
    c = {}
    c["ident_f"] = np.eye(128, dtype=np.float32)
    c["ident_b"] = np.eye(128).astype(ml_dtypes.bfloat16)
    for tag, P, L in (("p", 128, 64), ("s", 64, 16)):
        s = np.arange(P)[:, None]
        t = np.arange(P)[None, :]
        same = (s // L) == (t // L)
        tri = same & (s <= t)
        ref = (t // L) * L + L // 2
        refm = same & (s <= ref)
        endm = same
        c["tri_" + tag] = tri.astype(np.float32)
        c["d1m_" + tag] = tri.astype(np.float32) - refm.astype(np.float32)
        c["d2m_" + tag] = endm.astype(np.float32) - tri.astype(np.float32)
        nc_ = P // L
        chk = (np.arange(P)[:, None] // L) == np.arange(nc_)[None, :]
        c["chk_" + tag] = chk.astype(np.float32)
        c["tri4_" + tag] = np.tile(tri.astype(np.float32), (1, 4)).astype(ml_dtypes.bfloat16)
    ms = (np.arange(64)[None, :] // 16) == np.arange(4)[:, None]
    c["ms"] = np.broadcast_to(ms[None].astype(np.float32), (128, 4, 64)).astype(ml_dtypes.bfloat16).copy()
    j = np.arange(128)[:, None]
    i = np.arange(128)[None, :]
    c["maskg"] = (~((i < 64) & (j >= 64))).astype(np.float32)
    return c


def build(cfg):
    nc = bass.Bass("TRN2", target_bir_lowering=False)
    D, KT, S, T = cfg.D, cfg.KT, cfg.S, cfg.T
    WA, WB, HA, HB, DFF = cfg.WA, cfg.WB, cfg.HA, cfg.HB, cfg.DFF
    NP, NS, NSTR, LS = cfg.NP, cfg.NS, cfg.NSTR, cfg.LS
    KS = cfg.KS

    def din(name, shape, dt=F32):
        return nc.dram_tensor(name, list(shape), dt, kind="ExternalInput").ap()

    def dout(name, shape, dt=F32):
        return nc.dram_tensor(name, list(shape), dt, kind="ExternalOutput").ap()

    xp = din("xp", [NP, D])
    xprev = din("xprev", [NP, D])
    xs = din("xs", [NS, D])
    s0 = din("s0", [NSTR, HB, 128, 128])
    w_in = din("w_in", [D, cfg.NIN])
    w_out = din("w_out", [D, D])
    w_up = din("w_up", [D, DFF])
    w_down = din("w_down", [DFF, D])
    norm1 = din("norm1", [D])
    norm2 = din("norm2", [D])
    norm_f = din("norm_f", [D])
    norm_v = din("norm_v", [WA])
    norm_o = din("norm_o", [WB])
    lbl = din("lb_logits", [2, WB])
    w_s = din("w_s", [HA, 128, 128])
    b_s = din("b_s", [HA, 128])
    cst = {}
    for k, v in make_consts().items():
        cst[k] = din("c_" + k, v.shape, BF16 if v.dtype == ml_dtypes.bfloat16 else F32)

    yp = dout("yp", [NP, D])
    ys = dout("ys", [NS, D])
    spo = dout("spo", [HB, 128, 128])
    sso = dout("sso", [NSTR, HB, 128, 128])
    vso = dout("vso", [NS, WA])

    NSLAB = (D * cfg.NIN + D * D + 2 * D * DFF) // (KS * 128 * 512)
    WSC_CH = 64
    wsc_t = [nc.dram_tensor("wscratch%d" % i, [WSC_CH, 128, KS * 512], BF16, kind="Internal").ap()
             for i in range((NSLAB + WSC_CH - 1) // WSC_CH)]

    class _Wsc:
        def __getitem__(self, key):
            idx = key[0]
            return wsc_t[idx // WSC_CH][(idx % WSC_CH,) + tuple(key[1:])]
    wsc = _Wsc()
    wcache = {}
    r_wsc = {}

    base = [nc._sbuf_addr_for_side(None)]
    base[0] = (base[0] + 63) // 64 * 64
    limit = base[0] + nc.sbuf_bytes_remaining - 64

    def salloc(name, shape, dt, at=None):
        nbytes = int(np.prod(shape[1:])) * (2 if dt == BF16 else 4)
        nbytes = (nbytes + 63) // 64 * 64
        if at is None:
            off = base[0]
            base[0] += nbytes
            assert base[0] <= limit, ("SBUF overflow", name, base[0], limit)
        else:
            off = at
        return nc.alloc_sbuf_tensor_at(name, list(shape), dt, offset=off), off, nbytes

    X1, X1off, X1bytes = salloc("X1", [128, S, D], F32)
    if X1bytes < 65536:
        _, _, padb = salloc("X1pad", [128, (65536 - X1bytes) // 4], F32)
        X1bytes += padb
    HT, _, _ = salloc("HT", [128, KT, T], BF16)
    YT, YToff, YTbytes = salloc("YT", [128, KT, T], BF16)
    need_yt = max(2 * (cfg.FC // 128) * T * 2, D * 4)
    if YTbytes < need_yt:
        _, _, padb = salloc("YTpad", [128, (need_yt - YTbytes) // 4], F32)
        YTbytes += padb
    WR = [salloc("WR%d" % i, [128, KS, 512], BF16)[0] for i in range(cfg.WSLOTS)]
    S32, _, _ = salloc("S32", [128, max(HB, 16), 128], F32)
    LB, _, _ = salloc("LB", [128, WB], F32)
    OML, _, _ = salloc("OML", [128, WB], F32)
    ident_f, _, _ = salloc("ident_f", [128, 128], F32)
    ident_b, _, _ = salloc("ident_b", [128, 128], BF16)
    C = {}
    for tag, P in (("p", 128), ("s", 64)):
        for nm in ("tri", "d1m", "d2m"):
            C[nm + "_" + tag], _, _ = salloc(nm + "_" + tag, [P, P], F32)
        C["chk_" + tag], _, _ = salloc("chk_" + tag, [P, P // (64 if tag == "p" else 16)], F32)
        C["tri4_" + tag], _, _ = salloc("tri4_" + tag, [P, 4 * P], BF16)
    MS, _, _ = salloc("MS", [128, 4, 64], BF16)
    WT, _, _ = salloc("WT", [128, HA, 128], BF16)
    WTS, _, _ = salloc("WTS", [64, HA, 64], BF16)
    BSC, _, _ = salloc("BSC", [128, HA], F32)
    BSS, _, _ = salloc("BSS", [64, HA], F32)
    GC1, _, _ = salloc("GC1", [128, KT], F32)
    GC2, _, _ = salloc("GC2", [128, KT], F32)
    SM, _, _ = salloc("SM", [128, 64], F32)
    EPSC, _, _ = salloc("EPSC", [128, 1], F32)

    ov = [X1off]

    def oalloc(name, shape, dt):
        t, off, nb = salloc(name, shape, dt, at=ov[0])
        ov[0] += nb
        assert ov[0] <= X1off + X1bytes, ("overlay overflow", name)
        return t

    NVO = oalloc("NVO", [128, max(WA, WB)], F32)
    RAW = [oalloc("RAW%d" % i, [128, S, 512], F32) for i in range(3)]
    IBt = oalloc("IB", [128, S, 512], BF16)
    IBt2 = oalloc("IB2", [128, S, 512], BF16)
    Ft = oalloc("Ft", [128, 512], F32)
    KKt = oalloc("KKt", [128, 512], F32)
    Et = [oalloc("Et%d" % i, [128, 512], F32) for i in range(2)]
    GNt = oalloc("GNt", [128, 512], F32)
    BFt = [oalloc("BFt%d" % i, [128, 512], BF16) for i in range(6)]
    TTt = [oalloc("TTt%d" % i, [128, 4, 128], BF16) for i in range(3)]
    SBt = oalloc("SBt", [128, 4, 4, 128], BF16)
    DECt = oalloc("DECt", [128, 16], F32)
    if S >= 4:
        raw2_off = X1off + (max(WA, WB) * 4 + 63) // 64 * 64 + 2 * (S * 512 * 4)
        KEM = nc.alloc_sbuf_tensor_at("KEM", [64, 4, 512], BF16, offset=raw2_off + 2048)
        QBM = nc.alloc_sbuf_tensor_at("QBM", [128, 4, 4, 64], BF16, offset=raw2_off + 2048 + 4096)
    else:
        KEM = oalloc("KEM", [64, 4, 512], BF16)
        QBM = oalloc("QBM", [128, 4, 4, 64], BF16)
    stage1_end = ov[0]
    ov[0] = X1off
    XIN = [oalloc("XIN%d" % i, [128, D], F32) for i in range(2)]
    HIDT = [nc.alloc_sbuf_tensor_at("HIDT%d" % i, [128, cfg.FC // 128, T], BF16,
                                    offset=YToff + i * (cfg.FC // 128) * T * 2) for i in range(2)]
    assert 2 * (cfg.FC // 128) * T * 2 <= YTbytes
    JUNK, _, _ = salloc("JUNK", [128, 512], BF16)
    XSCP = [salloc("XSCP%d" % i, [128, 512], F32)[0] for i in range(2)]
    RT, _, _ = salloc("RT", [128, 512], F32)
    HTM = [salloc("HTM%d" % i, [128, 512], BF16)[0] for i in range(2)]
    NFB = None
    NFBt = nc.alloc_sbuf_tensor_at("NFB", [128, D], F32, offset=YToff)
    assert D * 4 <= YTbytes

    GPS = [nc.alloc_psum_tensor("gps%d" % i, [128, 512], F32) for i in range(4)]
    AUX = [nc.alloc_psum_tensor("aux%d" % i, [128, 512], F32) for i in range(4)]

    pg = Prog()
    R = {}

    def res(name):
        if name not in R:
            R[name] = Res()
        return R[name]

    r_gps = [Res(excl=True) for _ in range(4)]
    r_aux = [Res(excl=True) for _ in range(4)]
    r_wr = [Res() for _ in range(cfg.WSLOTS)]
    r_x1reg = Res()
    aux_i = [0]

    def aux():
        i = aux_i[0] % 4
        aux_i[0] += 1
        return AUX[i], r_aux[i]

    wr_i = [0]

    flip = [0]

    def evac_eng():
        flip[0] ^= 1
        return "act" if flip[0] else "dve"

    def copy_op(eng, out, in_, reads, writes, upd=()):
        if eng == "act":
            return pg.op("act", lambda e: e.activation(out=out, in_=in_, func=AF.Copy), reads, writes, upd)
        return pg.op("dve", lambda e: e.tensor_copy(out=out, in_=in_), reads, writes, upd)

    deferred = []

    def pump():
        if deferred:
            deferred.pop(0)()

    def flush():
        while deferred:
            deferred.pop(0)()

    def drive(gen):
        def step():
            try:
                next(gen)
                deferred.insert(0, step)
            except StopIteration:
                pass
        deferred.append(step)

    def dma(eng, out, in_, reads=(), writes=()):
        return pg.op(eng, lambda e: e.dma_start(out=out, in_=in_, allow_slow_non_contiguous=True), reads, writes, dma=True)

    r_const = res("const")
    setup_ops = []
    setup_ops.append(dma("sp", ident_f[:, :], cst["ident_f"][:, :]))
    setup_ops.append(dma("sp", ident_b[:, :], cst["ident_b"][:, :]))
    for tag in ("p", "s"):
        for nm in ("tri", "d1m", "d2m", "chk", "tri4"):
            k = nm + "_" + tag
            setup_ops.append(dma("sp", C[k][:, :], cst[k][:, :]))
    setup_ops.append(dma("sp", MS[:, :, :], cst["ms"][:, :, :]))
    with nc.allow_non_contiguous_dma(reason="tiny parameter loads"):
        setup_ops.append(dma("sp", GC1[:, :], norm1.rearrange("(kt p) -> p kt", p=128)))
        setup_ops.append(dma("sp", GC2[:, :], norm2.rearrange("(kt p) -> p kt", p=128)))
        setup_ops.append(dma("sp", BSC[:, :], b_s.rearrange("h i -> i h")))
        for st in range(NSTR):
            setup_ops.append(dma("sp", BSS[st * LS:(st + 1) * LS, :], b_s[:, 0:LS].rearrange("h i -> i h")))
    setup_ops.append(dma("sp", LB[:, :], lbl[0, :].partition_broadcast(128)))
    setup_ops.append(dma("sp", OML[:, :], lbl[1, :].partition_broadcast(128)))
    r_setup = Res()
    r_setup.writers = list(setup_ops)
    o1 = pg.op("dve", lambda e: e.tensor_tensor(out=LB[:, :], in0=LB[:, :], in1=OML[:, :], op=ALU.subtract),
               reads=[r_setup], writes=[res("LB")])
    pg.op("act", lambda e: e.activation(out=LB[:, :], in_=LB[:, :], func=AF.Sigmoid), reads=[], writes=[res("LB")])
    pg.op("dve", lambda e: e.tensor_scalar(out=OML[:, :], in0=LB[:, :], scalar1=-1.0, scalar2=1.0,
                                           op0=ALU.mult, op1=ALU.add), reads=[res("LB")], writes=[res("OML")])
    pg.op("dve", lambda e: e.memset(EPSC[:, :], EPS), writes=[res("EPSC")])
    pg.op("dve", lambda e: e.memset(S32[:, :, :], 0.0), writes=[res("S32")])

    r_stage0 = res("x1region_setup")
    WSTG = XIN[0]
    assert HA * 128 <= D
    MG = XIN[1]
    o_ws = dma("sp", WSTG[:, 0:HA * 128].rearrange("p (h j) -> p h j", j=128), w_s.rearrange("h i j -> i h j"),
               writes=[r_stage0])
    o_mg = dma("sp", MG[:, D - 128:D], cst["maskg"][:, :], reads=[r_stage0])
    r_mg = Res()
    r_mg.writers = [o_mg]
    for h in range(HA):
        ps, rps = aux()
        pg.op("pe", lambda e, h=h, ps=ps: e.transpose(out=ps[:, 0:128], in_=WSTG[:, h * 128:(h + 1) * 128],
                                                      identity=ident_f[:, :]),
              reads=[r_stage0, r_setup], writes=[rps])
        pg.op("dve", lambda e, h=h, ps=ps: e.tensor_tensor(out=WT[:, h, :], in0=ps[:, 0:128], in1=MG[:, D - 128:D],
                                                           op=ALU.mult),
              reads=[rps, r_mg], upd=[res("WT")])
    WSS = XIN[1]
    o_z = pg.op("dve", lambda e: e.memset(WSS[0:64, 0:HA * 64], 0.0), reads=[r_stage0], writes=[res("WSS")])
    with nc.allow_non_contiguous_dma(reason="tiny parameter loads"):
        dd = []
        for st in range(NSTR):
            dd.append(dma("sp", WSS[st * LS:(st + 1) * LS, 0:HA * 64].rearrange("p (h j) -> p h j", j=64)[:, :, st * LS:(st + 1) * LS],
                          w_s[:, 0:LS, 0:LS].rearrange("h i j -> i h j"), reads=[res("WSS")]))
    r_wss = Res()
    r_wss.writers = dd
    for h in range(HA):
        ps, rps = aux()
        pg.op("pe", lambda e, h=h, ps=ps: e.transpose(out=ps[0:64, 0:64], in_=WSS[0:64, h * 64:(h + 1) * 64],
                                                      identity=ident_f[0:64, 0:64]),
              reads=[r_wss, r_setup], writes=[rps])
        pg.op("dve", lambda e, h=h, ps=ps: e.tensor_copy(out=WTS[:, h, :], in_=ps[0:64, 0:64]),
              reads=[rps], upd=[res("WT")])
    r_x1 = res("x1coarse")

    def stage_barrier(tag):
        pass

    def rstd_from_ss(ssap, P, n, invn, rkey):
        pg.op("act", lambda e: e.activation(out=ssap, in_=ssap, func=AF.Ln, scale=invn, bias=EPSC[0:P, :]),
              reads=[rkey, res("EPSC")], writes=[rkey])
        pg.op("act", lambda e: e.activation(out=ssap, in_=ssap, func=AF.Exp, scale=-0.5), reads=[rkey], writes=[rkey])

    NPC = D // 512

    def sumsq(xap, P, ssap, rsrc, rss):
        rpart = res("sspart")
        ck("n0")
        for c in range(NPC):
            pg.op("act", lambda e, c=c: e.activation(out=JUNK[0:P, :], in_=xap[:, c * 512:(c + 1) * 512], func=AF.Square,
                                                     accum_out=SM[0:P, 32 + c:33 + c]),
                  reads=[rsrc], writes=[res("junk")], upd=[rpart])
        pg.op("dve", lambda e: e.tensor_reduce(out=ssap, in_=SM[0:P, 32:32 + NPC], axis=mybir.AxisListType.X, op=ALU.add),
              reads=[rpart], writes=[rss])
        rstd_from_ss(ssap, P, 1, 1.0 / D, rss)

    def norm_to_T(srcs, gcol, dstT, dst_res):
        for (s, xap, P, rsrc) in srcs:
            ssap = SM[0:P, s:s + 1]
            rss = res("ss%d" % s)
            sumsq(xap, P, ssap, rsrc, rss)
            ck("n1")
            wl = []
            for c in range(NPC):
                xb = XSCP[c % 2]
                rxs = res("xscp%d" % (c % 2))
                pg.op("dve", lambda e, c=c, xb=xb, xap=xap, ssap=ssap, P=P: e.tensor_scalar(
                    out=xb[0:P, :], in0=xap[:, c * 512:(c + 1) * 512], scalar1=ssap, scalar2=None, op0=ALU.mult),
                      reads=[rsrc, rss], writes=[rxs])
                ps, rps = aux()

                def tr(e, ps=ps, xb=xb, P=P):
                    last = None
                    for j in range(4):
                        last = e.transpose(out=ps[:, j * 128:j * 128 + P], in_=xb[0:P, j * 128:(j + 1) * 128],
                                           identity=ident_f[0:P, 0:P])
                    return last
                ck("n2")
                pg.op("pe", tr, reads=[rxs, r_setup], writes=[rps])
                ck("n3")
                eng = evac_eng()
                for j in range(4):
                    kt = c * 4 + j
                    out = dstT[:, kt, s * 128:s * 128 + P]
                    in_ = ps[:, j * 128:j * 128 + P]
                    if eng == "act":
                        o = pg.op("act", lambda e, out=out, in_=in_, kt=kt: e.activation(out=out, in_=in_, func=AF.Identity,
                                                                                          scale=gcol[:, kt:kt + 1]),
                                  reads=[rps, r_setup])
                    else:
                        o = pg.op("dve", lambda e, out=out, in_=in_, kt=kt: e.tensor_scalar(out=out, in0=in_,
                                                                                            scalar1=gcol[:, kt:kt + 1],
                                                                                            scalar2=None, op0=ALU.mult),
                                  reads=[rps, r_setup])
                    wl = [x for x in wl if x.eng != o.eng] + [o]
                    ck("n4%d" % j)
            dst_res[s].writers = wl
            ck("n5")

    def gemm(actT, act_res, nk, wsrc, row0, col0, subt, evac, do_pump=True, wname=None, npump=1):
        nsl = (nk + KS - 1) // KS
        for sl in range(nsl):
            kk = min(KS, nk - sl * KS)
            slot = wr_i[0] % cfg.WSLOTS
            wr_i[0] += 1
            wt = WR[slot]
            src = wsrc[row0 + sl * KS * 128: row0 + (sl * KS + kk) * 128, col0:col0 + 512].rearrange(
                "(kt p) n -> p kt n", p=128)
            key = (wname, row0 + sl * KS * 128, col0)
            if key in wcache:
                idx = wcache[key]
                pg.op("sp", lambda e, wt=wt, idx=idx, kk=kk: e.dma_start(
                    out=wt[:, 0:kk, :], in_=wsc[idx, :, 0:kk * 512].rearrange("p (k n) -> p k n", n=512)),
                    reads=[r_wsc[idx]], writes=[r_wr[slot]], dma=True)
            else:
                idx = len(wcache)
                wcache[key] = idx
                r_wsc[idx] = Res()
                pg.op("pool", lambda e, wt=wt, src=src, kk=kk: e.dma_start(out=wt[:, 0:kk, :], in_=src),
                      writes=[r_wr[slot]], dma=True)
                pg.op("sp", lambda e, wt=wt, idx=idx, kk=kk: e.dma_start(
                    out=wsc[idx, :, 0:kk * 512].rearrange("p (k n) -> p k n", n=512), in_=wt[:, 0:kk, :]),
                    reads=[r_wr[slot]], writes=[r_wsc[idx]], dma=True)
            for (s, P) in subt:
                def mm(e, s=s, P=P, sl=sl, kk=kk, wt=wt):
                    last = None
                    for j in range(kk):
                        kt = sl * KS + j
                        last = e.matmul(GPS[s][0:P, :], lhsT=actT[:, kt, s * 128:s * 128 + P], rhs=wt[:, j, :],
                                        start=(kt == 0), stop=(kt == nk - 1))
                    return last
                if sl == 0:
                    pg.op("pe", mm, reads=[act_res[s], r_wr[slot]], writes=[r_gps[s]])
                else:
                    pg.op("pe", mm, reads=[act_res[s], r_wr[slot]], upd=[r_gps[s]])
                if do_pump:
                    for _ in range(npump):
                        pump()
                if sl == nsl - 1:
                    evac(s, P, GPS[s][0:P, :], r_gps[s])

    r_ht = [Res() for _ in range(S)]
    r_yt = [Res() for _ in range(S)]
    r_xin = [Res(), Res()]

    def mixer_A(jp, subt, sample):
        vb = 1 + jp % 2
        U, V = RAW[0], RAW[vb]
        for (s, P) in subt:
            rU, rV = res("raw0_%d" % s), res("raw%d_%d" % (vb, s))
            rss = res("ssv")
            for hh in range(2):
                pg.op("act", lambda e, s=s, P=P, hh=hh: e.activation(out=JUNK[0:P, 0:256], in_=V[0:P, s, hh * 256:(hh + 1) * 256],
                                                                      func=AF.Square, accum_out=SM[0:P, 8 + hh:9 + hh]),
                      reads=[rV], writes=[res("junk")], upd=[rss])
            rss.writers = [pg.q["act"][-1]]
            rstd_from_ss(SM[0:P, 8:10], P, 2, 1.0 / 256, rss)
            VN = BFt[5]
            rvn = res("vn")
            first = True
            for hh in range(2):
                c0 = hh * 256
                if sample:
                    pg.op("dve", lambda e, s=s, P=P, hh=hh, c0=c0: e.scalar_tensor_tensor(
                        out=Et[0][0:P, c0:c0 + 256], in0=V[0:P, s, c0:c0 + 256], scalar=SM[0:P, 8 + hh:9 + hh],
                        in1=NVO[0:P, jp * 512 + c0: jp * 512 + c0 + 256], op0=ALU.mult, op1=ALU.mult),
                        reads=[rV, rss, res("NVO")], writes=[res("E0")] if first else [], upd=[] if first else [res("E0")])
                    pg.op("dve", lambda e, P=P, c0=c0: e.tensor_copy(out=VN[0:P, c0:c0 + 256], in_=Et[0][0:P, c0:c0 + 256]),
                          reads=[res("E0")], writes=[rvn] if first else [], upd=[] if first else [rvn])
                else:
                    pg.op("dve", lambda e, s=s, P=P, hh=hh, c0=c0: e.scalar_tensor_tensor(
                        out=VN[0:P, c0:c0 + 256], in0=V[0:P, s, c0:c0 + 256], scalar=SM[0:P, 8 + hh:9 + hh],
                        in1=NVO[0:P, jp * 512 + c0: jp * 512 + c0 + 256], op0=ALU.mult, op1=ALU.mult),
                        reads=[rV, rss, res("NVO")], writes=[rvn] if first else [], upd=[] if first else [rvn])
                first = False
            if sample:
                dma("sp", vso[0:P, jp * 512:(jp + 1) * 512], Et[0][0:P, :], reads=[res("E0")])
                out_dmas.append(pg.q["sp"][-1])
            yield
            ps, rps = aux()

            def mix(e, P=P, ps=ps):
                last = None
                for hh in range(2):
                    hg = 2 * jp + hh
                    lhsT = WTS[0:P, hg, :] if sample else WT[:, hg, :]
                    last = e.matmul(ps[0:P, hh * 256:(hh + 1) * 256], lhsT=lhsT, rhs=VN[0:P, hh * 256:(hh + 1) * 256],
                                    start=True, stop=True)
                return last
            pg.op("pe", mix, reads=[rvn, res("WT")], writes=[rps])
            YA = BFt[4]
            rya = res("ya")
            for hh in range(2):
                hg = 2 * jp + hh
                bcol = BSS[0:P, hg:hg + 1] if sample else BSC[:, hg:hg + 1]
                pg.op("dve", lambda e, s=s, P=P, hh=hh, ps=ps, bcol=bcol: e.scalar_tensor_tensor(
                    out=YA[0:P, hh * 256:(hh + 1) * 256], in0=ps[0:P, hh * 256:(hh + 1) * 256], scalar=bcol,
                    in1=U[0:P, s, hh * 256:(hh + 1) * 256], op0=ALU.add, op1=ALU.mult),
                    reads=[rps, rU, r_setup], writes=[rya] if hh == 0 else [], upd=[] if hh == 0 else [rya])
            yield
            y_to_T(YA, rya, s, P, 4 * jp)
            yield

    def y_to_T(Y, ry, s, P, kt0):
        ps, rps = aux()
        psb = ps[:, :].bitcast(BF16)

        def tr(e, P=P, psb=psb):
            last = None
            for j in range(4):
                last = e.transpose(out=psb[:, j * 128:j * 128 + P], in_=Y[0:P, j * 128:(j + 1) * 128],
                                   identity=ident_b[0:P, 0:P])
            return last
        pg.op("pe", tr, reads=[ry, r_setup], writes=[rps])
        eng = evac_eng()
        out = YT[:, kt0:kt0 + 4, s * 128:s * 128 + P]
        in_ = psb[:, 0:512].rearrange("p (j t) -> p j t", t=128)[:, :, 0:P]
        o = copy_op(eng, out, in_, [rps], [], upd=[])
        r_yt[s].writers = [x for x in r_yt[s].writers if x.eng != o.eng] + [o]

    IBt_a = IBt

    def mixer_B(Q, subt, sample, pre):
        SG, QS, GS = RAW[0], RAW[1], RAW[2]
        IBt = (IBt_a, IBt2)[Q % 2]
        tag = "s" if sample else "p"
        TRI, D1M, D2M, CHK, TRI4 = C["tri_" + tag], C["d1m_" + tag], C["d2m_" + tag], C["chk_" + tag], C["tri4_" + tag]
        for (s, P) in subt:
            L = LS if sample else 64
            NC = P // L
            rSG, rQS, rGS, rIB = res("raw0_%d" % s), res("raw1_%d" % s), res("raw2_%d" % s), res("ib%d_%d" % (Q % 2, s))
            rF, rK = res("F"), res("KK")
            cq = slice(Q * 512, (Q + 1) * 512)
            pg.op("dve", lambda e, s=s, P=P: e.tensor_tensor(out=Ft[0:P, :], in0=SG[0:P, s, :], in1=OML[0:P, cq], op=ALU.mult),
                  reads=[rSG, res("OML")], writes=[rF])
            pg.op("dve", lambda e, P=P: e.tensor_tensor(out=Ft[0:P, :], in0=Ft[0:P, :], in1=LB[0:P, cq], op=ALU.add),
                  reads=[rF, res("LB")], writes=[rF])
            pg.op("dve", lambda e, P=P: e.tensor_scalar(out=KKt[0:P, :], in0=Ft[0:P, :], scalar1=-1.0, scalar2=1.0,
                                                        op0=ALU.mult, op1=ALU.add), reads=[rF], writes=[rK])
            pg.op("act", lambda e, P=P: e.activation(out=Ft[0:P, :], in_=Ft[0:P, :], func=AF.Ln), reads=[rF, rK], writes=[rF])
            if not pre:
                rGN = res("GN")
                pg.op("dve", lambda e, s=s, P=P: e.tensor_tensor(out=GNt[0:P, :], in0=GS[0:P, s, :], in1=NVO[0:P, cq], op=ALU.mult),
                      reads=[rGS, res("NVO")], writes=[rGN])
            yield
            def cum(M, P=P):
                ps, rps = aux()
                pg.op("pe", lambda e, ps=ps: e.matmul(ps[0:P, :], lhsT=M[0:P, 0:P], rhs=Ft[0:P, :], start=True, stop=True),
                      reads=[rF, r_setup], writes=[rps])
                return ps, rps
            psD2, rD2 = cum(D2M)
            psDC, rDC = aux()

            def dec(e, P=P, psDC=psDC, NC=NC):
                last = None
                for h in range(4):
                    last = e.matmul(psDC[:, h * NC:(h + 1) * NC], lhsT=Ft[0:P, h * 128:(h + 1) * 128], rhs=CHK[0:P, 0:NC],
                                    start=True, stop=True)
                return last
            pg.op("pe", dec, reads=[rF, r_setup], writes=[rDC])
            rDEC = res("DEC")
            pg.op("act", lambda e, psDC=psDC, NC=NC: e.activation(out=DECt[:, 0:4 * NC], in_=psDC[:, 0:4 * NC], func=AF.Exp),
                  reads=[rDC], writes=[rDEC])
            KE, rKE = BFt[3], res("KE")
            rE1 = res("E1")
            pg.op("act", lambda e, P=P, psD2=psD2: e.activation(out=Et[1][0:P, :], in_=psD2[0:P, :], func=AF.Exp),
                  reads=[rD2], writes=[rE1])
            pg.op("dve", lambda e, P=P: e.tensor_tensor(out=KE[0:P, :], in0=KKt[0:P, :], in1=Et[1][0:P, :], op=ALU.mult),
                  reads=[rK, rE1], writes=[rKE])
            if not pre:
                psB, rB = cum(TRI)
                psD1, rD1 = cum(D1M)
                QR, KR, QB = BFt[0], BFt[1], BFt[2]
                rQR, rKR, rQB = res("QR"), res("KR"), res("QB")
                rE0 = res("E0")
                pg.op("act", lambda e, P=P, psD1=psD1: e.activation(out=Et[0][0:P, :], in_=psD1[0:P, :], func=AF.Exp),
                      reads=[rD1], writes=[rE0])
                pg.op("dve", lambda e, s=s, P=P: e.tensor_tensor(out=QR[0:P, :], in0=QS[0:P, s, :], in1=Et[0][0:P, :], op=ALU.mult),
                      reads=[rQS, rE0], writes=[rQR])
                pg.op("act", lambda e, P=P, psD1=psD1: e.activation(out=Et[1][0:P, :], in_=psD1[0:P, :], func=AF.Exp, scale=-1.0),
                      reads=[rD1, rKE], writes=[rE1])
                pg.op("dve", lambda e, P=P: e.tensor_tensor(out=KR[0:P, :], in0=KKt[0:P, :], in1=Et[1][0:P, :], op=ALU.mult),
                      reads=[rK, rE1], writes=[rKR])
                pg.op("act", lambda e, P=P, psB=psB: e.activation(out=Et[0][0:P, :], in_=psB[0:P, :], func=AF.Exp),
                      reads=[rB, rQR], writes=[rE0])
                pg.op("dve", lambda e, s=s, P=P: e.tensor_tensor(out=QB[0:P, :], in0=QS[0:P, s, :], in1=Et[0][0:P, :], op=ALU.mult),
                      reads=[rQS, rE0], writes=[rQB])
            yield
            rS = res("S32")
            rSB = res("SB")
            hq = slice(4 * Q, 4 * Q + 4)
            if sample:
                dd = []
                for st in range(NSTR):
                    dd.append(dma("sp", S32[:, st * 4:(st + 1) * 4, :], s0[st, 4 * Q:4 * Q + 4, :, :].rearrange("h d v -> d h v"),
                                  writes=[rS] if st == 0 else [], reads=[] if st == 0 else []))
                rS.writers = dd
                if not pre:
                    pg.op("act", lambda e: e.activation(out=SBt[:, :, :, :].rearrange("d h c v -> d c h v"),
                                                        in_=S32[:, 0:16, :].rearrange("d (c h) v -> d c h v", h=4), func=AF.Copy),
                          reads=[rS], writes=[rSB])
                rKEM = res("KEM")
                for st in range(NSTR):
                    pg.op("dve", lambda e, st=st: e.tensor_scalar(out=KEM[:, st, :], in0=KE[0:64, :], scalar1=CHK[0:64, st:st + 1],
                                                                  scalar2=None, op0=ALU.mult),
                          reads=[rKE, r_setup], writes=[rKEM] if st == 0 else [], upd=[] if st == 0 else [rKEM])
            else:
                if not pre:
                    pg.op("act", lambda e: e.activation(out=SBt[:, :, 0, :], in_=S32[:, hq, :], func=AF.Copy),
                          reads=[rS], writes=[rSB])
            psDS = []
            for c in range(NC):
                ps, rps = aux()
                psDS.append((ps, rps))

                def dsm(e, c=c, ps=ps, L=L, s=s):
                    last = None
                    for h in range(4):
                        if sample:
                            lhsT = KEM[:, c, h * 128:(h + 1) * 128]
                            rhs = IBt[0:64, s, h * 128:(h + 1) * 128]
                        else:
                            lhsT = KE[c * L:(c + 1) * L, h * 128:(h + 1) * 128]
                            rhs = IBt[c * L:(c + 1) * L, s, h * 128:(h + 1) * 128]
                        last = e.matmul(ps[:, h * 128:(h + 1) * 128], lhsT=lhsT, rhs=rhs, start=True, stop=True)
                    return last
                pg.op("pe", dsm, reads=[res("KEM") if sample else rKE, rIB], writes=[rps])
            yield
            for c in range(NC):
                ps, rps = psDS[c]
                for h in range(4):
                    sidx = (c * 4 + h) if sample else (4 * Q + h)
                    pg.op("dve", lambda e, c=c, h=h, ps=ps, sidx=sidx, NC=NC: e.scalar_tensor_tensor(
                        out=S32[:, sidx, :], in0=S32[:, sidx, :], scalar=DECt[:, h * NC + c:h * NC + c + 1],
                        in1=ps[:, h * 128:(h + 1) * 128], op0=ALU.mult, op1=ALU.add),
                        reads=[rps, rDEC, rS] + ([rSB] if (not pre and (sample or c == 0)) else []), writes=[rS])
                if (not sample) and (not pre) and c + 1 < NC:
                    pg.op("act", lambda e, c=c: e.activation(out=SBt[:, :, c + 1, :], in_=S32[:, hq, :], func=AF.Copy),
                          reads=[rS], upd=[rSB])
                    rSB.writers = [pg.q["act"][-1]]
            if sample:
                for st in range(NSTR):
                    dma("sp", sso[st, 4 * Q:4 * Q + 4, :, :].rearrange("h d v -> d h v"), S32[:, st * 4:(st + 1) * 4, :], reads=[rS])
                    out_dmas.append(pg.q["sp"][-1])
                    rS.readers.append(pg.q["sp"][-1])
            if pre:
                continue
            psT1, rT1 = aux()
            psT2, rT2 = aux()
            pT1 = psT1[:, :].bitcast(BF16)
            pT2 = psT2[:, :].bitcast(BF16)

            def trq(e, P=P, pT1=pT1, pT2=pT2):
                last = None
                for h in range(4):
                    e.transpose(out=pT1[:, h * 128:h * 128 + P], in_=QR[0:P, h * 128:(h + 1) * 128], identity=ident_b[0:P, 0:P])
                    e.transpose(out=pT1[:, 512 + h * 128:512 + h * 128 + P], in_=KR[0:P, h * 128:(h + 1) * 128],
                                identity=ident_b[0:P, 0:P])
                    last = e.transpose(out=pT2[:, h * 128:h * 128 + P], in_=QB[0:P, h * 128:(h + 1) * 128],
                                       identity=ident_b[0:P, 0:P])
                return last
            pg.op("pe", trq, reads=[rQR, rKR, rQB, r_setup], writes=[rT1, rT2])
            QRT, KRT, QBT = TTt[0], TTt[1], TTt[2]
            rQRT, rKRT, rQBT = res("QRT"), res("KRT"), res("QBT")
            copy_op("act", QRT[:, :, 0:P], pT1[:, 0:512].rearrange("p (h t) -> p h t", t=128)[:, :, 0:P], [rT1], [rQRT])
            copy_op("dve", KRT[:, :, 0:P], pT1[:, 512:1024].rearrange("p (h t) -> p h t", t=128)[:, :, 0:P], [rT1], [rKRT])
            copy_op("act", QBT[:, :, 0:P], pT2[:, 0:512].rearrange("p (h t) -> p h t", t=128)[:, :, 0:P], [rT2], [rQBT])
            if sample:
                rQBM = res("QBM")
                for st in range(NSTR):
                    pg.op("dve", lambda e, st=st: e.tensor_tensor(out=QBM[:, :, st, :], in0=QBT[:, :, 0:64],
                                                                  in1=MS[:, st:st + 1, :].broadcast_to([128, 4, 64]) if False else
                                                                  MS[:, st, :].rearrange("p (o t) -> p o t", o=1).broadcast_to([128, 4, 64]),
                                                                  op=ALU.mult),
                          reads=[rQBT, r_setup], writes=[rQBM] if st == 0 else [], upd=[] if st == 0 else [rQBM])
            yield
            psS, rpS = aux()

            def sc(e, P=P, psS=psS):
                last = None
                for h in range(4):
                    last = e.matmul(psS[0:P, h * P:(h + 1) * P], lhsT=KRT[:, h, 0:P], rhs=QRT[:, h, 0:P], start=True, stop=True)
                return last
            pg.op("pe", sc, reads=[rQRT, rKRT], writes=[rpS])
            SCM, rSCM = BFt[4], res("SCM")
            pg.op("dve", lambda e, P=P, psS=psS: e.tensor_tensor(out=SCM[0:P, 0:4 * P], in0=psS[0:P, 0:4 * P], in1=TRI4[0:P, 0:4 * P],
                                                                 op=ALU.mult), reads=[rpS, r_setup], writes=[rSCM])
            yield
            psO, rpO = aux()

            def om(e, s=s, P=P, psO=psO, NC=NC, L=L):
                last = None
                for h in range(4):
                    e.matmul(psO[0:P, h * 128:(h + 1) * 128], lhsT=SCM[0:P, h * P:(h + 1) * P], rhs=IBt[0:P, s, h * 128:(h + 1) * 128],
                             start=True, stop=False)
                    for c in range(NC):
                        if sample:
                            last = e.matmul(psO[0:P, h * 128:(h + 1) * 128], lhsT=QBM[:, h, c, :], rhs=SBt[:, h, c, :],
                                            start=False, stop=(c == NC - 1))
                        else:
                            last = e.matmul(psO[c * L:(c + 1) * L, h * 128:(h + 1) * 128], lhsT=QBT[:, h, c * L:(c + 1) * L],
                                            rhs=SBt[:, h, c, :], start=False, stop=True)
                return last
            pg.op("pe", om, reads=[rSCM, rIB, rSB, rQBT] + ([res("QBM")] if sample else []), writes=[rpO])
            rso = res("sso")
            for h in range(4):
                pg.op("act", lambda e, P=P, h=h, psO=psO: e.activation(out=JUNK[0:P, 0:128], in_=psO[0:P, h * 128:(h + 1) * 128],
                                                                        func=AF.Square, accum_out=SM[0:P, 16 + h:17 + h]),
                      reads=[rpO], writes=[res("junk")], upd=[rso])
            rso.writers = [pg.q["act"][-1]]
            rstd_from_ss(SM[0:P, 16:20], P, 4, 1.0 / 128, rso)
            YB, rYB = BFt[5], res("YB")
            for h in range(4):
                pg.op("dve", lambda e, P=P, h=h, psO=psO: e.scalar_tensor_tensor(
                    out=YB[0:P, h * 128:(h + 1) * 128], in0=psO[0:P, h * 128:(h + 1) * 128], scalar=SM[0:P, 16 + h:17 + h],
                    in1=GNt[0:P, h * 128:(h + 1) * 128], op0=ALU.mult, op1=ALU.mult),
                    reads=[rpO, rso, res("GN")], writes=[rYB] if h == 0 else [], upd=[] if h == 0 else [rYB])
            yield
            y_to_T(YB, rYB, s, P, KT // 2 + 4 * Q)
            yield

    out_dmas = []
    r_hidt = [Res(), Res()]
    last_barrier_dma = {"sp": 0, "pool": 0, "act": 0}

    def barrier():
        lasts = []
        for e in ENGS:
            for o in reversed(pg.q[e]):
                if not o.dma and o.fn is not None and getattr(o, "real", True):
                    lasts.append(o)
                    break
        dmas = []
        for e in ("sp",):
            dmas.extend(pg.dma_ops[e][last_barrier_dma[e]:])
            last_barrier_dma[e] = len(pg.dma_ops[e])
        rb = Res()
        rb.writers = lasts + dmas
        for e in ("pe", "act", "dve", "sp"):
            pg.op(e, None, reads=[rb])

    def load_bc(dst, src1d, n, name):
        return dma("sp", dst[:, 0:n], src1d.partition_broadcast(128), writes=[res(name)])

    def act_evac(func, dst_fn, rname):
        def ev(s, P, ps, rps):
            out = dst_fn(s, P)
            pg.op("act", lambda e: e.activation(out=out, in_=ps, func=func), reads=[rps], writes=[res(rname % s)])
        return ev

    def do_tile(kind, xsrc, tok0, subt, ydst):
        sample = kind == "sample"
        pre = kind == "pre"
        for (s, P) in subt:
            b = s % 2
            dma("sp", XIN[b][0:P, :], xsrc[tok0 + s * 128: tok0 + s * 128 + P, :], writes=[r_xin[b]])
            norm_to_T([(s, XIN[b][0:P, :], P, r_xin[b])], GC1, HT, r_ht)
        barrier()
        ck("s0")
        if not pre:
            load_bc(NVO, norm_v, WA, "NVO")
            for jp in range(cfg.NA):
                def ev_flush(inner):
                    def ev(s, P, ps, rps):
                        flush()
                        inner(s, P, ps, rps)
                    return ev
                vb = 1 + jp % 2
                gemm(HT, r_ht, KT, w_in, 0, WA + jp * 512, subt,
                     act_evac(AF.Gelu_apprx_tanh, lambda s, P, vb=vb: RAW[vb][0:P, s, :], "raw%d_%%d" % vb), wname="w_in")
                gemm(HT, r_ht, KT, w_in, 0, jp * 512, subt,
                     ev_flush(act_evac(AF.Gelu_apprx_tanh, lambda s, P: RAW[0][0:P, s, :], "raw0_%d")), wname="w_in")
                drive(mixer_A(jp, subt, sample))
            deferred.append(lambda: load_bc(NVO, norm_o, WB, "NVO"))
        for Q in range(cfg.NQ):
            def ev_flush(inner):
                def ev(s, P, ps, rps):
                    flush()
                    inner(s, P, ps, rps)
                return ev
            c_q, c_f, c_i, c_g = 2 * WA + Q * 512, 2 * WA + WB + Q * 512, 2 * WA + 2 * WB + Q * 512, 2 * WA + 3 * WB + Q * 512
            ibb = (IBt, IBt2)[Q % 2]
            gemm(HT, r_ht, KT, w_in, 0, c_i, subt, act_evac(AF.Copy, lambda s, P, ibb=ibb: ibb[0:P, s, :], "ib%d_%%d" % (Q % 2)), wname="w_in")
            gemm(HT, r_ht, KT, w_in, 0, c_f, subt, ev_flush(act_evac(AF.Sigmoid, lambda s, P: RAW[0][0:P, s, :], "raw0_%d")), wname="w_in")
            if not pre:
                gemm(HT, r_ht, KT, w_in, 0, c_q, subt, ev_flush(act_evac(AF.Silu, lambda s, P: RAW[1][0:P, s, :], "raw1_%d")), wname="w_in")
                gemm(HT, r_ht, KT, w_in, 0, c_g, subt, ev_flush(act_evac(AF.Silu, lambda s, P: RAW[2][0:P, s, :], "raw2_%d")), wname="w_in")
            drive(mixer_B(Q, subt, sample, pre))
        flush()
        barrier()
        ck("s1" + kind)
        if pre:
            return
        r_x = [res("x1_%d" % s) for s in range(S)]
        for (s, P) in subt:
            dma("sp", X1[0:P, s, :], xsrc[tok0 + s * 128: tok0 + s * 128 + P, :], writes=[r_x[s]])

        def add_evac(cb):
            def ev(s, P, ps, rps):
                pg.op("dve", lambda e: e.tensor_tensor(out=X1[0:P, s, cb * 512:(cb + 1) * 512], in0=ps,
                                                       in1=X1[0:P, s, cb * 512:(cb + 1) * 512], op=ALU.add),
                      reads=[rps, r_x[s]], writes=[r_x[s]])
            return ev
        for cb in range(D // 512):
            gemm(YT, r_yt, KT, w_out, 0, cb * 512, subt, add_evac(cb), wname="w_out")
        norm_to_T([(s, X1[0:P, s, :], P, r_x[s]) for (s, P) in subt], GC2, HT, r_ht)
        ck("s2")
        NCH = DFF // cfg.FC
        KF = cfg.FC // 128
        for j in range(NCH):
            hb = HIDT[j % 2]
            rh = r_hidt[j % 2]
            wl = {}
            for cb in range(cfg.FC // 512):
                def ev(s, P, ps, rps, cb=cb, hb=hb, rh=rh):
                    k = (cb * S + s) % 2
                    rrt, rhtm = res("RT"), res("HTM%d" % k)
                    pg.op("act", lambda e: e.activation(out=RT[0:P, :], in_=ps, func=AF.Relu), reads=[rps], writes=[rrt])
                    pg.op("dve", lambda e: e.tensor_tensor(out=HTM[k][0:P, :], in0=RT[0:P, :], in1=RT[0:P, :], op=ALU.mult),
                          reads=[rrt], writes=[rhtm])

                    def tr_step():
                        ps2, rps2 = aux()
                        psb = ps2[:, :].bitcast(BF16)

                        def tr(e):
                            last = None
                            for jj in range(4):
                                last = e.transpose(out=psb[:, jj * 128:jj * 128 + P], in_=HTM[k][0:P, jj * 128:(jj + 1) * 128],
                                                   identity=ident_b[0:P, 0:P])
                            return last
                        pg.op("pe", tr, reads=[rhtm, r_setup], writes=[rps2])
                        o = copy_op(evac_eng(), hb[:, cb * 4:cb * 4 + 4, s * 128:s * 128 + P],
                                    psb[:, 0:512].rearrange("p (j t) -> p j t", t=128)[:, :, 0:P], [rps2], [])
                        rh.writers = [x for x in rh.writers if x.eng != o.eng] + [o]
                    deferred.append(tr_step)
                if cb == 0:
                    pg.op("act", None, writes=[rh])
                    pg.op("dve", None, writes=[rh])
                    rh.writers = []
                gemm(HT, r_ht, KT, w_up, 0, j * cfg.FC + cb * 512, subt, ev, wname="w_up")
            flush()
            r_h4 = [rh] * S
            for cb in range(D // 512):
                gemm(hb, r_h4, KF, w_down, j * cfg.FC, cb * 512, subt, add_evac(cb), wname="w_down")
        barrier()
        ck("s3")
        load_bc(NFBt, norm_f, D, "NFB")
        for (s, P) in subt:
            ssap = SM[0:P, s:s + 1]
            rss = res("ss%d" % s)
            sumsq(X1[0:P, s, :], P, ssap, r_x[s], rss)
            pg.op("dve", lambda e, s=s, P=P, ssap=ssap: e.scalar_tensor_tensor(out=X1[0:P, s, :], in0=X1[0:P, s, :], scalar=ssap,
                                                                                in1=NFBt[0:P, :], op0=ALU.mult, op1=ALU.mult),
                  reads=[rss, res("NFB"), r_x[s]], writes=[r_x[s]])
            dma("sp", ydst[tok0 + s * 128: tok0 + s * 128 + P, :], X1[0:P, s, :], reads=[r_x[s]])
            out_dmas.append(pg.q["sp"][-1])
        barrier()

    full = [(s, 128) for s in range(S)]
    barrier()
    try:
        ck("setup")
        for t in range(NP // T):
            do_tile("pre", xprev, t * T, full, None)
            ck("pre%d" % t)
        for t in range(NP // T):
            do_tile("prompt", xp, t * T, full, yp)
            ck("prompt%d" % t)
        dma("sp", spo.rearrange("h d v -> d h v"), S32[:, 0:HB, :], reads=[res("S32")])
        out_dmas.append(pg.q["sp"][-1])
        res("S32").readers.append(pg.q["sp"][-1])
        barrier()
        do_tile("sample", xs, 0, [(0, 64)], ys)
    except _Stop:
        del deferred[:]
        barrier()
    rfin = Res()
    rfin.writers = list(out_dmas)
    pg.op("sp", None, reads=[rfin])
    return nc, pg


_CACHE = {}


def get_program(cfg_key, cfg):
    if cfg_key not in _CACHE:
        nc, pg = build(cfg)
        es = ExitStack()
        pg.emit(nc, es)
        es.close()
        _CACHE[cfg_key] = nc
    return _CACHE[cfg_key]


def run(cfg, x_prompt, x_sample, state_hgrn, norm1, w_in, w_s, b_s, norm_v, lb_logits, norm_o,
        w_out, norm2, w_up, w_down, norm_f, trace=False):
    f = lambda a: np.ascontiguousarray(np.asarray(a, dtype=np.float32))
    x_prompt, x_sample, state_hgrn = f(x_prompt), f(x_sample), f(state_hgrn)
    D, NP, NSTR, LS, HB, WA = cfg.D, cfg.NP, cfg.NSTR, cfg.LS, cfg.HB, cfg.WA
    n = cfg.n_cores
    B = x_prompt.shape[0]
    assert n == 2 * B and x_prompt.shape[1] == 2 * NP
    shared = {
        "w_in": f(w_in[0]), "w_out": f(w_out[0]), "w_up": f(w_up[0]), "w_down": f(w_down[0]),
        "norm1": f(norm1[0]), "norm2": f(norm2[0]), "norm_f": f(norm_f), "norm_v": f(norm_v[0]).reshape(-1),
        "norm_o": f(norm_o[0]).reshape(-1), "lb_logits": f(lb_logits), "w_s": f(w_s[0]), "b_s": f(b_s[0]),
    }
    for k, v in make_consts().items():
        shared["c_" + k] = v
    zeros = np.zeros((NP, D), np.float32)
    in_maps = []
    for c in range(n):
        b, half = c // 2, c % 2
        m = dict(shared)
        m["xp"] = x_prompt[b, half * NP:(half + 1) * NP]
        m["xprev"] = x_prompt[b, 0:NP] if half == 1 else zeros
        m["xs"] = x_sample[c * NSTR:(c + 1) * NSTR].reshape(NSTR * LS, D)
        m["s0"] = state_hgrn[0, c * NSTR:(c + 1) * NSTR]
        in_maps.append(m)
    nc = get_program((D, NP, n), cfg)
    r = run_bass_kernel_spmd(nc, in_maps, core_ids=list(range(n)), **({"trace": True} if trace else {}))
    outs = r.results
    y_prompt = np.zeros((B, 2 * NP, D), np.float32)
    y_sample = np.zeros((n * NSTR, LS, D), np.float32)
    st_p = np.zeros((1, B, HB, 128, 128), np.float32)
    st_s = np.zeros((1, n * NSTR, HB, 128, 128), np.float32)
    v_s = np.zeros((1, n * NSTR, LS, WA), np.float32)
    for c in range(n):
        b, half = c // 2, c % 2
        o = outs[c]
        y_prompt[b, half * NP:(half + 1) * NP] = o["yp"]
        y_sample[c * NSTR:(c + 1) * NSTR] = o["ys"].reshape(NSTR, LS, D)
        if half == 1:
            st_p[0, b] = o["spo"]
        st_s[0, c * NSTR:(c + 1) * NSTR] = o["sso"]
        v_s[0, c * NSTR:(c + 1) * NSTR] = o["vso"].reshape(NSTR, LS, WA)
    if trace:
        return (y_prompt, y_sample, st_p, st_s, v_s), r
    return (y_prompt, y_sample, st_p, st_s, v_s)


def kernel(**inputs):
    cfg = Cfg()
    return run(cfg, **inputs)
```

```python
import numpy as np
import ml_dtypes
from contextlib import ExitStack
import concourse.bass as bass
import concourse.mybir as mybir
from concourse.bass_utils import run_bass_kernel_spmd

F32 = mybir.dt.float32
BF16 = mybir.dt.bfloat16
AF = mybir.ActivationFunctionType
ALU = mybir.AluOpType
EPS = 1e-6


class Cfg:
    def __init__(self, D=4096, HA=8, HB=16, NP=2048, NSTR=4, LS=16, S=4, n_cores=8, WSLOTS=4):
        self.D = D
        self.HA = HA
        self.HB = HB
        self.WA = D // 2
        self.WB = D // 2
        self.HDA = self.WA // HA
        assert self.HDA == 256 and self.WB // HB == 128
        self.DFF = 4 * D
        self.KT = D // 128
        self.NIN = 2 * self.WA + 4 * self.WB
        self.NP = NP
        self.NSTR = NSTR
        self.LS = LS
        self.NS = NSTR * LS
        assert self.NS == 64
        self.S = S
        self.T = S * 128
        assert NP % self.T == 0
        self.n_cores = n_cores
        self.WSLOTS = WSLOTS
        self.KS = min(8, self.KT)
        self.FC = min(2048, self.DFF)
        self.NQ = self.WB // 512
        self.NA = self.WA // 512
        assert self.WA % 512 == 0 and self.WB % 512 == 0


class Op:
    __slots__ = ("eng", "fn", "deps", "waited", "semval", "dma", "dma_idx")

    def __init__(self, eng, fn, dma):
        self.eng = eng
        self.fn = fn
        self.dma = dma
        self.waited = False
        self.semval = 0
        self.deps = []
        self.dma_idx = -1


class Res:
    __slots__ = ("writers", "readers", "excl")

    def __init__(self, excl=False):
        self.writers = []
        self.readers = []
        self.excl = excl


NDMASEM = 6
STOP_AT = None


class _Stop(Exception):
    pass


def ck(name):
    if STOP_AT == name:
        raise _Stop()
ENGS = ("pe", "act", "dve", "pool", "sp")


class Prog:
    def __init__(self):
        self.q = {e: [] for e in ENGS}
        self.dma_ops = {"sp": [], "pool": [], "act": []}

    def op(self, eng, fn, reads=(), writes=(), upd=(), dma=False):
        o = Op(eng, fn, dma)
        deps = []
        for r in reads:
            deps.extend(r.writers)
            if r.excl:
                deps.extend(x for x in r.readers if x.eng != eng)
        for w in writes:
            deps.extend(w.writers)
            deps.extend(w.readers)
        if dma:
            lst = self.dma_ops[eng]
            if len(lst) >= NDMASEM:
                deps.append(lst[-NDMASEM])
            o.dma_idx = len(lst)
            lst.append(o)
        seen = set()
        i = 0
        while i < len(deps):
            d = deps[i]
            i += 1
            if id(d) in seen:
                continue
            seen.add(id(d))
            if d.fn is None:
                deps.extend(d.deps)
                continue
            if d.eng == "pe" and eng == "pe" and not d.dma:
                continue
            d.waited = True
            o.deps.append(d)
        for w in writes:
            w.writers = [o]
            w.readers = []
        for w in upd:
            w.writers = [o]
        for r in reads:
            r.readers = [x for x in r.readers if x.dma or x.eng != eng] + [o]
        self.q[eng].append(o)
        return o

    def emit(self, nc, es):
        sems = {e: es.enter_context(nc.semaphore("s_" + e)) for e in ENGS}
        dsems = {e: [es.enter_context(nc.semaphore("d_%s%d" % (e, i))) for i in range(NDMASEM)]
                 for e in self.dma_ops}
        for e in ENGS:
            c = 0
            for o in self.q[e]:
                if o.waited and not o.dma:
                    c += 1
                    o.semval = c
        block = es.enter_context(nc.Block())

        def run(e, eng):
            waited = {}
            for o in self.q[e]:
                for d in o.deps:
                    if d.dma:
                        sem = dsems[d.eng][d.dma_idx % NDMASEM]
                        val = 16 * (d.dma_idx // NDMASEM + 1)
                    else:
                        sem = sems[d.eng]
                        val = d.semval
                    key = id(sem)
                    if waited.get(key, 0) >= val:
                        continue
                    waited[key] = val
                    eng.wait_ge(sem, val)
                if o.fn is None:
                    continue
                inst = o.fn(eng)
                if o.dma:
                    inst.then_inc(dsems[e][o.dma_idx % NDMASEM], 16)
                elif o.waited:
                    inst.then_inc(sems[e], 1)

        @block.tensor
        def _(eng):
            run("pe", eng)

        @block.scalar
        def _(eng):
            run("act", eng)

        @block.vector
        def _(eng):
            run("dve", eng)

        @block.gpsimd
        def _(eng):
            run("pool", eng)

        @block.sync
        def _(eng):
            run("sp", eng)


def make_consts():
    c = {}
    c["ident_f"] = np.eye(128, dtype=np.float32)
    c["ident_b"] = np.eye(128).astype(ml_dtypes.bfloat16)
    for tag, P, L in (("p", 128, 64), ("s", 64, 16)):
        s = np.arange(P)[:, None]
        t = np.arange(P)[None, :]
        same = (s // L) == (t // L)
        tri = same & (s <= t)
        ref = (t // L) * L + L // 2
        refm = same & (s <= ref)
        endm = same
        c["tri_" + tag] = tri.astype(np.float32)
        c["d1m_" + tag] = tri.astype(np.float32) - refm.astype(np.float32)
        c["d2m_" + tag] = endm.astype(np.float32) - tri.astype(np.float32)
        nc_ = P // L
        chk = (np.arange(P)[:, None] // L) == np.arange(nc_)[None, :]
        c["chk_" + tag] = chk.astype(np.float32)
        c["tri4_" + tag] = np.tile(tri.astype(np.float32), (1, 4)).astype(ml_dtypes.bfloat16)
    ms = (np.arange(64)[None, :] // 16) == np.arange(4)[:, None]
    c["ms"] = np.broadcast_to(ms[None].astype(np.float32), (128, 4, 64)).astype(ml_dtypes.bfloat16).copy()
    j = np.arange(128)[:, None]
    i = np.arange(128)[None, :]
    c["maskg"] = (~((i < 64) & (j >= 64))).astype(np.float32)
    return c


def build(cfg):
    nc = bass.Bass("TRN2", target_bir_lowering=False)
    D, KT, S, T = cfg.D, cfg.KT, cfg.S, cfg.T
    WA, WB, HA, HB, DFF = cfg.WA, cfg.WB, cfg.HA, cfg.HB, cfg.DFF
    NP, NS, NSTR, LS = cfg.NP, cfg.NS, cfg.NSTR, cfg.LS
    KS = cfg.KS

    def din(name, shape, dt=F32):
        return nc.dram_tensor(name, list(shape), dt, kind="ExternalInput").ap()

    def dout(name, shape, dt=F32):
        return nc.dram_tensor(name, list(shape), dt, kind="ExternalOutput").ap()

    xp = din("xp", [NP, D])
    xprev = din("xprev", [NP, D])
    xs = din("xs", [NS, D])
    s0 = din("s0", [NSTR, HB, 128, 128])
    w_in = din("w_in", [D, cfg.NIN])
    w_out = din("w_out", [D, D])
    w_up = din("w_up", [D, DFF])
    w_down = din("w_down", [DFF, D])
    norm1 = din("norm1", [D])
    norm2 = din("norm2", [D])
    norm_f = din("norm_f", [D])
    norm_v = din("norm_v", [WA])
    norm_o = din("norm_o", [WB])
    lbl = din("lb_logits", [2, WB])
    w_s = din("w_s", [HA, 128, 128])
    b_s = din("b_s", [HA, 128])
    cst = {}
    for k, v in make_consts().items():
        cst[k] = din("c_" + k, v.shape, BF16 if v.dtype == ml_dtypes.bfloat16 else F32)

    yp = dout("yp", [NP, D])
    ys = dout("ys", [NS, D])
    spo = dout("spo", [HB, 128, 128])
    sso = dout("sso", [NSTR, HB, 128, 128])
    vso = dout("vso", [NS, WA])

    NSLAB = (D * cfg.NIN + D * D + 2 * D * DFF) // (KS * 128 * 512)
    WSC_CH = 64
    wsc_t = [nc.dram_tensor("wscratch%d" % i, [WSC_CH, 128, KS * 512], BF16, kind="Internal").ap()
             for i in range((NSLAB + WSC_CH - 1) // WSC_CH)]

    class _Wsc:
        def __getitem__(self, key):
            idx = key[0]
            return wsc_t[idx // WSC_CH][(idx % WSC_CH,) + tuple(key[1:])]
    wsc = _Wsc()
    wcache = {}
    r_wsc = {}

    base = [nc._sbuf_addr_for_side(None)]
    base[0] = (base[0] + 63) // 64 * 64
    limit = base[0] + nc.sbuf_bytes_remaining - 64

    def salloc(name, shape, dt, at=None):
        nbytes = int(np.prod(shape[1:])) * (2 if dt == BF16 else 4)
        nbytes = (nbytes + 63) // 64 * 64
        if at is None:
            off = base[0]
            base[0] += nbytes
            assert base[0] <= limit, ("SBUF overflow", name, base[0], limit)
        else:
            off = at
        return nc.alloc_sbuf_tensor_at(name, list(shape), dt, offset=off), off, nbytes

    X1, X1off, X1bytes = salloc("X1", [128, S, D], F32)
    if X1bytes < 65536:
        _, _, padb = salloc("X1pad", [128, (65536 - X1bytes) // 4], F32)
        X1bytes += padb
    HT, _, _ = salloc("HT", [128, KT, T], BF16)
    YT, YToff, YTbytes = salloc("YT", [128, KT, T], BF16)
    need_yt = max(2 * (cfg.FC // 128) * T * 2, D * 4)
    if YTbytes < need_yt:
        _, _, padb = salloc("YTpad", [128, (need_yt - YTbytes) // 4], F32)
        YTbytes += padb
    WR = [salloc("WR%d" % i, [128, KS, 512], BF16)[0] for i in range(cfg.WSLOTS)]
    S32, _, _ = salloc("S32", [128, max(HB, 16), 128], F32)
    LB, _, _ = salloc("LB", [128, WB], F32)
    OML, _, _ = salloc("OML", [128, WB], F32)
    ident_f, _, _ = salloc("ident_f", [128, 128], F32)
    ident_b, _, _ = salloc("ident_b", [128, 128], BF16)
    C = {}
    for tag, P in (("p", 128), ("s", 64)):
        for nm in ("tri", "d1m", "d2m"):
            C[nm + "_" + tag], _, _ = salloc(nm + "_" + tag, [P, P], F32)
        C["chk_" + tag], _, _ = salloc("chk_" + tag, [P, P // (64 if tag == "p" else 16)], F32)
        C["tri4_" + tag], _, _ = salloc("tri4_" + tag, [P, 4 * P], BF16)
    MS, _, _ = salloc("MS", [128, 4, 64], BF16)
    WT, _, _ = salloc("WT", [128, HA, 128], BF16)
    WTS, _, _ = salloc("WTS", [64, HA, 64], BF16)
    BSC, _, _ = salloc("BSC", [128, HA], F32)
    BSS, _, _ = salloc("BSS", [64, HA], F32)
    GC1, _, _ = salloc("GC1", [128, KT], F32)
    GC2, _, _ = salloc("GC2", [128, KT], F32)
    SM, _, _ = salloc("SM", [128, 64], F32)
    EPSC, _, _ = salloc("EPSC", [128, 1], F32)

    ov = [X1off]

    def oalloc(name, shape, dt):
        t, off, nb = salloc(name, shape, dt, at=ov[0])
        ov[0] += nb
        assert ov[0] <= X1off + X1bytes, ("overlay overflow", name)
        return t

    NVO = oalloc("NVO", [128, max(WA, WB)], F32)
    RAW = [oalloc("RAW%d" % i, [128, S, 512], F32) for i in range(3)]
    IBt = oalloc("IB", [128, S, 512], BF16)
    IBt2 = oalloc("IB2", [128, S, 512], BF16)
    Ft = oalloc("Ft", [128, 512], F32)
    KKt = oalloc("KKt", [128, 512], F32)
    Et = [oalloc("Et%d" % i, [128, 512], F32) for i in range(2)]
    GNt = oalloc("GNt", [128, 512], F32)
    BFt = [oalloc("BFt%d" % i, [128, 512], BF16) for i in range(6)]
    TTt = [oalloc("TTt%d" % i, [128, 4, 128], BF16) for i in range(3)]
    SBt = oalloc("SBt", [128, 4, 4, 128], BF16)
    DECt = oalloc("DECt", [128, 16], F32)
    if S >= 4:
        raw2_off = X1off + (max(WA, WB) * 4 + 63) // 64 * 64 + 2 * (S * 512 * 4)
        KEM = nc.alloc_sbuf_tensor_at("KEM", [64, 4, 512], BF16, offset=raw2_off + 2048)
        QBM = nc.alloc_sbuf_tensor_at("QBM", [128, 4, 4, 64], BF16, offset=raw2_off + 2048 + 4096)
    else:
        KEM = oalloc("KEM", [64, 4, 512], BF16)
        QBM = oalloc("QBM", [128, 4, 4, 64], BF16)
    stage1_end = ov[0]
    ov[0] = X1off
    XIN = [oalloc("XIN%d" % i, [128, D], F32) for i in range(2)]
    HIDT = [nc.alloc_sbuf_tensor_at("HIDT%d" % i, [128, cfg.FC // 128, T], BF16,
                                    offset=YToff + i * (cfg.FC // 128) * T * 2) for i in range(2)]
    assert 2 * (cfg.FC // 128) * T * 2 <= YTbytes
    JUNK, _, _ = salloc("JUNK", [128, 512], BF16)
    XSCP = [salloc("XSCP%d" % i, [128, 512], F32)[0] for i in range(2)]
    RT, _, _ = salloc("RT", [128, 512], F32)
    HTM = [salloc("HTM%d" % i, [128, 512], BF16)[0] for i in range(2)]
    NFB = None
    NFBt = nc.alloc_sbuf_tensor_at("NFB", [128, D], F32, offset=YToff)
    assert D * 4 <= YTbytes

    GPS = [nc.alloc_psum_tensor("gps%d" % i, [128, 512], F32) for i in range(4)]
    AUX = [nc.alloc_psum_tensor("aux%d" % i, [128, 512], F32) for i in range(4)]

    pg = Prog()
    R = {}

    def res(name):
        if name not in R:
            R[name] = Res()
        return R[name]

    r_gps = [Res(excl=True) for _ in range(4)]
    r_aux = [Res(excl=True) for _ in range(4)]
    r_wr = [Res() for _ in range(cfg.WSLOTS)]
    r_x1reg = Res()
    aux_i = [0]

    def aux():
        i = aux_i[0] % 4
        aux_i[0] += 1
        return AUX[i], r_aux[i]

    wr_i = [0]

    flip = [0]

    def evac_eng():
        flip[0] ^= 1
        return "act" if flip[0] else "dve"

    def copy_op(eng, out, in_, reads, writes, upd=()):
        if eng == "act":
            return pg.op("act", lambda e: e.activation(out=out, in_=in_, func=AF.Copy), reads, writes, upd)
        return pg.op("dve", lambda e: e.tensor_copy(out=out, in_=in_), reads, writes, upd)

    deferred = []

    def pump():
        if deferred:
            deferred.pop(0)()

    def flush():
        while deferred:
            deferred.pop(0)()

    def drive(gen):
        def step():
            try:
                next(gen)
                deferred.insert(0, step)
            except StopIteration:
                pass
        deferred.append(step)

    def dma(eng, out, in_, reads=(), writes=()):
        return pg.op(eng, lambda e: e.dma_start(out=out, in_=in_, allow_slow_non_contiguous=True), reads, writes, dma=True)

    r_const = res("const")
    setup_ops = []
    setup_ops.append(dma("sp", ident_f[:, :], cst["ident_f"][:, :]))
    setup_ops.append(dma("sp", ident_b[:, :], cst["ident_b"][:, :]))
    for tag in ("p", "s"):
        for nm in ("tri", "d1m", "d2m", "chk", "tri4"):
            k = nm + "_" + tag
            setup_ops.append(dma("sp", C[k][:, :], cst[k][:, :]))
    setup_ops.append(dma("sp", MS[:, :, :], cst["ms"][:, :, :]))
    with nc.allow_non_contiguous_dma(reason="tiny parameter loads"):
        setup_ops.append(dma("sp", GC1[:, :], norm1.rearrange("(kt p) -> p kt", p=128)))
        setup_ops.append(dma("sp", GC2[:, :], norm2.rearrange("(kt p) -> p kt", p=128)))
        setup_ops.append(dma("sp", BSC[:, :], b_s.rearrange("h i -> i h")))
        for st in range(NSTR):
            setup_ops.append(dma("sp", BSS[st * LS:(st + 1) * LS, :], b_s[:, 0:LS].rearrange("h i -> i h")))
    setup_ops.append(dma("sp", LB[:, :], lbl[0, :].partition_broadcast(128)))
    setup_ops.append(dma("sp", OML[:, :], lbl[1, :].partition_broadcast(128)))
    r_setup = Res()
    r_setup.writers = list(setup_ops)
    o1 = pg.op("dve", lambda e: e.tensor_tensor(out=LB[:, :], in0=LB[:, :], in1=OML[:, :], op=ALU.subtract),
               reads=[r_setup], writes=[res("LB")])
    pg.op("act", lambda e: e.activation(out=LB[:, :], in_=LB[:, :], func=AF.Sigmoid), reads=[], writes=[res("LB")])
    pg.op("dve", lambda e: e.tensor_scalar(out=OML[:, :], in0=LB[:, :], scalar1=-1.0, scalar2=1.0,
                                           op0=ALU.mult, op1=ALU.add), reads=[res("LB")], writes=[res("OML")])
    pg.op("dve", lambda e: e.memset(EPSC[:, :], EPS), writes=[res("EPSC")])
    pg.op("dve", lambda e: e.memset(S32[:, :, :], 0.0), writes=[res("S32")])

    r_stage0 = res("x1region_setup")
    WSTG = XIN[0]
    assert HA * 128 <= D
    MG = XIN[1]
    o_ws = dma("sp", WSTG[:, 0:HA * 128].rearrange("p (h j) -> p h j", j=128), w_s.rearrange("h i j -> i h j"),
               writes=[r_stage0])
    o_mg = dma("sp", MG[:, D - 128:D], cst["maskg"][:, :], reads=[r_stage0])
    r_mg = Res()
    r_mg.writers = [o_mg]
    for h in range(HA):
        ps, rps = aux()
        pg.op("pe", lambda e, h=h, ps=ps: e.transpose(out=ps[:, 0:128], in_=WSTG[:, h * 128:(h + 1) * 128],
                                                      identity=ident_f[:, :]),
              reads=[r_stage0, r_setup], writes=[rps])
        pg.op("dve", lambda e, h=h, ps=ps: e.tensor_tensor(out=WT[:, h, :], in0=ps[:, 0:128], in1=MG[:, D - 128:D],
                                                           op=ALU.mult),
              reads=[rps, r_mg], upd=[res("WT")])
    WSS = XIN[1]
    o_z = pg.op("dve", lambda e: e.memset(WSS[0:64, 0:HA * 64], 0.0), reads=[r_stage0], writes=[res("WSS")])
    with nc.allow_non_contiguous_dma(reason="tiny parameter loads"):
        dd = []
        for st in range(NSTR):
            dd.append(dma("sp", WSS[st * LS:(st + 1) * LS, 0:HA * 64].rearrange("p (h j) -> p h j", j=64)[:, :, st * LS:(st + 1) * LS],
                          w_s[:, 0:LS, 0:LS].rearrange("h i j -> i h j"), reads=[res("WSS")]))
    r_wss = Res()
    r_wss.writers = dd
    for h in range(HA):
        ps, rps = aux()
        pg.op("pe", lambda e, h=h, ps=ps: e.transpose(out=ps[0:64, 0:64], in_=WSS[0:64, h * 64:(h + 1) * 64],
                                                      identity=ident_f[0:64, 0:64]),
              reads=[r_wss, r_setup], writes=[rps])
        pg.op("dve", lambda e, h=h, ps=ps: e.tensor_copy(out=WTS[:, h, :], in_=ps[0:64, 0:64]),
              reads=[rps], upd=[res("WT")])
    r_x1 = res("x1coarse")

    def stage_barrier(tag):
        pass

    def rstd_from_ss(ssap, P, n, invn, rkey):
        pg.op("act", lambda e: e.activation(out=ssap, in_=ssap, func=AF.Ln, scale=invn, bias=EPSC[0:P, :]),
              reads=[rkey, res("EPSC")], writes=[rkey])
        pg.op("act", lambda e: e.activation(out=ssap, in_=ssap, func=AF.Exp, scale=-0.5), reads=[rkey], writes=[rkey])

    NPC = D // 512

    def sumsq(xap, P, ssap, rsrc, rss):
        rpart = res("sspart")
        ck("n0")
        for c in range(NPC):
            pg.op("act", lambda e, c=c: e.activation(out=JUNK[0:P, :], in_=xap[:, c * 512:(c + 1) * 512], func=AF.Square,
                                                     accum_out=SM[0:P, 32 + c:33 + c]),
                  reads=[rsrc], writes=[res("junk")], upd=[rpart])
        pg.op("dve", lambda e: e.tensor_reduce(out=ssap, in_=SM[0:P, 32:32 + NPC], axis=mybir.AxisListType.X, op=ALU.add),
              reads=[rpart], writes=[rss])
        rstd_from_ss(ssap, P, 1, 1.0 / D, rss)

    def norm_stats(s, xap, P, rsrc):
        sumsq(xap, P, SM[0:P, s:s + 1], rsrc, res("ss%d" % s))

    def norm_apply(s, xap, P, rsrc, gcol, dstT, dst_res):
        ssap = SM[0:P, s:s + 1]
        rss = res("ss%d" % s)
        wl = []
        for c in range(NPC):
            xb = XSCP[c % 2]
            rxs = res("xscp%d" % (c % 2))
            pg.op("pool", lambda e, c=c, xb=xb: e.tensor_scalar(out=xb[0:P, :], in0=xap[:, c * 512:(c + 1) * 512], scalar1=ssap,
                                                                scalar2=1.0, op0=ALU.mult, op1=ALU.mult),
                  reads=[rsrc, rss], writes=[rxs])
            ps, rps = aux()

            def tr(e, ps=ps, xb=xb):
                last = None
                for j in range(4):
                    last = e.transpose(out=ps[:, j * 128:j * 128 + P], in_=xb[0:P, j * 128:(j + 1) * 128],
                                       identity=ident_f[0:P, 0:P])
                return last
            pg.op("pe", tr, reads=[rxs, r_setup], writes=[rps])
            eng = evac_eng()
            for j in range(4):
                kt = c * 4 + j
                out = dstT[:, kt, s * 128:s * 128 + P]
                in_ = ps[:, j * 128:j * 128 + P]
                if eng == "act":
                    o = pg.op("act", lambda e, out=out, in_=in_, kt=kt: e.activation(out=out, in_=in_, func=AF.Identity,
                                                                                      scale=gcol[:, kt:kt + 1]),
                              reads=[rps, r_setup])
                else:
                    o = pg.op("dve", lambda e, out=out, in_=in_, kt=kt: e.tensor_scalar(out=out, in0=in_,
                                                                                        scalar1=gcol[:, kt:kt + 1],
                                                                                        scalar2=None, op0=ALU.mult),
                              reads=[rps, r_setup])
                wl = [x for x in wl if x.eng != o.eng] + [o]
        dst_res[s].writers = wl

    def load_slab(wname, wsrc, row0, sl, kk, col0):
        slot = wr_i[0] % cfg.WSLOTS
        wr_i[0] += 1
        wt = WR[slot]
        src = wsrc[row0 + sl * KS * 128: row0 + (sl * KS + kk) * 128, col0:col0 + 512].rearrange(
            "(kt p) n -> p kt n", p=128)
        key = (wname, row0 + sl * KS * 128, col0)
        if key in wcache:
            idx = wcache[key]
            pg.op("sp", lambda e, wt=wt, idx=idx, kk=kk: e.dma_start(
                out=wt[:, 0:kk, :], in_=wsc[idx, :, 0:kk * 512].rearrange("p (k n) -> p k n", n=512)),
                reads=[r_wsc[idx]], writes=[r_wr[slot]], dma=True)
        else:
            idx = len(wcache)
            wcache[key] = idx
            r_wsc[idx] = Res()
            pg.op("pool", lambda e, wt=wt, src=src, kk=kk: e.dma_start(out=wt[:, 0:kk, :], in_=src),
                  writes=[r_wr[slot]], dma=True)
            pg.op("sp", lambda e, wt=wt, idx=idx, kk=kk: e.dma_start(
                out=wsc[idx, :, 0:kk * 512].rearrange("p (k n) -> p k n", n=512), in_=wt[:, 0:kk, :]),
                reads=[r_wr[slot]], writes=[r_wsc[idx]], dma=True)
        return wt, slot

    def gemm_fm(actT, act_res, nk, wsrc, row0, col0, subt, evac_m, wname=None):
        ncol = sum(P for (_, P) in subt)
        nsl = (nk + KS - 1) // KS
        rd = [act_res[s] for (s, _) in subt]
        for sl in range(nsl):
            kk = min(KS, nk - sl * KS)
            wt, slot = load_slab(wname, wsrc, row0, sl, kk, col0)
            for m in range(4):
                def mm(e, m=m, sl=sl, kk=kk, wt=wt):
                    last = None
                    for j in range(kk):
                        kt = sl * KS + j
                        last = e.matmul(GPS[m][:, 0:ncol], lhsT=wt[:, j, m * 128:(m + 1) * 128], rhs=actT[:, kt, 0:ncol],
                                        start=(kt == 0), stop=(kt == nk - 1))
                    return last
                if sl == 0:
                    pg.op("pe", mm, reads=rd + [r_wr[slot]], writes=[r_gps[m]])
                else:
                    pg.op("pe", mm, reads=rd + [r_wr[slot]], upd=[r_gps[m]])
                pump()
                if sl == nsl - 1:
                    evac_m(m, ncol, GPS[m][:, 0:ncol], r_gps[m])

    def gemm(actT, act_res, nk, wsrc, row0, col0, subt, evac, do_pump=True, wname=None, npump=1):
        nsl = (nk + KS - 1) // KS
        for sl in range(nsl):
            kk = min(KS, nk - sl * KS)
            wt, slot = load_slab(wname, wsrc, row0, sl, kk, col0)
            for (s, P) in subt:
                def mm(e, s=s, P=P, sl=sl, kk=kk, wt=wt):
                    last = None
                    for j in range(kk):
                        kt = sl * KS + j
                        last = e.matmul(GPS[s][0:P, :], lhsT=actT[:, kt, s * 128:s * 128 + P], rhs=wt[:, j, :],
                                        start=(kt == 0), stop=(kt == nk - 1))
                    return last
                if sl == 0:
                    pg.op("pe", mm, reads=[act_res[s], r_wr[slot]], writes=[r_gps[s]])
                else:
                    pg.op("pe", mm, reads=[act_res[s], r_wr[slot]], upd=[r_gps[s]])
                if do_pump:
                    for _ in range(npump):
                        pump()
                if sl == nsl - 1:
                    evac(s, P, GPS[s][0:P, :], r_gps[s])

    r_ht = [Res() for _ in range(S)]
    r_yt = [Res() for _ in range(S)]
    r_xin = [Res(), Res()]

    def mixer_A(jp, subt, sample):
        vb = 1 + jp % 2
        U, V = RAW[0], RAW[vb]
        for (s, P) in subt:
            rU, rV = res("raw0_%d" % s), res("raw%d_%d" % (vb, s))
            rss = res("ssv")
            for hh in range(2):
                pg.op("act", lambda e, s=s, P=P, hh=hh: e.activation(out=JUNK[0:P, 0:256], in_=V[0:P, s, hh * 256:(hh + 1) * 256],
                                                                      func=AF.Square, accum_out=SM[0:P, 8 + hh:9 + hh]),
                      reads=[rV], writes=[res("junk")], upd=[rss])
            rss.writers = [pg.q["act"][-1]]
            rstd_from_ss(SM[0:P, 8:10], P, 2, 1.0 / 256, rss)
            VN = BFt[5]
            rvn = res("vn")
            first = True
            for hh in range(2):
                c0 = hh * 256
                if sample:
                    pg.op("dve", lambda e, s=s, P=P, hh=hh, c0=c0: e.scalar_tensor_tensor(
                        out=Et[0][0:P, c0:c0 + 256], in0=V[0:P, s, c0:c0 + 256], scalar=SM[0:P, 8 + hh:9 + hh],
                        in1=NVO[0:P, jp * 512 + c0: jp * 512 + c0 + 256], op0=ALU.mult, op1=ALU.mult),
                        reads=[rV, rss, res("NVO")], writes=[res("E0")] if first else [], upd=[] if first else [res("E0")])
                    pg.op("dve", lambda e, P=P, c0=c0: e.tensor_copy(out=VN[0:P, c0:c0 + 256], in_=Et[0][0:P, c0:c0 + 256]),
                          reads=[res("E0")], writes=[rvn] if first else [], upd=[] if first else [rvn])
                else:
                    pg.op("dve", lambda e, s=s, P=P, hh=hh, c0=c0: e.scalar_tensor_tensor(
                        out=VN[0:P, c0:c0 + 256], in0=V[0:P, s, c0:c0 + 256], scalar=SM[0:P, 8 + hh:9 + hh],
                        in1=NVO[0:P, jp * 512 + c0: jp * 512 + c0 + 256], op0=ALU.mult, op1=ALU.mult),
                        reads=[rV, rss, res("NVO")], writes=[rvn] if first else [], upd=[] if first else [rvn])
                first = False
            if sample:
                dma("sp", vso[0:P, jp * 512:(jp + 1) * 512], Et[0][0:P, :], reads=[res("E0")])
                out_dmas.append(pg.q["sp"][-1])
            yield
            ps, rps = aux()

            def mix(e, P=P, ps=ps):
                last = None
                for hh in range(2):
                    hg = 2 * jp + hh
                    lhsT = WTS[0:P, hg, :] if sample else WT[:, hg, :]
                    last = e.matmul(ps[0:P, hh * 256:(hh + 1) * 256], lhsT=lhsT, rhs=VN[0:P, hh * 256:(hh + 1) * 256],
                                    start=True, stop=True)
                return last
            pg.op("pe", mix, reads=[rvn, res("WT")], writes=[rps])
            YA = BFt[4]
            rya = res("ya")
            for hh in range(2):
                hg = 2 * jp + hh
                bcol = BSS[0:P, hg:hg + 1] if sample else BSC[:, hg:hg + 1]
                pg.op("dve", lambda e, s=s, P=P, hh=hh, ps=ps, bcol=bcol: e.scalar_tensor_tensor(
                    out=YA[0:P, hh * 256:(hh + 1) * 256], in0=ps[0:P, hh * 256:(hh + 1) * 256], scalar=bcol,
                    in1=U[0:P, s, hh * 256:(hh + 1) * 256], op0=ALU.add, op1=ALU.mult),
                    reads=[rps, rU, r_setup], writes=[rya] if hh == 0 else [], upd=[] if hh == 0 else [rya])
            yield
            y_to_T(YA, rya, s, P, 4 * jp)
            yield

    def y_to_T(Y, ry, s, P, kt0):
        ps, rps = aux()
        psb = ps[:, :].bitcast(BF16)

        def tr(e, P=P, psb=psb):
            last = None
            for j in range(4):
                last = e.transpose(out=psb[:, j * 128:j * 128 + P], in_=Y[0:P, j * 128:(j + 1) * 128],
                                   identity=ident_b[0:P, 0:P])
            return last
        pg.op("pe", tr, reads=[ry, r_setup], writes=[rps])
        eng = evac_eng()
        out = YT[:, kt0:kt0 + 4, s * 128:s * 128 + P]
        in_ = psb[:, 0:512].rearrange("p (j t) -> p j t", t=128)[:, :, 0:P]
        o = copy_op(eng, out, in_, [rps], [], upd=[])
        r_yt[s].writers = [x for x in r_yt[s].writers if x.eng != o.eng] + [o]

    IBt_a = IBt

    def mixer_B(Q, subt, sample, pre):
        SG, QS, GS = RAW[0], RAW[1], RAW[2]
        IBt = (IBt_a, IBt2)[Q % 2]
        tag = "s" if sample else "p"
        TRI, D1M, D2M, CHK, TRI4 = C["tri_" + tag], C["d1m_" + tag], C["d2m_" + tag], C["chk_" + tag], C["tri4_" + tag]
        for (s, P) in subt:
            L = LS if sample else 64
            NC = P // L
            rSG, rQS, rGS, rIB = res("raw0_%d" % s), res("raw1_%d" % s), res("raw2_%d" % s), res("ib%d_%d" % (Q % 2, s))
            rF, rK = res("F"), res("KK")
            cq = slice(Q * 512, (Q + 1) * 512)
            pg.op("dve", lambda e, s=s, P=P: e.tensor_tensor(out=Ft[0:P, :], in0=SG[0:P, s, :], in1=OML[0:P, cq], op=ALU.mult),
                  reads=[rSG, res("OML")], writes=[rF])
            pg.op("dve", lambda e, P=P: e.tensor_tensor(out=Ft[0:P, :], in0=Ft[0:P, :], in1=LB[0:P, cq], op=ALU.add),
                  reads=[rF, res("LB")], writes=[rF])
            pg.op("dve", lambda e, P=P: e.tensor_scalar(out=KKt[0:P, :], in0=Ft[0:P, :], scalar1=-1.0, scalar2=1.0,
                                                        op0=ALU.mult, op1=ALU.add), reads=[rF], writes=[rK])
            pg.op("act", lambda e, P=P: e.activation(out=Ft[0:P, :], in_=Ft[0:P, :], func=AF.Ln), reads=[rF, rK], writes=[rF])
            if not pre:
                rGN = res("GN")
                pg.op("dve", lambda e, s=s, P=P: e.tensor_tensor(out=GNt[0:P, :], in0=GS[0:P, s, :], in1=NVO[0:P, cq], op=ALU.mult),
                      reads=[rGS, res("NVO")], writes=[rGN])
            yield
            def cum(M, P=P):
                ps, rps = aux()
                pg.op("pe", lambda e, ps=ps: e.matmul(ps[0:P, :], lhsT=M[0:P, 0:P], rhs=Ft[0:P, :], start=True, stop=True),
                      reads=[rF, r_setup], writes=[rps])
                return ps, rps
            psD2, rD2 = cum(D2M)
            psDC, rDC = aux()

            def dec(e, P=P, psDC=psDC, NC=NC):
                last = None
                for h in range(4):
                    last = e.matmul(psDC[:, h * NC:(h + 1) * NC], lhsT=Ft[0:P, h * 128:(h + 1) * 128], rhs=CHK[0:P, 0:NC],
                                    start=True, stop=True)
                return last
            pg.op("pe", dec, reads=[rF, r_setup], writes=[rDC])
            rDEC = res("DEC")
            pg.op("act", lambda e, psDC=psDC, NC=NC: e.activation(out=DECt[:, 0:4 * NC], in_=psDC[:, 0:4 * NC], func=AF.Exp),
                  reads=[rDC], writes=[rDEC])
            KE, rKE = BFt[3], res("KE")
            rE1 = res("E1")
            pg.op("act", lambda e, P=P, psD2=psD2: e.activation(out=Et[1][0:P, :], in_=psD2[0:P, :], func=AF.Exp),
                  reads=[rD2], writes=[rE1])
            pg.op("dve", lambda e, P=P: e.tensor_tensor(out=KE[0:P, :], in0=KKt[0:P, :], in1=Et[1][0:P, :], op=ALU.mult),
                  reads=[rK, rE1], writes=[rKE])
            if not pre:
                psB, rB = cum(TRI)
                psD1, rD1 = cum(D1M)
                QR, KR, QB = BFt[0], BFt[1], BFt[2]
                rQR, rKR, rQB = res("QR"), res("KR"), res("QB")
                rE0 = res("E0")
                pg.op("act", lambda e, P=P, psD1=psD1: e.activation(out=Et[0][0:P, :], in_=psD1[0:P, :], func=AF.Exp),
                      reads=[rD1], writes=[rE0])
                pg.op("dve", lambda e, s=s, P=P: e.tensor_tensor(out=QR[0:P, :], in0=QS[0:P, s, :], in1=Et[0][0:P, :], op=ALU.mult),
                      reads=[rQS, rE0], writes=[rQR])
                pg.op("act", lambda e, P=P, psD1=psD1: e.activation(out=Et[1][0:P, :], in_=psD1[0:P, :], func=AF.Exp, scale=-1.0),
                      reads=[rD1, rKE], writes=[rE1])
                pg.op("dve", lambda e, P=P: e.tensor_tensor(out=KR[0:P, :], in0=KKt[0:P, :], in1=Et[1][0:P, :], op=ALU.mult),
                      reads=[rK, rE1], writes=[rKR])
                pg.op("act", lambda e, P=P, psB=psB: e.activation(out=Et[0][0:P, :], in_=psB[0:P, :], func=AF.Exp),
                      reads=[rB, rQR], writes=[rE0])
                pg.op("dve", lambda e, s=s, P=P: e.tensor_tensor(out=QB[0:P, :], in0=QS[0:P, s, :], in1=Et[0][0:P, :], op=ALU.mult),
                      reads=[rQS, rE0], writes=[rQB])
            yield
            rS = res("S32")
            rSB = res("SB")
            hq = slice(4 * Q, 4 * Q + 4)
            if sample:
                dd = []
                for st in range(NSTR):
                    dd.append(dma("sp", S32[:, st * 4:(st + 1) * 4, :], s0[st, 4 * Q:4 * Q + 4, :, :].rearrange("h d v -> d h v"),
                                  writes=[rS] if st == 0 else [], reads=[] if st == 0 else []))
                rS.writers = dd
                if not pre:
                    pg.op("act", lambda e: e.activation(out=SBt[:, :, :, :].rearrange("d h c v -> d c h v"),
                                                        in_=S32[:, 0:16, :].rearrange("d (c h) v -> d c h v", h=4), func=AF.Copy),
                          reads=[rS], writes=[rSB])
                rKEM = res("KEM")
                for st in range(NSTR):
                    pg.op("dve", lambda e, st=st: e.tensor_scalar(out=KEM[:, st, :], in0=KE[0:64, :], scalar1=CHK[0:64, st:st + 1],
                                                                  scalar2=None, op0=ALU.mult),
                          reads=[rKE, r_setup], writes=[rKEM] if st == 0 else [], upd=[] if st == 0 else [rKEM])
            else:
                if not pre:
                    pg.op("act", lambda e: e.activation(out=SBt[:, :, 0, :], in_=S32[:, hq, :], func=AF.Copy),
                          reads=[rS], writes=[rSB])
            psDS = []
            for c in range(NC):
                ps, rps = aux()
                psDS.append((ps, rps))

                def dsm(e, c=c, ps=ps, L=L, s=s):
                    last = None
                    for h in range(4):
                        if sample:
                            lhsT = KEM[:, c, h * 128:(h + 1) * 128]
                            rhs = IBt[0:64, s, h * 128:(h + 1) * 128]
                        else:
                            lhsT = KE[c * L:(c + 1) * L, h * 128:(h + 1) * 128]
                            rhs = IBt[c * L:(c + 1) * L, s, h * 128:(h + 1) * 128]
                        last = e.matmul(ps[:, h * 128:(h + 1) * 128], lhsT=lhsT, rhs=rhs, start=True, stop=True)
                    return last
                pg.op("pe", dsm, reads=[res("KEM") if sample else rKE, rIB], writes=[rps])
            yield
            for c in range(NC):
                ps, rps = psDS[c]
                for h in range(4):
                    sidx = (c * 4 + h) if sample else (4 * Q + h)
                    pg.op("dve", lambda e, c=c, h=h, ps=ps, sidx=sidx, NC=NC: e.scalar_tensor_tensor(
                        out=S32[:, sidx, :], in0=S32[:, sidx, :], scalar=DECt[:, h * NC + c:h * NC + c + 1],
                        in1=ps[:, h * 128:(h + 1) * 128], op0=ALU.mult, op1=ALU.add),
                        reads=[rps, rDEC, rS] + ([rSB] if (not pre and (sample or c == 0)) else []), writes=[rS])
                if (not sample) and (not pre) and c + 1 < NC:
                    pg.op("act", lambda e, c=c: e.activation(out=SBt[:, :, c + 1, :], in_=S32[:, hq, :], func=AF.Copy),
                          reads=[rS], upd=[rSB])
                    rSB.writers = [pg.q["act"][-1]]
            if sample:
                for st in range(NSTR):
                    dma("sp", sso[st, 4 * Q:4 * Q + 4, :, :].rearrange("h d v -> d h v"), S32[:, st * 4:(st + 1) * 4, :], reads=[rS])
                    out_dmas.append(pg.q["sp"][-1])
                    rS.readers.append(pg.q["sp"][-1])
            if pre:
                continue
            psT1, rT1 = aux()
            psT2, rT2 = aux()
            pT1 = psT1[:, :].bitcast(BF16)
            pT2 = psT2[:, :].bitcast(BF16)

            def trq(e, P=P, pT1=pT1, pT2=pT2):
                last = None
                for h in range(4):
                    e.transpose(out=pT1[:, h * 128:h * 128 + P], in_=QR[0:P, h * 128:(h + 1) * 128], identity=ident_b[0:P, 0:P])
                    e.transpose(out=pT1[:, 512 + h * 128:512 + h * 128 + P], in_=KR[0:P, h * 128:(h + 1) * 128],
                                identity=ident_b[0:P, 0:P])
                    last = e.transpose(out=pT2[:, h * 128:h * 128 + P], in_=QB[0:P, h * 128:(h + 1) * 128],
                                       identity=ident_b[0:P, 0:P])
                return last
            pg.op("pe", trq, reads=[rQR, rKR, rQB, r_setup], writes=[rT1, rT2])
            QRT, KRT, QBT = TTt[0], TTt[1], TTt[2]
            rQRT, rKRT, rQBT = res("QRT"), res("KRT"), res("QBT")
            copy_op("act", QRT[:, :, 0:P], pT1[:, 0:512].rearrange("p (h t) -> p h t", t=128)[:, :, 0:P], [rT1], [rQRT])
            copy_op("dve", KRT[:, :, 0:P], pT1[:, 512:1024].rearrange("p (h t) -> p h t", t=128)[:, :, 0:P], [rT1], [rKRT])
            copy_op("act", QBT[:, :, 0:P], pT2[:, 0:512].rearrange("p (h t) -> p h t", t=128)[:, :, 0:P], [rT2], [rQBT])
            if sample:
                rQBM = res("QBM")
                for st in range(NSTR):
                    pg.op("dve", lambda e, st=st: e.tensor_tensor(out=QBM[:, :, st, :], in0=QBT[:, :, 0:64],
                                                                  in1=MS[:, st:st + 1, :].broadcast_to([128, 4, 64]) if False else
                                                                  MS[:, st, :].rearrange("p (o t) -> p o t", o=1).broadcast_to([128, 4, 64]),
                                                                  op=ALU.mult),
                          reads=[rQBT, r_setup], writes=[rQBM] if st == 0 else [], upd=[] if st == 0 else [rQBM])
            yield
            psS, rpS = aux()

            def sc(e, P=P, psS=psS):
                last = None
                for h in range(4):
                    last = e.matmul(psS[0:P, h * P:(h + 1) * P], lhsT=KRT[:, h, 0:P], rhs=QRT[:, h, 0:P], start=True, stop=True)
                return last
            pg.op("pe", sc, reads=[rQRT, rKRT], writes=[rpS])
            SCM, rSCM = BFt[4], res("SCM")
            pg.op("dve", lambda e, P=P, psS=psS: e.tensor_tensor(out=SCM[0:P, 0:4 * P], in0=psS[0:P, 0:4 * P], in1=TRI4[0:P, 0:4 * P],
                                                                 op=ALU.mult), reads=[rpS, r_setup], writes=[rSCM])
            yield
            psO, rpO = aux()

            def om(e, s=s, P=P, psO=psO, NC=NC, L=L):
                last = None
                for h in range(4):
                    e.matmul(psO[0:P, h * 128:(h + 1) * 128], lhsT=SCM[0:P, h * P:(h + 1) * P], rhs=IBt[0:P, s, h * 128:(h + 1) * 128],
                             start=True, stop=False)
                    for c in range(NC):
                        if sample:
                            last = e.matmul(psO[0:P, h * 128:(h + 1) * 128], lhsT=QBM[:, h, c, :], rhs=SBt[:, h, c, :],
                                            start=False, stop=(c == NC - 1))
                        else:
                            last = e.matmul(psO[c * L:(c + 1) * L, h * 128:(h + 1) * 128], lhsT=QBT[:, h, c * L:(c + 1) * L],
                                            rhs=SBt[:, h, c, :], start=False, stop=True)
                return last
            pg.op("pe", om, reads=[rSCM, rIB, rSB, rQBT] + ([res("QBM")] if sample else []), writes=[rpO])
            rso = res("sso")
            for h in range(4):
                pg.op("act", lambda e, P=P, h=h, psO=psO: e.activation(out=JUNK[0:P, 0:128], in_=psO[0:P, h * 128:(h + 1) * 128],
                                                                        func=AF.Square, accum_out=SM[0:P, 16 + h:17 + h]),
                      reads=[rpO], writes=[res("junk")], upd=[rso])
            rso.writers = [pg.q["act"][-1]]
            rstd_from_ss(SM[0:P, 16:20], P, 4, 1.0 / 128, rso)
            YB, rYB = BFt[5], res("YB")
            for h in range(4):
                pg.op("dve", lambda e, P=P, h=h, psO=psO: e.scalar_tensor_tensor(
                    out=YB[0:P, h * 128:(h + 1) * 128], in0=psO[0:P, h * 128:(h + 1) * 128], scalar=SM[0:P, 16 + h:17 + h],
                    in1=GNt[0:P, h * 128:(h + 1) * 128], op0=ALU.mult, op1=ALU.mult),
                    reads=[rpO, rso, res("GN")], writes=[rYB] if h == 0 else [], upd=[] if h == 0 else [rYB])
            yield
            y_to_T(YB, rYB, s, P, KT // 2 + 4 * Q)
            yield

    out_dmas = []
    r_hidt = [Res(), Res()]
    last_barrier_dma = {"sp": 0, "pool": 0, "act": 0}

    def barrier():
        lasts = []
        for e in ENGS:
            for o in reversed(pg.q[e]):
                if not o.dma and o.fn is not None and getattr(o, "real", True):
                    lasts.append(o)
                    break
        dmas = []
        for e in ("sp",):
            dmas.extend(pg.dma_ops[e][last_barrier_dma[e]:])
            last_barrier_dma[e] = len(pg.dma_ops[e])
        rb = Res()
        rb.writers = lasts + dmas
        for e in ("pe", "act", "dve", "sp"):
            pg.op(e, None, reads=[rb])

    def load_bc(dst, src1d, n, name):
        return dma("sp", dst[:, 0:n], src1d.partition_broadcast(128), writes=[res(name)])

    def act_evac(func, dst_fn, rname):
        def ev(s, P, ps, rps):
            out = dst_fn(s, P)
            pg.op("act", lambda e: e.activation(out=out, in_=ps, func=func), reads=[rps], writes=[res(rname % s)])
        return ev

    def do_tile(kind, xsrc, tok0, subt, ydst):
        sample = kind == "sample"
        pre = kind == "pre"
        def ldx(i):
            s, P = subt[i]
            dma("sp", XIN[s % 2][0:P, :], xsrc[tok0 + s * 128: tok0 + s * 128 + P, :], writes=[r_xin[s % 2]])
        for i in range(min(2, len(subt))):
            ldx(i)
        norm_stats(subt[0][0], XIN[subt[0][0] % 2][0:subt[0][1], :], subt[0][1], r_xin[subt[0][0] % 2])
        for i, (s, P) in enumerate(subt):
            if i + 1 < len(subt):
                s1, P1 = subt[i + 1]
                norm_stats(s1, XIN[s1 % 2][0:P1, :], P1, r_xin[s1 % 2])
            norm_apply(s, XIN[s % 2][0:P, :], P, r_xin[s % 2], GC1, HT, r_ht)
            if i + 2 < len(subt):
                ldx(i + 2)
        barrier()
        ck("s0")
        if not pre:
            load_bc(NVO, norm_v, WA, "NVO")
            for jp in range(cfg.NA):
                def ev_flush(inner):
                    def ev(s, P, ps, rps):
                        flush()
                        inner(s, P, ps, rps)
                    return ev
                vb = 1 + jp % 2
                gemm(HT, r_ht, KT, w_in, 0, WA + jp * 512, subt,
                     act_evac(AF.Gelu_apprx_tanh, lambda s, P, vb=vb: RAW[vb][0:P, s, :], "raw%d_%%d" % vb), wname="w_in")
                gemm(HT, r_ht, KT, w_in, 0, jp * 512, subt,
                     ev_flush(act_evac(AF.Gelu_apprx_tanh, lambda s, P: RAW[0][0:P, s, :], "raw0_%d")), wname="w_in")
                drive(mixer_A(jp, subt, sample))
            deferred.append(lambda: load_bc(NVO, norm_o, WB, "NVO"))
        for Q in range(cfg.NQ):
            def ev_flush(inner):
                def ev(s, P, ps, rps):
                    flush()
                    inner(s, P, ps, rps)
                return ev
            c_q, c_f, c_i, c_g = 2 * WA + Q * 512, 2 * WA + WB + Q * 512, 2 * WA + 2 * WB + Q * 512, 2 * WA + 3 * WB + Q * 512
            ibb = (IBt, IBt2)[Q % 2]
            gemm(HT, r_ht, KT, w_in, 0, c_i, subt, act_evac(AF.Copy, lambda s, P, ibb=ibb: ibb[0:P, s, :], "ib%d_%%d" % (Q % 2)), wname="w_in")
            gemm(HT, r_ht, KT, w_in, 0, c_f, subt, ev_flush(act_evac(AF.Sigmoid, lambda s, P: RAW[0][0:P, s, :], "raw0_%d")), wname="w_in")
            if not pre:
                gemm(HT, r_ht, KT, w_in, 0, c_q, subt, ev_flush(act_evac(AF.Silu, lambda s, P: RAW[1][0:P, s, :], "raw1_%d")), wname="w_in")
                gemm(HT, r_ht, KT, w_in, 0, c_g, subt, ev_flush(act_evac(AF.Silu, lambda s, P: RAW[2][0:P, s, :], "raw2_%d")), wname="w_in")
            drive(mixer_B(Q, subt, sample, pre))
        flush()
        barrier()
        ck("s1" + kind)
        if pre:
            return
        r_x = [res("x1_%d" % s) for s in range(S)]
        for (s, P) in subt:
            dma("sp", X1[0:P, s, :], xsrc[tok0 + s * 128: tok0 + s * 128 + P, :], writes=[r_x[s]])

        def add_evac(cb):
            def ev(s, P, ps, rps):
                pg.op("dve", lambda e: e.tensor_tensor(out=X1[0:P, s, cb * 512:(cb + 1) * 512], in0=ps,
                                                       in1=X1[0:P, s, cb * 512:(cb + 1) * 512], op=ALU.add),
                      reads=[rps, r_x[s]], writes=[r_x[s]])
            return ev
        for cb in range(D // 512):
            gemm(YT, r_yt, KT, w_out, 0, cb * 512, subt, add_evac(cb), wname="w_out")
        for (s, P) in subt:
            norm_stats(s, X1[0:P, s, :], P, r_x[s])
        for (s, P) in subt:
            norm_apply(s, X1[0:P, s, :], P, r_x[s], GC2, HT, r_ht)
        ck("s2")
        NCH = DFF // cfg.FC
        KF = cfg.FC // 128
        for j in range(NCH):
            hb = HIDT[j % 2]
            rh = r_hidt[j % 2]
            wl = {}
            pg.op("dve", None, writes=[rh])
            rh.writers = []
            for cb in range(cfg.FC // 512):
                def evm(m, ncol, ps, rps, cb=cb, hb=hb, rh=rh):
                    rrt = res("RT")
                    pg.op("act", lambda e: e.activation(out=RT[:, 0:ncol], in_=ps, func=AF.Relu), reads=[rps], writes=[rrt])
                    o = pg.op("dve", lambda e: e.tensor_tensor(out=hb[:, cb * 4 + m, 0:ncol], in0=RT[:, 0:ncol], in1=RT[:, 0:ncol],
                                                               op=ALU.mult), reads=[rrt])
                    rh.writers = [o]
                gemm_fm(HT, r_ht, KT, w_up, 0, j * cfg.FC + cb * 512, subt, evm, wname="w_up")
            flush()
            r_h4 = [rh] * S
            for cb in range(D // 512):
                gemm(hb, r_h4, KF, w_down, j * cfg.FC, cb * 512, subt, add_evac(cb), wname="w_down")
        barrier()
        ck("s3")
        load_bc(NFBt, norm_f, D, "NFB")
        for (s, P) in subt:
            ssap = SM[0:P, s:s + 1]
            rss = res("ss%d" % s)
            sumsq(X1[0:P, s, :], P, ssap, r_x[s], rss)
            pg.op("dve", lambda e, s=s, P=P, ssap=ssap: e.scalar_tensor_tensor(out=X1[0:P, s, :], in0=X1[0:P, s, :], scalar=ssap,
                                                                                in1=NFBt[0:P, :], op0=ALU.mult, op1=ALU.mult),
                  reads=[rss, res("NFB"), r_x[s]], writes=[r_x[s]])
            dma("sp", ydst[tok0 + s * 128: tok0 + s * 128 + P, :], X1[0:P, s, :], reads=[r_x[s]])
            out_dmas.append(pg.q["sp"][-1])
        barrier()

    full = [(s, 128) for s in range(S)]
    barrier()
    try:
        ck("setup")
        for t in range(NP // T):
            do_tile("pre", xprev, t * T, full, None)
            ck("pre%d" % t)
        for t in range(NP // T):
            do_tile("prompt", xp, t * T, full, yp)
            ck("prompt%d" % t)
        dma("sp", spo.rearrange("h d v -> d h v"), S32[:, 0:HB, :], reads=[res("S32")])
        out_dmas.append(pg.q["sp"][-1])
        res("S32").readers.append(pg.q["sp"][-1])
        barrier()
        do_tile("sample", xs, 0, [(0, 64)], ys)
    except _Stop:
        del deferred[:]
        barrier()
    rfin = Res()
    rfin.writers = list(out_dmas)
    pg.op("sp", None, reads=[rfin])
    return nc, pg


_CACHE = {}


def get_program(cfg_key, cfg):
    if cfg_key not in _CACHE:
        nc, pg = build(cfg)
        es = ExitStack()
        pg.emit(nc, es)
        es.close()
        _CACHE[cfg_key] = nc
    return _CACHE[cfg_key]


def run(cfg, x_prompt, x_sample, state_hgrn, norm1, w_in, w_s, b_s, norm_v, lb_logits, norm_o,
        w_out, norm2, w_up, w_down, norm_f, trace=False):
    f = lambda a: np.ascontiguousarray(np.asarray(a, dtype=np.float32))
    x_prompt, x_sample, state_hgrn = f(x_prompt), f(x_sample), f(state_hgrn)
    D, NP, NSTR, LS, HB, WA = cfg.D, cfg.NP, cfg.NSTR, cfg.LS, cfg.HB, cfg.WA
    n = cfg.n_cores
    B = x_prompt.shape[0]
    assert n == 2 * B and x_prompt.shape[1] == 2 * NP
    shared = {
        "w_in": f(w_in[0]), "w_out": f(w_out[0]), "w_up": f(w_up[0]), "w_down": f(w_down[0]),
        "norm1": f(norm1[0]), "norm2": f(norm2[0]), "norm_f": f(norm_f), "norm_v": f(norm_v[0]).reshape(-1),
        "norm_o": f(norm_o[0]).reshape(-1), "lb_logits": f(lb_logits), "w_s": f(w_s[0]), "b_s": f(b_s[0]),
    }
    for k, v in make_consts().items():
        shared["c_" + k] = v
    zeros = np.zeros((NP, D), np.float32)
    in_maps = []
    for c in range(n):
        b, half = c // 2, c % 2
        m = dict(shared)
        m["xp"] = x_prompt[b, half * NP:(half + 1) * NP]
        m["xprev"] = x_prompt[b, 0:NP] if half == 1 else zeros
        m["xs"] = x_sample[c * NSTR:(c + 1) * NSTR].reshape(NSTR * LS, D)
        m["s0"] = state_hgrn[0, c * NSTR:(c + 1) * NSTR]
        in_maps.append(m)
    nc = get_program((D, NP, n), cfg)
    r = run_bass_kernel_spmd(nc, in_maps, core_ids=list(range(n)), **({"trace": True} if trace else {}))
    outs = r.results
    y_prompt = np.zeros((B, 2 * NP, D), np.float32)
    y_sample = np.zeros((n * NSTR, LS, D), np.float32)
    st_p = np.zeros((1, B, HB, 128, 128), np.float32)
    st_s = np.zeros((1, n * NSTR, HB, 128, 128), np.float32)
    v_s = np.zeros((1, n * NSTR, LS, WA), np.float32)
    for c in range(n):
        b, half = c // 2, c % 2
        o = outs[c]
        y_prompt[b, half * NP:(half + 1) * NP] = o["yp"]
        y_sample[c * NSTR:(c + 1) * NSTR] = o["ys"].reshape(NSTR, LS, D)
        if half == 1:
            st_p[0, b] = o["spo"]
        st_s[0, c * NSTR:(c + 1) * NSTR] = o["sso"]
        v_s[0, c * NSTR:(c + 1) * NSTR] = o["vso"].reshape(NSTR, LS, WA)
    if trace:
        return (y_prompt, y_sample, st_p, st_s, v_s), r
    return (y_prompt, y_sample, st_p, st_s, v_s)


def kernel(**inputs):
    cfg = Cfg()
    return run(cfg, **inputs)
```

```python
import numpy as np
import ml_dtypes
from contextlib import ExitStack
import concourse.bass as bass
import concourse.mybir as mybir
from concourse.bass_utils import run_bass_kernel_spmd

F32 = mybir.dt.float32
BF16 = mybir.dt.bfloat16
AF = mybir.ActivationFunctionType
ALU = mybir.AluOpType
EPS = 1e-6


class Cfg:
    def __init__(self, D=4096, HA=8, HB=16, NP=2048, NSTR=4, LS=16, S=4, n_cores=8, WSLOTS=4):
        self.D = D
        self.HA = HA
        self.HB = HB
        self.WA = D // 2
        self.WB = D // 2
        self.HDA = self.WA // HA
        assert self.HDA == 256 and self.WB // HB == 128
        self.DFF = 4 * D
        self.KT = D // 128
        self.NIN = 2 * self.WA + 4 * self.WB
        self.NP = NP
        self.NSTR = NSTR
        self.LS = LS
        self.NS = NSTR * LS
        assert self.NS == 64
        self.S = S
        self.T = S * 128
        assert NP % self.T == 0
        self.n_cores = n_cores
        self.WSLOTS = WSLOTS
        self.KS = min(8, self.KT)
        self.FC = min(2048, self.DFF)
        self.NQ = self.WB // 512
        self.NA = self.WA // 512
        assert self.WA % 512 == 0 and self.WB % 512 == 0


class Op:
    __slots__ = ("eng", "fn", "deps", "waited", "semval", "dma", "dma_idx")

    def __init__(self, eng, fn, dma):
        self.eng = eng
        self.fn = fn
        self.dma = dma
        self.waited = False
        self.semval = 0
        self.deps = []
        self.dma_idx = -1


class Res:
    __slots__ = ("writers", "readers", "excl")

    def __init__(self, excl=False):
        self.writers = []
        self.readers = []
        self.excl = excl


NDMASEM = 6
STOP_AT = None


class _Stop(Exception):
    pass


def ck(name):
    if STOP_AT == name:
        raise _Stop()
ENGS = ("pe", "act", "dve", "pool", "sp")


class Prog:
    def __init__(self):
        self.q = {e: [] for e in ENGS}
        self.dma_ops = {"sp": [], "pool": [], "act": []}

    def op(self, eng, fn, reads=(), writes=(), upd=(), dma=False):
        o = Op(eng, fn, dma)
        deps = []
        for r in reads:
            deps.extend(r.writers)
            if r.excl:
                deps.extend(x for x in r.readers if x.eng != eng)
        for w in writes:
            deps.extend(w.writers)
            deps.extend(w.readers)
        if dma:
            lst = self.dma_ops[eng]
            if len(lst) >= NDMASEM:
                deps.append(lst[-NDMASEM])
            o.dma_idx = len(lst)
            lst.append(o)
        seen = set()
        i = 0
        while i < len(deps):
            d = deps[i]
            i += 1
            if id(d) in seen:
                continue
            seen.add(id(d))
            if d.fn is None:
                deps.extend(d.deps)
                continue
            if d.eng == "pe" and eng == "pe" and not d.dma:
                continue
            d.waited = True
            o.deps.append(d)
        for w in writes:
            w.writers = [o]
            w.readers = []
        for w in upd:
            w.writers = [o]
        for r in reads:
            r.readers = [x for x in r.readers if x.dma or x.eng != eng] + [o]
        self.q[eng].append(o)
        return o

    def emit(self, nc, es):
        sems = {e: es.enter_context(nc.semaphore("s_" + e)) for e in ENGS}
        dsems = {e: [es.enter_context(nc.semaphore("d_%s%d" % (e, i))) for i in range(NDMASEM)]
                 for e in self.dma_ops}
        for e in ENGS:
            c = 0
            for o in self.q[e]:
                if o.waited and not o.dma:
                    c += 1
                    o.semval = c
        block = es.enter_context(nc.Block())

        def run(e, eng):
            waited = {}
            for o in self.q[e]:
                for d in o.deps:
                    if d.dma:
                        sem = dsems[d.eng][d.dma_idx % NDMASEM]
                        val = 16 * (d.dma_idx // NDMASEM + 1)
                    else:
                        sem = sems[d.eng]
                        val = d.semval
                    key = id(sem)
                    if waited.get(key, 0) >= val:
                        continue
                    waited[key] = val
                    eng.wait_ge(sem, val)
                if o.fn is None:
                    continue
                inst = o.fn(eng)
                if o.dma:
                    inst.then_inc(dsems[e][o.dma_idx % NDMASEM], 16)
                elif o.waited:
                    inst.then_inc(sems[e], 1)

        @block.tensor
        def _(eng):
            run("pe", eng)

        @block.scalar
        def _(eng):
            run("act", eng)

        @block.vector
        def _(eng):
            run("dve", eng)

        @block.gpsimd
        def _(eng):
            run("pool", eng)

        @block.sync
        def _(eng):
            run("sp", eng)


def make_consts():
    c = {}
    c["ident_f"] = np.eye(128, dtype=np.float32)
    c["ident_b"] = np.eye(128).astype(ml_dtypes.bfloat16)
    for tag, P, L in (("p", 128, 64), ("s", 64, 16)):
        s = np.arange(P)[:, None]
        t = np.arange(P)[None, :]
        same = (s // L) == (t // L)
        tri = same & (s <= t)
        ref = (t // L) * L + L // 2
        refm = same & (s <= ref)
        endm = same
        c["tri_" + tag] = tri.astype(np.float32)
        c["d1m_" + tag] = tri.astype(np.float32) - refm.astype(np.float32)
        c["d2m_" + tag] = endm.astype(np.float32) - tri.astype(np.float32)
        nc_ = P // L
        chk = (np.arange(P)[:, None] // L) == np.arange(nc_)[None, :]
        c["chk_" + tag] = chk.astype(np.float32)
        c["tri4_" + tag] = np.tile(tri.astype(np.float32), (1, 4)).astype(ml_dtypes.bfloat16)
    ms = (np.arange(64)[None, :] // 16) == np.arange(4)[:, None]
    c["ms"] = np.broadcast_to(ms[None].astype(np.float32), (128, 4, 64)).astype(ml_dtypes.bfloat16).copy()
    j = np.arange(128)[:, None]
    i = np.arange(128)[None, :]
    c["maskg"] = (~((i < 64) & (j >= 64))).astype(np.float32)
    return c


def build(cfg):
    nc = bass.Bass("TRN2", target_bir_lowering=False)
    D, KT, S, T = cfg.D, cfg.KT, cfg.S, cfg.T
    WA, WB, HA, HB, DFF = cfg.WA, cfg.WB, cfg.HA, cfg.HB, cfg.DFF
    NP, NS, NSTR, LS = cfg.NP, cfg.NS, cfg.NSTR, cfg.LS
    KS = cfg.KS

    def din(name, shape, dt=F32):
        return nc.dram_tensor(name, list(shape), dt, kind="ExternalInput").ap()

    def dout(name, shape, dt=F32):
        return nc.dram_tensor(name, list(shape), dt, kind="ExternalOutput").ap()

    xp = din("xp", [NP, D])
    xprev = din("xprev", [NP, D])
    xs = din("xs", [NS, D])
    s0 = din("s0", [NSTR, HB, 128, 128])
    w_in = din("w_in", [D, cfg.NIN])
    w_out = din("w_out", [D, D])
    w_up = din("w_up", [D, DFF])
    w_down = din("w_down", [DFF, D])
    norm1 = din("norm1", [D])
    norm2 = din("norm2", [D])
    norm_f = din("norm_f", [D])
    norm_v = din("norm_v", [WA])
    norm_o = din("norm_o", [WB])
    lbl = din("lb_logits", [2, WB])
    w_s = din("w_s", [HA, 128, 128])
    b_s = din("b_s", [HA, 128])
    cst = {}
    for k, v in make_consts().items():
        cst[k] = din("c_" + k, v.shape, BF16 if v.dtype == ml_dtypes.bfloat16 else F32)

    yp = dout("yp", [NP, D])
    ys = dout("ys", [NS, D])
    spo = dout("spo", [HB, 128, 128])
    sso = dout("sso", [NSTR, HB, 128, 128])
    vso = dout("vso", [NS, WA])

    NSLAB = (D * cfg.NIN + D * D + 2 * D * DFF) // (KS * 128 * 512)
    WSC_CH = 64
    wsc_t = [nc.dram_tensor("wscratch%d" % i, [WSC_CH, 128, KS * 512], BF16, kind="Internal").ap()
             for i in range((NSLAB + WSC_CH - 1) // WSC_CH)]

    class _Wsc:
        def __getitem__(self, key):
            idx = key[0]
            return wsc_t[idx // WSC_CH][(idx % WSC_CH,) + tuple(key[1:])]
    wsc = _Wsc()
    wcache = {}
    r_wsc = {}

    base = [nc._sbuf_addr_for_side(None)]
    base[0] = (base[0] + 63) // 64 * 64
    limit = base[0] + nc.sbuf_bytes_remaining - 64

    def salloc(name, shape, dt, at=None):
        nbytes = int(np.prod(shape[1:])) * (2 if dt == BF16 else 4)
        nbytes = (nbytes + 63) // 64 * 64
        if at is None:
            off = base[0]
            base[0] += nbytes
            assert base[0] <= limit, ("SBUF overflow", name, base[0], limit)
        else:
            off = at
        return nc.alloc_sbuf_tensor_at(name, list(shape), dt, offset=off), off, nbytes

    X1, X1off, X1bytes = salloc("X1", [128, S, D], F32)
    if X1bytes < 65536:
        _, _, padb = salloc("X1pad", [128, (65536 - X1bytes) // 4], F32)
        X1bytes += padb
    HT, _, _ = salloc("HT", [128, KT, T], BF16)
    YT, YToff, YTbytes = salloc("YT", [128, KT, T], BF16)
    need_yt = max(2 * (cfg.FC // 128) * T * 2, D * 4)
    if YTbytes < need_yt:
        _, _, padb = salloc("YTpad", [128, (need_yt - YTbytes) // 4], F32)
        YTbytes += padb
    WR = [salloc("WR%d" % i, [128, KS, 512], BF16)[0] for i in range(cfg.WSLOTS)]
    S32, _, _ = salloc("S32", [128, max(HB, 16), 128], F32)
    LB, _, _ = salloc("LB", [128, WB], F32)
    OML, _, _ = salloc("OML", [128, WB], F32)
    ident_f, _, _ = salloc("ident_f", [128, 128], F32)
    ident_b, _, _ = salloc("ident_b", [128, 128], BF16)
    C = {}
    for tag, P in (("p", 128), ("s", 64)):
        for nm in ("tri", "d1m", "d2m"):
            C[nm + "_" + tag], _, _ = salloc(nm + "_" + tag, [P, P], F32)
        C["chk_" + tag], _, _ = salloc("chk_" + tag, [P, P // (64 if tag == "p" else 16)], F32)
        C["tri4_" + tag], _, _ = salloc("tri4_" + tag, [P, 4 * P], BF16)
    MS, _, _ = salloc("MS", [128, 4, 64], BF16)
    WT, _, _ = salloc("WT", [128, HA, 128], BF16)
    WTS, _, _ = salloc("WTS", [64, HA, 64], BF16)
    BSC, _, _ = salloc("BSC", [128, HA], F32)
    BSS, _, _ = salloc("BSS", [64, HA], F32)
    GC1, _, _ = salloc("GC1", [128, KT], F32)
    GC2, _, _ = salloc("GC2", [128, KT], F32)
    SM, _, _ = salloc("SM", [128, 64], F32)
    EPSC, _, _ = salloc("EPSC", [128, 1], F32)

    ov = [X1off]

    ooff = {}

    def oalloc(name, shape, dt):
        t, off, nb = salloc(name, shape, dt, at=ov[0])
        ooff[name] = off
        ov[0] += nb
        assert ov[0] <= X1off + X1bytes, ("overlay overflow", name)
        return t

    NVO = oalloc("NVO", [128, max(WA, WB)], F32)
    RAW = [oalloc("RAW%d" % i, [128, S, 512], F32) for i in range(3)]
    IBt = oalloc("IB", [128, S, 512], BF16)
    IBt2 = oalloc("IB2", [128, S, 512], BF16)
    Ft = oalloc("Ft", [128, 512], F32)
    KKt = oalloc("KKt", [128, 512], F32)
    Et = [oalloc("Et%d" % i, [128, 512], F32) for i in range(2)]
    GNt = oalloc("GNt", [128, 512], F32)
    BFt = [oalloc("BFt%d" % i, [128, 512], BF16) for i in range(6)]
    TTt = [oalloc("TTt%d" % i, [128, 4, 128], BF16) for i in range(3)]
    SBt = oalloc("SBt", [128, 4, 4, 128], BF16)
    DECt = oalloc("DECt", [128, 16], F32)
    if S >= 4:
        raw2_off = X1off + (max(WA, WB) * 4 + 63) // 64 * 64 + 2 * (S * 512 * 4)
        KEM = nc.alloc_sbuf_tensor_at("KEM", [64, 4, 512], BF16, offset=raw2_off + 2048)
        QBM = nc.alloc_sbuf_tensor_at("QBM", [128, 4, 4, 64], BF16, offset=raw2_off + 2048 + 4096)
    else:
        KEM = oalloc("KEM", [64, 4, 512], BF16)
        QBM = oalloc("QBM", [128, 4, 4, 64], BF16)
    stage1_end = ov[0]
    assert S * 512 * 4 <= 2048 + 6 * 1024
    SGP = nc.alloc_sbuf_tensor_at("SGP", [128, S, 512], F32, offset=ooff["GNt"])
    KEP = nc.alloc_sbuf_tensor_at("KEP", [128, 512], BF16, offset=ooff["TTt0"])
    ov[0] = X1off
    XIN = [oalloc("XIN%d" % i, [128, D], F32) for i in range(2)]
    HIDT = [nc.alloc_sbuf_tensor_at("HIDT%d" % i, [128, cfg.FC // 128, T], BF16,
                                    offset=YToff + i * (cfg.FC // 128) * T * 2) for i in range(2)]
    assert 2 * (cfg.FC // 128) * T * 2 <= YTbytes
    JUNK, _, _ = salloc("JUNK", [128, 512], BF16)
    XSCP = [salloc("XSCP%d" % i, [128, 512], F32)[0] for i in range(2)]
    RT, _, _ = salloc("RT", [128, 512], F32)
    HTM = [salloc("HTM%d" % i, [128, 512], BF16)[0] for i in range(2)]
    NFB = None
    NFBt = nc.alloc_sbuf_tensor_at("NFB", [128, D], F32, offset=YToff)
    assert D * 4 <= YTbytes

    GPS = [nc.alloc_psum_tensor("gps%d" % i, [128, 512], F32) for i in range(4)]
    AUX = [nc.alloc_psum_tensor("aux%d" % i, [128, 512], F32) for i in range(4)]

    pg = Prog()
    R = {}

    def res(name):
        if name not in R:
            R[name] = Res()
        return R[name]

    r_gps = [Res(excl=True) for _ in range(4)]
    r_aux = [Res(excl=True) for _ in range(4)]
    r_wr = [Res() for _ in range(cfg.WSLOTS)]
    r_x1reg = Res()
    aux_i = [0]

    def aux():
        i = aux_i[0] % 4
        aux_i[0] += 1
        return AUX[i], r_aux[i]

    wr_i = [0]

    flip = [0]

    def evac_eng():
        flip[0] ^= 1
        return "act" if flip[0] else "dve"

    def copy_op(eng, out, in_, reads, writes, upd=()):
        if eng == "act":
            return pg.op("act", lambda e: e.activation(out=out, in_=in_, func=AF.Copy), reads, writes, upd)
        return pg.op("dve", lambda e: e.tensor_copy(out=out, in_=in_), reads, writes, upd)

    deferred = []

    deferred_bg = []

    def pump():
        if deferred:
            deferred.pop(0)()
        if deferred_bg:
            deferred_bg.pop(0)()

    def flush_bg():
        while deferred_bg:
            deferred_bg.pop(0)()

    def drive_bg(gen):
        def step():
            try:
                next(gen)
                deferred_bg.insert(0, step)
            except StopIteration:
                pass
        deferred_bg.append(step)

    def flush():
        while deferred:
            deferred.pop(0)()

    def drive(gen):
        def step():
            try:
                next(gen)
                deferred.insert(0, step)
            except StopIteration:
                pass
        deferred.append(step)

    def dma(eng, out, in_, reads=(), writes=()):
        return pg.op(eng, lambda e: e.dma_start(out=out, in_=in_, allow_slow_non_contiguous=True), reads, writes, dma=True)

    r_const = res("const")
    setup_ops = []
    setup_ops.append(dma("sp", ident_f[:, :], cst["ident_f"][:, :]))
    setup_ops.append(dma("sp", ident_b[:, :], cst["ident_b"][:, :]))
    for tag in ("p", "s"):
        for nm in ("tri", "d1m", "d2m", "chk", "tri4"):
            k = nm + "_" + tag
            setup_ops.append(dma("sp", C[k][:, :], cst[k][:, :]))
    setup_ops.append(dma("sp", MS[:, :, :], cst["ms"][:, :, :]))
    with nc.allow_non_contiguous_dma(reason="tiny parameter loads"):
        setup_ops.append(dma("sp", GC1[:, :], norm1.rearrange("(kt p) -> p kt", p=128)))
        setup_ops.append(dma("sp", GC2[:, :], norm2.rearrange("(kt p) -> p kt", p=128)))
        setup_ops.append(dma("sp", BSC[:, :], b_s.rearrange("h i -> i h")))
        for st in range(NSTR):
            setup_ops.append(dma("sp", BSS[st * LS:(st + 1) * LS, :], b_s[:, 0:LS].rearrange("h i -> i h")))
    setup_ops.append(dma("sp", LB[:, :], lbl[0, :].partition_broadcast(128)))
    setup_ops.append(dma("sp", OML[:, :], lbl[1, :].partition_broadcast(128)))
    r_setup = Res()
    r_setup.writers = list(setup_ops)
    o1 = pg.op("dve", lambda e: e.tensor_tensor(out=LB[:, :], in0=LB[:, :], in1=OML[:, :], op=ALU.subtract),
               reads=[r_setup], writes=[res("LB")])
    pg.op("act", lambda e: e.activation(out=LB[:, :], in_=LB[:, :], func=AF.Sigmoid), reads=[], writes=[res("LB")])
    pg.op("dve", lambda e: e.tensor_scalar(out=OML[:, :], in0=LB[:, :], scalar1=-1.0, scalar2=1.0,
                                           op0=ALU.mult, op1=ALU.add), reads=[res("LB")], writes=[res("OML")])
    pg.op("dve", lambda e: e.memset(EPSC[:, :], EPS), writes=[res("EPSC")])
    pg.op("dve", lambda e: e.memset(S32[:, :, :], 0.0), writes=[res("S32")])

    r_stage0 = res("x1region_setup")
    WSTG = XIN[0]
    assert HA * 128 <= D
    MG = XIN[1]
    o_ws = dma("sp", WSTG[:, 0:HA * 128].rearrange("p (h j) -> p h j", j=128), w_s.rearrange("h i j -> i h j"),
               writes=[r_stage0])
    o_mg = dma("sp", MG[:, D - 128:D], cst["maskg"][:, :], reads=[r_stage0])
    r_mg = Res()
    r_mg.writers = [o_mg]
    for h in range(HA):
        ps, rps = aux()
        pg.op("pe", lambda e, h=h, ps=ps: e.transpose(out=ps[:, 0:128], in_=WSTG[:, h * 128:(h + 1) * 128],
                                                      identity=ident_f[:, :]),
              reads=[r_stage0, r_setup], writes=[rps])
        pg.op("dve", lambda e, h=h, ps=ps: e.tensor_tensor(out=WT[:, h, :], in0=ps[:, 0:128], in1=MG[:, D - 128:D],
                                                           op=ALU.mult),
              reads=[rps, r_mg], upd=[res("WT")])
    WSS = XIN[1]
    o_z = pg.op("dve", lambda e: e.memset(WSS[0:64, 0:HA * 64], 0.0), reads=[r_stage0], writes=[res("WSS")])
    with nc.allow_non_contiguous_dma(reason="tiny parameter loads"):
        dd = []
        for st in range(NSTR):
            dd.append(dma("sp", WSS[st * LS:(st + 1) * LS, 0:HA * 64].rearrange("p (h j) -> p h j", j=64)[:, :, st * LS:(st + 1) * LS],
                          w_s[:, 0:LS, 0:LS].rearrange("h i j -> i h j"), reads=[res("WSS")]))
    r_wss = Res()
    r_wss.writers = dd
    for h in range(HA):
        ps, rps = aux()
        pg.op("pe", lambda e, h=h, ps=ps: e.transpose(out=ps[0:64, 0:64], in_=WSS[0:64, h * 64:(h + 1) * 64],
                                                      identity=ident_f[0:64, 0:64]),
              reads=[r_wss, r_setup], writes=[rps])
        pg.op("dve", lambda e, h=h, ps=ps: e.tensor_copy(out=WTS[:, h, :], in_=ps[0:64, 0:64]),
              reads=[rps], upd=[res("WT")])
    r_x1 = res("x1coarse")

    def stage_barrier(tag):
        pass

    def rstd_from_ss(ssap, P, n, invn, rkey):
        pg.op("act", lambda e: e.activation(out=ssap, in_=ssap, func=AF.Ln, scale=invn, bias=EPSC[0:P, :]),
              reads=[rkey, res("EPSC")], writes=[rkey])
        pg.op("act", lambda e: e.activation(out=ssap, in_=ssap, func=AF.Exp, scale=-0.5), reads=[rkey], writes=[rkey])

    NPC = D // 512

    def sumsq(xap, P, ssap, rsrc, rss):
        rpart = res("sspart")
        ck("n0")
        for c in range(NPC):
            pg.op("act", lambda e, c=c: e.activation(out=JUNK[0:P, :], in_=xap[:, c * 512:(c + 1) * 512], func=AF.Square,
                                                     accum_out=SM[0:P, 32 + c:33 + c]),
                  reads=[rsrc], writes=[res("junk")], upd=[rpart])
        pg.op("dve", lambda e: e.tensor_reduce(out=ssap, in_=SM[0:P, 32:32 + NPC], axis=mybir.AxisListType.X, op=ALU.add),
              reads=[rpart], writes=[rss])
        rstd_from_ss(ssap, P, 1, 1.0 / D, rss)

    def norm_stats(s, xap, P, rsrc):
        sumsq(xap, P, SM[0:P, s:s + 1], rsrc, res("ss%d" % s))

    def norm_apply_gen(s, xap, P, rsrc, gcol, dstT, dst_res):
        ssap = SM[0:P, s:s + 1]
        rss = res("ss%d" % s)
        wl = []

        def scale(c):
            xb = XSCP[c % 2]
            rxs = res("xscp%d" % (c % 2))
            pg.op("pool", lambda e, c=c, xb=xb: e.tensor_scalar(out=xb[0:P, :], in0=xap[:, c * 512:(c + 1) * 512], scalar1=ssap,
                                                                scalar2=1.0, op0=ALU.mult, op1=ALU.mult),
                  reads=[rsrc, rss], writes=[rxs])

        def trev(c):
            xb = XSCP[c % 2]
            rxs = res("xscp%d" % (c % 2))
            ps, rps = aux()

            def tr(e, ps=ps, xb=xb):
                last = None
                for j in range(4):
                    last = e.transpose(out=ps[:, j * 128:j * 128 + P], in_=xb[0:P, j * 128:(j + 1) * 128],
                                       identity=ident_f[0:P, 0:P])
                return last
            pg.op("pe", tr, reads=[rxs, r_setup], writes=[rps])
            eng = evac_eng()
            for j in range(4):
                kt = c * 4 + j
                out = dstT[:, kt, s * 128:s * 128 + P]
                in_ = ps[:, j * 128:j * 128 + P]
                if eng == "act":
                    o = pg.op("act", lambda e, out=out, in_=in_, kt=kt: e.activation(out=out, in_=in_, func=AF.Identity,
                                                                                      scale=gcol[:, kt:kt + 1]),
                              reads=[rps, r_setup])
                else:
                    o = pg.op("dve", lambda e, out=out, in_=in_, kt=kt: e.tensor_scalar(out=out, in0=in_,
                                                                                        scalar1=gcol[:, kt:kt + 1],
                                                                                        scalar2=None, op0=ALU.mult),
                              reads=[rps, r_setup])
                wl[:] = [x for x in wl if x.eng != o.eng] + [o]
        for c in range(NPC):
            scale(c)
            if c > 0:
                trev(c - 1)
            yield
        trev(NPC - 1)
        dst_res[s].writers = list(wl)
        yield

    def norm_apply(s, xap, P, rsrc, gcol, dstT, dst_res):
        for _ in norm_apply_gen(s, xap, P, rsrc, gcol, dstT, dst_res):
            pass

    def load_slab(wname, wsrc, row0, sl, kk, col0):
        slot = wr_i[0] % cfg.WSLOTS
        wr_i[0] += 1
        wt = WR[slot]
        src = wsrc[row0 + sl * KS * 128: row0 + (sl * KS + kk) * 128, col0:col0 + 512].rearrange(
            "(kt p) n -> p kt n", p=128)
        key = (wname, row0 + sl * KS * 128, col0)
        if key in wcache:
            idx = wcache[key]
            pg.op("sp", lambda e, wt=wt, idx=idx, kk=kk: e.dma_start(
                out=wt[:, 0:kk, :], in_=wsc[idx, :, 0:kk * 512].rearrange("p (k n) -> p k n", n=512)),
                reads=[r_wsc[idx]], writes=[r_wr[slot]], dma=True)
        else:
            idx = len(wcache)
            wcache[key] = idx
            r_wsc[idx] = Res()
            pg.op("pool", lambda e, wt=wt, src=src, kk=kk: e.dma_start(out=wt[:, 0:kk, :], in_=src),
                  writes=[r_wr[slot]], dma=True)
            pg.op("sp", lambda e, wt=wt, idx=idx, kk=kk: e.dma_start(
                out=wsc[idx, :, 0:kk * 512].rearrange("p (k n) -> p k n", n=512), in_=wt[:, 0:kk, :]),
                reads=[r_wr[slot]], writes=[r_wsc[idx]], dma=True)
        return wt, slot

    def gemm_fm(actT, act_res, nk, wsrc, row0, col0, subt, evac_m, wname=None):
        ncol = sum(P for (_, P) in subt)
        nsl = (nk + KS - 1) // KS
        rd = [act_res[s] for (s, _) in subt]
        for sl in range(nsl):
            kk = min(KS, nk - sl * KS)
            wt, slot = load_slab(wname, wsrc, row0, sl, kk, col0)
            for m in range(4):
                def mm(e, m=m, sl=sl, kk=kk, wt=wt):
                    last = None
                    for j in range(kk):
                        kt = sl * KS + j
                        last = e.matmul(GPS[m][:, 0:ncol], lhsT=wt[:, j, m * 128:(m + 1) * 128], rhs=actT[:, kt, 0:ncol],
                                        start=(kt == 0), stop=(kt == nk - 1))
                    return last
                if sl == 0:
                    pg.op("pe", mm, reads=rd + [r_wr[slot]], writes=[r_gps[m]])
                else:
                    pg.op("pe", mm, reads=rd + [r_wr[slot]], upd=[r_gps[m]])
                pump()
                if sl == nsl - 1:
                    evac_m(m, ncol, GPS[m][:, 0:ncol], r_gps[m])

    def gemm(actT, act_res, nk, wsrc, row0, col0, subt, evac, do_pump=True, wname=None, npump=1):
        nsl = (nk + KS - 1) // KS
        for sl in range(nsl):
            kk = min(KS, nk - sl * KS)
            wt, slot = load_slab(wname, wsrc, row0, sl, kk, col0)
            for (s, P) in subt:
                def mm(e, s=s, P=P, sl=sl, kk=kk, wt=wt):
                    last = None
                    for j in range(kk):
                        kt = sl * KS + j
                        last = e.matmul(GPS[s][0:P, :], lhsT=actT[:, kt, s * 128:s * 128 + P], rhs=wt[:, j, :],
                                        start=(kt == 0), stop=(kt == nk - 1))
                    return last
                if sl == 0:
                    pg.op("pe", mm, reads=[act_res[s], r_wr[slot]], writes=[r_gps[s]])
                else:
                    pg.op("pe", mm, reads=[act_res[s], r_wr[slot]], upd=[r_gps[s]])
                if do_pump:
                    for _ in range(npump):
                        pump()
                if sl == nsl - 1:
                    evac(s, P, GPS[s][0:P, :], r_gps[s])

    r_ht = [Res() for _ in range(S)]
    r_yt = [Res() for _ in range(S)]
    r_xin = [Res(), Res()]

    def mixer_A(jp, subt, sample):
        vb = 1 + jp % 2
        U, V = RAW[0], RAW[vb]
        for (s, P) in subt:
            rU, rV = res("raw0_%d" % s), res("raw%d_%d" % (vb, s))
            rss = res("ssv")
            for hh in range(2):
                pg.op("act", lambda e, s=s, P=P, hh=hh: e.activation(out=JUNK[0:P, 0:256], in_=V[0:P, s, hh * 256:(hh + 1) * 256],
                                                                      func=AF.Square, accum_out=SM[0:P, 8 + hh:9 + hh]),
                      reads=[rV], writes=[res("junk")], upd=[rss])
            rss.writers = [pg.q["act"][-1]]
            rstd_from_ss(SM[0:P, 8:10], P, 2, 1.0 / 256, rss)
            VN = BFt[5]
            rvn = res("vn")
            first = True
            for hh in range(2):
                c0 = hh * 256
                if sample:
                    pg.op("dve", lambda e, s=s, P=P, hh=hh, c0=c0: e.scalar_tensor_tensor(
                        out=Et[0][0:P, c0:c0 + 256], in0=V[0:P, s, c0:c0 + 256], scalar=SM[0:P, 8 + hh:9 + hh],
                        in1=NVO[0:P, jp * 512 + c0: jp * 512 + c0 + 256], op0=ALU.mult, op1=ALU.mult),
                        reads=[rV, rss, res("NVO")], writes=[res("E0")] if first else [], upd=[] if first else [res("E0")])
                    pg.op("dve", lambda e, P=P, c0=c0: e.tensor_copy(out=VN[0:P, c0:c0 + 256], in_=Et[0][0:P, c0:c0 + 256]),
                          reads=[res("E0")], writes=[rvn] if first else [], upd=[] if first else [rvn])
                else:
                    pg.op("dve", lambda e, s=s, P=P, hh=hh, c0=c0: e.scalar_tensor_tensor(
                        out=VN[0:P, c0:c0 + 256], in0=V[0:P, s, c0:c0 + 256], scalar=SM[0:P, 8 + hh:9 + hh],
                        in1=NVO[0:P, jp * 512 + c0: jp * 512 + c0 + 256], op0=ALU.mult, op1=ALU.mult),
                        reads=[rV, rss, res("NVO")], writes=[rvn] if first else [], upd=[] if first else [rvn])
                first = False
            if sample:
                dma("sp", vso[0:P, jp * 512:(jp + 1) * 512], Et[0][0:P, :], reads=[res("E0")])
                out_dmas.append(pg.q["sp"][-1])
            yield
            ps, rps = aux()

            def mix(e, P=P, ps=ps):
                last = None
                for hh in range(2):
                    hg = 2 * jp + hh
                    lhsT = WTS[0:P, hg, :] if sample else WT[:, hg, :]
                    last = e.matmul(ps[0:P, hh * 256:(hh + 1) * 256], lhsT=lhsT, rhs=VN[0:P, hh * 256:(hh + 1) * 256],
                                    start=True, stop=True)
                return last
            pg.op("pe", mix, reads=[rvn, res("WT")], writes=[rps])
            YA = BFt[4]
            rya = res("ya")
            for hh in range(2):
                hg = 2 * jp + hh
                bcol = BSS[0:P, hg:hg + 1] if sample else BSC[:, hg:hg + 1]
                pg.op("dve", lambda e, s=s, P=P, hh=hh, ps=ps, bcol=bcol: e.scalar_tensor_tensor(
                    out=YA[0:P, hh * 256:(hh + 1) * 256], in0=ps[0:P, hh * 256:(hh + 1) * 256], scalar=bcol,
                    in1=U[0:P, s, hh * 256:(hh + 1) * 256], op0=ALU.add, op1=ALU.mult),
                    reads=[rps, rU, r_setup], writes=[rya] if hh == 0 else [], upd=[] if hh == 0 else [rya])
            yield
            y_to_T(YA, rya, s, P, 4 * jp)
            yield

    def y_to_T(Y, ry, s, P, kt0):
        ps, rps = aux()
        psb = ps[:, :].bitcast(BF16)

        def tr(e, P=P, psb=psb):
            last = None
            for j in range(4):
                last = e.transpose(out=psb[:, j * 128:j * 128 + P], in_=Y[0:P, j * 128:(j + 1) * 128],
                                   identity=ident_b[0:P, 0:P])
            return last
        pg.op("pe", tr, reads=[ry, r_setup], writes=[rps])
        eng = evac_eng()
        out = YT[:, kt0:kt0 + 4, s * 128:s * 128 + P]
        in_ = psb[:, 0:512].rearrange("p (j t) -> p j t", t=128)[:, :, 0:P]
        o = copy_op(eng, out, in_, [rps], [], upd=[])
        r_yt[s].writers = [x for x in r_yt[s].writers if x.eng != o.eng] + [o]

    IBt_a = IBt

    def mixer_B(Q, subt, sample, pre):
        SG, QS, GS = (SGP if pre else RAW[0]), RAW[1], RAW[2]
        IBt = (IBt_a, IBt2)[Q % 2]
        tag = "s" if sample else "p"
        TRI, D1M, D2M, CHK, TRI4 = C["tri_" + tag], C["d1m_" + tag], C["d2m_" + tag], C["chk_" + tag], C["tri4_" + tag]
        for (s, P) in subt:
            L = LS if sample else 64
            NC = P // L
            rSG, rQS, rGS, rIB = res("raw0_%d" % s), res("raw1_%d" % s), res("raw2_%d" % s), res("ib%d_%d" % (Q % 2, s))
            rF, rK = res("F"), res("KK")
            cq = slice(Q * 512, (Q + 1) * 512)
            pg.op("dve", lambda e, s=s, P=P: e.tensor_tensor(out=Ft[0:P, :], in0=SG[0:P, s, :], in1=OML[0:P, cq], op=ALU.mult),
                  reads=[rSG, res("OML")], writes=[rF])
            pg.op("dve", lambda e, P=P: e.tensor_tensor(out=Ft[0:P, :], in0=Ft[0:P, :], in1=LB[0:P, cq], op=ALU.add),
                  reads=[rF, res("LB")], writes=[rF])
            pg.op("dve", lambda e, P=P: e.tensor_scalar(out=KKt[0:P, :], in0=Ft[0:P, :], scalar1=-1.0, scalar2=1.0,
                                                        op0=ALU.mult, op1=ALU.add), reads=[rF], writes=[rK])
            pg.op("act", lambda e, P=P: e.activation(out=Ft[0:P, :], in_=Ft[0:P, :], func=AF.Ln), reads=[rF, rK], writes=[rF])
            if not pre:
                rGN = res("GN")
                pg.op("dve", lambda e, s=s, P=P: e.tensor_tensor(out=GNt[0:P, :], in0=GS[0:P, s, :], in1=NVO[0:P, cq], op=ALU.mult),
                      reads=[rGS, res("NVO")], writes=[rGN])
            yield
            def cum(M, P=P):
                ps, rps = aux()
                pg.op("pe", lambda e, ps=ps: e.matmul(ps[0:P, :], lhsT=M[0:P, 0:P], rhs=Ft[0:P, :], start=True, stop=True),
                      reads=[rF, r_setup], writes=[rps])
                return ps, rps
            psD2, rD2 = cum(D2M)
            psDC, rDC = aux()

            def dec(e, P=P, psDC=psDC, NC=NC):
                last = None
                for h in range(4):
                    last = e.matmul(psDC[:, h * NC:(h + 1) * NC], lhsT=Ft[0:P, h * 128:(h + 1) * 128], rhs=CHK[0:P, 0:NC],
                                    start=True, stop=True)
                return last
            pg.op("pe", dec, reads=[rF, r_setup], writes=[rDC])
            rDEC = res("DEC")
            pg.op("act", lambda e, psDC=psDC, NC=NC: e.activation(out=DECt[:, 0:4 * NC], in_=psDC[:, 0:4 * NC], func=AF.Exp),
                  reads=[rDC], writes=[rDEC])
            KE, rKE = (KEP if pre else BFt[3]), res("KE")
            rE1 = res("E1")
            pg.op("act", lambda e, P=P, psD2=psD2: e.activation(out=Et[1][0:P, :], in_=psD2[0:P, :], func=AF.Exp),
                  reads=[rD2], writes=[rE1])
            pg.op("dve", lambda e, P=P: e.tensor_tensor(out=KE[0:P, :], in0=KKt[0:P, :], in1=Et[1][0:P, :], op=ALU.mult),
                  reads=[rK, rE1], writes=[rKE])
            if not pre:
                psB, rB = cum(TRI)
                psD1, rD1 = cum(D1M)
                QR, KR, QB = BFt[0], BFt[1], BFt[2]
                rQR, rKR, rQB = res("QR"), res("KR"), res("QB")
                rE0 = res("E0")
                pg.op("act", lambda e, P=P, psD1=psD1: e.activation(out=Et[0][0:P, :], in_=psD1[0:P, :], func=AF.Exp),
                      reads=[rD1], writes=[rE0])
                pg.op("dve", lambda e, s=s, P=P: e.tensor_tensor(out=QR[0:P, :], in0=QS[0:P, s, :], in1=Et[0][0:P, :], op=ALU.mult),
                      reads=[rQS, rE0], writes=[rQR])
                pg.op("act", lambda e, P=P, psD1=psD1: e.activation(out=Et[1][0:P, :], in_=psD1[0:P, :], func=AF.Exp, scale=-1.0),
                      reads=[rD1, rKE], writes=[rE1])
                pg.op("dve", lambda e, P=P: e.tensor_tensor(out=KR[0:P, :], in0=KKt[0:P, :], in1=Et[1][0:P, :], op=ALU.mult),
                      reads=[rK, rE1], writes=[rKR])
                pg.op("act", lambda e, P=P, psB=psB: e.activation(out=Et[0][0:P, :], in_=psB[0:P, :], func=AF.Exp),
                      reads=[rB, rQR], writes=[rE0])
                pg.op("dve", lambda e, s=s, P=P: e.tensor_tensor(out=QB[0:P, :], in0=QS[0:P, s, :], in1=Et[0][0:P, :], op=ALU.mult),
                      reads=[rQS, rE0], writes=[rQB])
            yield
            rS = res("S32")
            rSB = res("SB")
            hq = slice(4 * Q, 4 * Q + 4)
            if sample:
                dd = []
                for st in range(NSTR):
                    dd.append(dma("sp", S32[:, st * 4:(st + 1) * 4, :], s0[st, 4 * Q:4 * Q + 4, :, :].rearrange("h d v -> d h v"),
                                  writes=[rS] if st == 0 else [], reads=[] if st == 0 else []))
                rS.writers = dd
                if not pre:
                    pg.op("act", lambda e: e.activation(out=SBt[:, :, :, :].rearrange("d h c v -> d c h v"),
                                                        in_=S32[:, 0:16, :].rearrange("d (c h) v -> d c h v", h=4), func=AF.Copy),
                          reads=[rS], writes=[rSB])
                rKEM = res("KEM")
                for st in range(NSTR):
                    pg.op("dve", lambda e, st=st: e.tensor_scalar(out=KEM[:, st, :], in0=KE[0:64, :], scalar1=CHK[0:64, st:st + 1],
                                                                  scalar2=None, op0=ALU.mult),
                          reads=[rKE, r_setup], writes=[rKEM] if st == 0 else [], upd=[] if st == 0 else [rKEM])
            else:
                if not pre:
                    pg.op("act", lambda e: e.activation(out=SBt[:, :, 0, :], in_=S32[:, hq, :], func=AF.Copy),
                          reads=[rS], writes=[rSB])
            psDS = []
            for c in range(NC):
                ps, rps = aux()
                psDS.append((ps, rps))

                def dsm(e, c=c, ps=ps, L=L, s=s):
                    last = None
                    for h in range(4):
                        if sample:
                            lhsT = KEM[:, c, h * 128:(h + 1) * 128]
                            rhs = IBt[0:64, s, h * 128:(h + 1) * 128]
                        else:
                            lhsT = KE[c * L:(c + 1) * L, h * 128:(h + 1) * 128]
                            rhs = IBt[c * L:(c + 1) * L, s, h * 128:(h + 1) * 128]
                        last = e.matmul(ps[:, h * 128:(h + 1) * 128], lhsT=lhsT, rhs=rhs, start=True, stop=True)
                    return last
                pg.op("pe", dsm, reads=[res("KEM") if sample else rKE, rIB], writes=[rps])
            yield
            for c in range(NC):
                ps, rps = psDS[c]
                for h in range(4):
                    sidx = (c * 4 + h) if sample else (4 * Q + h)
                    pg.op("dve", lambda e, c=c, h=h, ps=ps, sidx=sidx, NC=NC: e.scalar_tensor_tensor(
                        out=S32[:, sidx, :], in0=S32[:, sidx, :], scalar=DECt[:, h * NC + c:h * NC + c + 1],
                        in1=ps[:, h * 128:(h + 1) * 128], op0=ALU.mult, op1=ALU.add),
                        reads=[rps, rDEC, rS] + ([rSB] if (not pre and (sample or c == 0)) else []), writes=[rS])
                if (not sample) and (not pre) and c + 1 < NC:
                    pg.op("act", lambda e, c=c: e.activation(out=SBt[:, :, c + 1, :], in_=S32[:, hq, :], func=AF.Copy),
                          reads=[rS], upd=[rSB])
                    rSB.writers = [pg.q["act"][-1]]
            if sample:
                for st in range(NSTR):
                    dma("sp", sso[st, 4 * Q:4 * Q + 4, :, :].rearrange("h d v -> d h v"), S32[:, st * 4:(st + 1) * 4, :], reads=[rS])
                    out_dmas.append(pg.q["sp"][-1])
                    rS.readers.append(pg.q["sp"][-1])
            if pre:
                continue
            psT1, rT1 = aux()
            psT2, rT2 = aux()
            pT1 = psT1[:, :].bitcast(BF16)
            pT2 = psT2[:, :].bitcast(BF16)

            def trq(e, P=P, pT1=pT1, pT2=pT2):
                last = None
                for h in range(4):
                    e.transpose(out=pT1[:, h * 128:h * 128 + P], in_=QR[0:P, h * 128:(h + 1) * 128], identity=ident_b[0:P, 0:P])
                    e.transpose(out=pT1[:, 512 + h * 128:512 + h * 128 + P], in_=KR[0:P, h * 128:(h + 1) * 128],
                                identity=ident_b[0:P, 0:P])
                    last = e.transpose(out=pT2[:, h * 128:h * 128 + P], in_=QB[0:P, h * 128:(h + 1) * 128],
                                       identity=ident_b[0:P, 0:P])
                return last
            pg.op("pe", trq, reads=[rQR, rKR, rQB, r_setup], writes=[rT1, rT2])
            QRT, KRT, QBT = TTt[0], TTt[1], TTt[2]
            rQRT, rKRT, rQBT = res("QRT"), res("KRT"), res("QBT")
            copy_op("act", QRT[:, :, 0:P], pT1[:, 0:512].rearrange("p (h t) -> p h t", t=128)[:, :, 0:P], [rT1], [rQRT])
            copy_op("dve", KRT[:, :, 0:P], pT1[:, 512:1024].rearrange("p (h t) -> p h t", t=128)[:, :, 0:P], [rT1], [rKRT])
            copy_op("act", QBT[:, :, 0:P], pT2[:, 0:512].rearrange("p (h t) -> p h t", t=128)[:, :, 0:P], [rT2], [rQBT])
            if sample:
                rQBM = res("QBM")
                for st in range(NSTR):
                    pg.op("dve", lambda e, st=st: e.tensor_tensor(out=QBM[:, :, st, :], in0=QBT[:, :, 0:64],
                                                                  in1=MS[:, st:st + 1, :].broadcast_to([128, 4, 64]) if False else
                                                                  MS[:, st, :].rearrange("p (o t) -> p o t", o=1).broadcast_to([128, 4, 64]),
                                                                  op=ALU.mult),
                          reads=[rQBT, r_setup], writes=[rQBM] if st == 0 else [], upd=[] if st == 0 else [rQBM])
            yield
            psS, rpS = aux()

            def sc(e, P=P, psS=psS):
                last = None
                for h in range(4):
                    last = e.matmul(psS[0:P, h * P:(h + 1) * P], lhsT=KRT[:, h, 0:P], rhs=QRT[:, h, 0:P], start=True, stop=True)
                return last
            pg.op("pe", sc, reads=[rQRT, rKRT], writes=[rpS])
            SCM, rSCM = BFt[4], res("SCM")
            pg.op("dve", lambda e, P=P, psS=psS: e.tensor_tensor(out=SCM[0:P, 0:4 * P], in0=psS[0:P, 0:4 * P], in1=TRI4[0:P, 0:4 * P],
                                                                 op=ALU.mult), reads=[rpS, r_setup], writes=[rSCM])
            yield
            psO, rpO = aux()

            def om(e, s=s, P=P, psO=psO, NC=NC, L=L):
                last = None
                for h in range(4):
                    e.matmul(psO[0:P, h * 128:(h + 1) * 128], lhsT=SCM[0:P, h * P:(h + 1) * P], rhs=IBt[0:P, s, h * 128:(h + 1) * 128],
                             start=True, stop=False)
                    for c in range(NC):
                        if sample:
                            last = e.matmul(psO[0:P, h * 128:(h + 1) * 128], lhsT=QBM[:, h, c, :], rhs=SBt[:, h, c, :],
                                            start=False, stop=(c == NC - 1))
                        else:
                            last = e.matmul(psO[c * L:(c + 1) * L, h * 128:(h + 1) * 128], lhsT=QBT[:, h, c * L:(c + 1) * L],
                                            rhs=SBt[:, h, c, :], start=False, stop=True)
                return last
            pg.op("pe", om, reads=[rSCM, rIB, rSB, rQBT] + ([res("QBM")] if sample else []), writes=[rpO])
            rso = res("sso")
            for h in range(4):
                pg.op("act", lambda e, P=P, h=h, psO=psO: e.activation(out=JUNK[0:P, 0:128], in_=psO[0:P, h * 128:(h + 1) * 128],
                                                                        func=AF.Square, accum_out=SM[0:P, 16 + h:17 + h]),
                      reads=[rpO], writes=[res("junk")], upd=[rso])
            rso.writers = [pg.q["act"][-1]]
            rstd_from_ss(SM[0:P, 16:20], P, 4, 1.0 / 128, rso)
            YB, rYB = BFt[5], res("YB")
            for h in range(4):
                pg.op("dve", lambda e, P=P, h=h, psO=psO: e.scalar_tensor_tensor(
                    out=YB[0:P, h * 128:(h + 1) * 128], in0=psO[0:P, h * 128:(h + 1) * 128], scalar=SM[0:P, 16 + h:17 + h],
                    in1=GNt[0:P, h * 128:(h + 1) * 128], op0=ALU.mult, op1=ALU.mult),
                    reads=[rpO, rso, res("GN")], writes=[rYB] if h == 0 else [], upd=[] if h == 0 else [rYB])
            yield
            y_to_T(YB, rYB, s, P, KT // 2 + 4 * Q)
            yield

    out_dmas = []
    r_hidt = [Res(), Res()]
    last_barrier_dma = {"sp": 0, "pool": 0, "act": 0}

    def barrier(engs=("pe", "act", "dve", "sp")):
        lasts = []
        for e in ENGS:
            for o in reversed(pg.q[e]):
                if not o.dma and o.fn is not None and getattr(o, "real", True):
                    lasts.append(o)
                    break
        dmas = []
        for e in ("sp",):
            dmas.extend(pg.dma_ops[e][last_barrier_dma[e]:])
            last_barrier_dma[e] = len(pg.dma_ops[e])
        rb = Res()
        rb.writers = lasts + dmas
        for e in engs:
            pg.op(e, None, reads=[rb])

    SP_ONLY = ("sp",)

    def load_bc(dst, src1d, n, name):
        return dma("sp", dst[:, 0:n], src1d.partition_broadcast(128), writes=[res(name)])

    def act_evac(func, dst_fn, rname):
        def ev(s, P, ps, rps):
            out = dst_fn(s, P)
            pg.op("act", lambda e: e.activation(out=out, in_=ps, func=func), reads=[rps], writes=[res(rname % s)])
        return ev

    def stage0_gen(xsrc, tok0, subt, ht, r_htl):
        def ldx(i):
            s, P = subt[i]
            dma("sp", XIN[s % 2][0:P, :], xsrc[tok0 + s * 128: tok0 + s * 128 + P, :], writes=[r_xin[s % 2]])
        for i in range(min(2, len(subt))):
            ldx(i)
        yield
        norm_stats(subt[0][0], XIN[subt[0][0] % 2][0:subt[0][1], :], subt[0][1], r_xin[subt[0][0] % 2])
        yield
        for i, (s, P) in enumerate(subt):
            if i + 1 < len(subt):
                s1, P1 = subt[i + 1]
                norm_stats(s1, XIN[s1 % 2][0:P1, :], P1, r_xin[s1 % 2])
                yield
            for _ in norm_apply_gen(s, XIN[s % 2][0:P, :], P, r_xin[s % 2], GC1, ht, r_htl):
                yield
            if i + 2 < len(subt):
                ldx(i + 2)

    def do_tile(kind, xsrc, tok0, subt, ydst, HT=HT, r_ht=r_ht, stage0=True):
        sample = kind == "sample"
        pre = kind == "pre"
        if stage0:
            for _ in stage0_gen(xsrc, tok0, subt, HT, r_ht):
                pass
        if not pre:
            barrier(SP_ONLY)
        ck("s0")
        if not pre:
            load_bc(NVO, norm_v, WA, "NVO")
            for jp in range(cfg.NA):
                def ev_flush(inner):
                    def ev(s, P, ps, rps):
                        flush()
                        inner(s, P, ps, rps)
                    return ev
                vb = 1 + jp % 2
                gemm(HT, r_ht, KT, w_in, 0, WA + jp * 512, subt,
                     act_evac(AF.Gelu_apprx_tanh, lambda s, P, vb=vb: RAW[vb][0:P, s, :], "raw%d_%%d" % vb), wname="w_in")
                gemm(HT, r_ht, KT, w_in, 0, jp * 512, subt,
                     ev_flush(act_evac(AF.Gelu_apprx_tanh, lambda s, P: RAW[0][0:P, s, :], "raw0_%d")), wname="w_in")
                drive(mixer_A(jp, subt, sample))
            deferred.append(lambda: load_bc(NVO, norm_o, WB, "NVO"))
        for Q in range(cfg.NQ):
            def ev_flush(inner):
                def ev(s, P, ps, rps):
                    flush()
                    inner(s, P, ps, rps)
                return ev
            c_q, c_f, c_i, c_g = 2 * WA + Q * 512, 2 * WA + WB + Q * 512, 2 * WA + 2 * WB + Q * 512, 2 * WA + 3 * WB + Q * 512
            ibb = (IBt, IBt2)[Q % 2]
            gemm(HT, r_ht, KT, w_in, 0, c_i, subt, act_evac(AF.Copy, lambda s, P, ibb=ibb: ibb[0:P, s, :], "ib%d_%%d" % (Q % 2)), wname="w_in")
            sgb = SGP if pre else RAW[0]
            gemm(HT, r_ht, KT, w_in, 0, c_f, subt, ev_flush(act_evac(AF.Sigmoid, lambda s, P, sgb=sgb: sgb[0:P, s, :], "raw0_%d")), wname="w_in")
            if not pre:
                gemm(HT, r_ht, KT, w_in, 0, c_q, subt, ev_flush(act_evac(AF.Silu, lambda s, P: RAW[1][0:P, s, :], "raw1_%d")), wname="w_in")
                gemm(HT, r_ht, KT, w_in, 0, c_g, subt, ev_flush(act_evac(AF.Silu, lambda s, P: RAW[2][0:P, s, :], "raw2_%d")), wname="w_in")
            drive(mixer_B(Q, subt, sample, pre))
        flush()
        if pre:
            flush_bg()
            return
        barrier(SP_ONLY)
        ck("s1" + kind)
        r_x = [res("x1_%d" % s) for s in range(S)]
        for (s, P) in subt:
            dma("sp", X1[0:P, s, :], xsrc[tok0 + s * 128: tok0 + s * 128 + P, :], writes=[r_x[s]])

        def add_evac(cb):
            def ev(s, P, ps, rps):
                pg.op("dve", lambda e: e.tensor_tensor(out=X1[0:P, s, cb * 512:(cb + 1) * 512], in0=ps,
                                                       in1=X1[0:P, s, cb * 512:(cb + 1) * 512], op=ALU.add),
                      reads=[rps, r_x[s]], writes=[r_x[s]])
            return ev
        for cb in range(D // 512):
            gemm(YT, r_yt, KT, w_out, 0, cb * 512, subt, add_evac(cb), wname="w_out")
        for (s, P) in subt:
            norm_stats(s, X1[0:P, s, :], P, r_x[s])
        for (s, P) in subt:
            norm_apply(s, X1[0:P, s, :], P, r_x[s], GC2, HT, r_ht)
        ck("s2")
        NCH = DFF // cfg.FC
        KF = cfg.FC // 128
        for j in range(NCH):
            hb = HIDT[j % 2]
            rh = r_hidt[j % 2]
            wl = {}
            pg.op("dve", None, writes=[rh])
            rh.writers = []
            for cb in range(cfg.FC // 512):
                def evm(m, ncol, ps, rps, cb=cb, hb=hb, rh=rh):
                    rrt = res("RT")
                    pg.op("act", lambda e: e.activation(out=RT[:, 0:ncol], in_=ps, func=AF.Relu), reads=[rps], writes=[rrt])
                    o = pg.op("dve", lambda e: e.tensor_tensor(out=hb[:, cb * 4 + m, 0:ncol], in0=RT[:, 0:ncol], in1=RT[:, 0:ncol],
                                                               op=ALU.mult), reads=[rrt])
                    rh.writers = [o]
                gemm_fm(HT, r_ht, KT, w_up, 0, j * cfg.FC + cb * 512, subt, evm, wname="w_up")
            flush()
            r_h4 = [rh] * S
            for cb in range(D // 512):
                gemm(hb, r_h4, KF, w_down, j * cfg.FC, cb * 512, subt, add_evac(cb), wname="w_down")
        barrier(SP_ONLY)
        ck("s3")
        load_bc(NFBt, norm_f, D, "NFB")
        for (s, P) in subt:
            ssap = SM[0:P, s:s + 1]
            rss = res("ss%d" % s)
            sumsq(X1[0:P, s, :], P, ssap, r_x[s], rss)
            pg.op("dve", lambda e, s=s, P=P, ssap=ssap: e.scalar_tensor_tensor(out=X1[0:P, s, :], in0=X1[0:P, s, :], scalar=ssap,
                                                                                in1=NFBt[0:P, :], op0=ALU.mult, op1=ALU.mult),
                  reads=[rss, res("NFB"), r_x[s]], writes=[r_x[s]])
            dma("sp", ydst[tok0 + s * 128: tok0 + s * 128 + P, :], X1[0:P, s, :], reads=[r_x[s]])
            out_dmas.append(pg.q["sp"][-1])
        barrier(SP_ONLY)

    full = [(s, 128) for s in range(S)]
    barrier()
    try:
        ck("setup")
        NPRE = NP // T
        assert NPRE % 2 == 0
        HTs = (HT, YT)
        r_hts = (r_ht, [Res() for _ in range(S)])
        for _ in stage0_gen(xprev, 0, full, HTs[0], r_hts[0]):
            pass
        for t in range(NPRE):
            nx = (xprev, (t + 1) * T) if t + 1 < NPRE else (xp, 0)
            drive_bg(stage0_gen(nx[0], nx[1], full, HTs[(t + 1) % 2], r_hts[(t + 1) % 2]))
            do_tile("pre", xprev, t * T, full, None, HT=HTs[t % 2], r_ht=r_hts[t % 2], stage0=False)
            ck("pre%d" % t)
        for t in range(NP // T):
            do_tile("prompt", xp, t * T, full, yp, stage0=(t > 0))
            ck("prompt%d" % t)
        dma("sp", spo.rearrange("h d v -> d h v"), S32[:, 0:HB, :], reads=[res("S32")])
        out_dmas.append(pg.q["sp"][-1])
        res("S32").readers.append(pg.q["sp"][-1])
        barrier()
        do_tile("sample", xs, 0, [(0, 64)], ys)
    except _Stop:
        del deferred[:]
        del deferred_bg[:]
        barrier()
    rfin = Res()
    rfin.writers = list(out_dmas)
    pg.op("sp", None, reads=[rfin])
    return nc, pg


_CACHE = {}


def get_program(cfg_key, cfg):
    if cfg_key not in _CACHE:
        nc, pg = build(cfg)
        es = ExitStack()
        pg.emit(nc, es)
        es.close()
        _CACHE[cfg_key] = nc
    return _CACHE[cfg_key]


def run(cfg, x_prompt, x_sample, state_hgrn, norm1, w_in, w_s, b_s, norm_v, lb_logits, norm_o,
        w_out, norm2, w_up, w_down, norm_f, trace=False):
    f = lambda a: np.ascontiguousarray(np.asarray(a, dtype=np.float32))
    x_prompt, x_sample, state_hgrn = f(x_prompt), f(x_sample), f(state_hgrn)
    D, NP, NSTR, LS, HB, WA = cfg.D, cfg.NP, cfg.NSTR, cfg.LS, cfg.HB, cfg.WA
    n = cfg.n_cores
    B = x_prompt.shape[0]
    assert n == 2 * B and x_prompt.shape[1] == 2 * NP
    shared = {
        "w_in": f(w_in[0]), "w_out": f(w_out[0]), "w_up": f(w_up[0]), "w_down": f(w_down[0]),
        "norm1": f(norm1[0]), "norm2": f(norm2[0]), "norm_f": f(norm_f), "norm_v": f(norm_v[0]).reshape(-1),
        "norm_o": f(norm_o[0]).reshape(-1), "lb_logits": f(lb_logits), "w_s": f(w_s[0]), "b_s": f(b_s[0]),
    }
    for k, v in make_consts().items():
        shared["c_" + k] = v
    zeros = np.zeros((NP, D), np.float32)
    in_maps = []
    for c in range(n):
        b, half = c // 2, c % 2
        m = dict(shared)
        m["xp"] = x_prompt[b, half * NP:(half + 1) * NP]
        m["xprev"] = x_prompt[b, 0:NP] if half == 1 else zeros
        m["xs"] = x_sample[c * NSTR:(c + 1) * NSTR].reshape(NSTR * LS, D)
        m["s0"] = state_hgrn[0, c * NSTR:(c + 1) * NSTR]
        in_maps.append(m)
    nc = get_program((D, NP, n), cfg)
    r = run_bass_kernel_spmd(nc, in_maps, core_ids=list(range(n)), **({"trace": True} if trace else {}))
    outs = r.results
    y_prompt = np.zeros((B, 2 * NP, D), np.float32)
    y_sample = np.zeros((n * NSTR, LS, D), np.float32)
    st_p = np.zeros((1, B, HB, 128, 128), np.float32)
    st_s = np.zeros((1, n * NSTR, HB, 128, 128), np.float32)
    v_s = np.zeros((1, n * NSTR, LS, WA), np.float32)
    for c in range(n):
        b, half = c // 2, c % 2
        o = outs[c]
        y_prompt[b, half * NP:(half + 1) * NP] = o["yp"]
        y_sample[c * NSTR:(c + 1) * NSTR] = o["ys"].reshape(NSTR, LS, D)
        if half == 1:
            st_p[0, b] = o["spo"]
        st_s[0, c * NSTR:(c + 1) * NSTR] = o["sso"]
        v_s[0, c * NSTR:(c + 1) * NSTR] = o["vso"].reshape(NSTR, LS, WA)
    if trace:
        return (y_prompt, y_sample, st_p, st_s, v_s), r
    return (y_prompt, y_sample, st_p, st_s, v_s)


def kernel(**inputs):
    cfg = Cfg()
    return run(cfg, **inputs)
```
